# Optimizing a Trainium2 kernel written in Bass

```python
import math
import jax, jax.numpy as jnp
from jax import lax
import numpy as np

D_MODEL = 1024
BATCH = 8
SEQ = 4096
DEPTH = 1

N_META = 16
BLOCK = 128
WINDOW = 128
HEAD_DIM = 64
A_HEADS = D_MODEL // 256
A_V_DIM = 2 * HEAD_DIM
A_WIDTH = A_HEADS * A_V_DIM
B_HEADS = D_MODEL // 128
B_KV_HEADS = max(1, B_HEADS // 4)
B_GROUP = B_HEADS // B_KV_HEADS
B_WIDTH = B_HEADS * HEAD_DIM
D_FF = -(-8 * D_MODEL // (3 * 256)) * 256
EPS = 1e-6
QA_W = A_HEADS * 2 * HEAD_DIM
KA_W = A_HEADS * 2 * HEAD_DIM
VA_W = A_WIDTH
QB_W = B_WIDTH
KB_W = B_KV_HEADS * HEAD_DIM
VB_W = B_KV_HEADS * HEAD_DIM
GATE_W = D_MODEL
PROJ_W = QA_W + KA_W + VA_W + QB_W + KB_W + VB_W + 2 * GATE_W

kernel_name = "hybrid_diffattn_swa_gated_encoder"


def rms_norm(x, g):
    xf = x.astype(jnp.float32)
    y = xf * lax.rsqrt(jnp.mean(xf * xf, axis=-1, keepdims=True) + EPS)
    return (y * g.astype(jnp.float32)).astype(x.dtype)


def alibi_slopes(n):
    return jnp.asarray(2.0 ** (-8.0 * np.arange(1, n + 1) / n), dtype=jnp.float32)


def lambda_init_for(layer):
    return 0.8 - 0.6 * math.exp(-0.3 * layer)


def diff_attn_rows(q1, q2, k1, k2, v, qpos, kpos, key_real, slopes, lam):
    scale = HEAD_DIM ** -0.5
    dist = jnp.abs(qpos[:, None] - kpos[None, :]).astype(jnp.float32) * key_real[None, :]
    bias = -slopes[:, None, None] * dist[None]
    s1 = jnp.einsum('bqhd,bkhd->bhqk', q1, k1).astype(jnp.float32) * scale + bias
    s2 = jnp.einsum('bqhd,bkhd->bhqk', q2, k2).astype(jnp.float32) * scale + bias
    attn = jax.nn.softmax(s1, axis=-1) - lam * jax.nn.softmax(s2, axis=-1)
    return jnp.einsum('bhqk,bkhe->bqhe', attn.astype(v.dtype), v)


def differential_attention(q, k, v, lam_q1, lam_k1, lam_q2, lam_k2, subln_g, lambda_init):
    Bn, L = q.shape[0], q.shape[1]
    S = L - N_META
    nb = S // BLOCK
    pos = jnp.arange(L)
    key_real = (pos >= N_META).astype(jnp.float32)
    slopes = alibi_slopes(A_HEADS)
    lam = (jnp.exp(jnp.sum(lam_q1.astype(jnp.float32) * lam_k1.astype(jnp.float32)))
           - jnp.exp(jnp.sum(lam_q2.astype(jnp.float32) * lam_k2.astype(jnp.float32)))
           + lambda_init)
    q1, q2 = q[..., 0, :], q[..., 1, :]
    k1, k2 = k[..., 0, :], k[..., 1, :]

    def attend(a, b, p):
        return diff_attn_rows(a, b, k1, k2, v, p, pos, key_real, slopes, lam)

    y_meta = attend(q1[:, :N_META], q2[:, :N_META], pos[:N_META])

    def to_blocks(t):
        return t[:, N_META:].reshape(Bn, nb, BLOCK, *t.shape[2:]).swapaxes(0, 1)

    y_real = lax.map(lambda a: attend(*a),
                     (to_blocks(q1), to_blocks(q2), pos[N_META:].reshape(nb, BLOCK)))
    y_real = y_real.swapaxes(0, 1).reshape(Bn, S, A_HEADS, A_V_DIM)
    y = jnp.concatenate([y_meta, y_real], axis=1)
    y = rms_norm(y, subln_g) * (1.0 - lambda_init)
    return y.reshape(Bn, L, A_WIDTH)


def window_attn_blocks(q, k, v, qpos, kpos, kvalid, kreal, slopes, sink):
    scale = HEAD_DIM ** -0.5
    dt = jnp.abs(qpos[:, :, None] - kpos[:, None, :])
    visible = kvalid[:, None, :] & ((~kreal[:, None, :]) | (dt <= WINDOW))
    dist = jnp.where(kreal[:, None, :], dt, 0).astype(jnp.float32)
    bias = -slopes[None, :, :, None, None] * dist[:, None, None]
    s = jnp.einsum('bnqgrd,bnkgd->bngrqk', q, k).astype(jnp.float32) * scale + bias
    s = jnp.where(visible[None, :, None, None], s, -jnp.inf)
    sink_b = sink.astype(jnp.float32)[None, None, :, :, None, None]
    m = jnp.maximum(jnp.max(s, axis=-1, keepdims=True), sink_b)
    p = jnp.exp(s - m)
    denom = jnp.sum(p, axis=-1, keepdims=True) + jnp.exp(sink_b - m)
    return jnp.einsum('bngrqk,bnkgd->bnqgrd', (p / denom).astype(v.dtype), v)


def windowed_gqa(q, k, v, sink):
    Bn, L = q.shape[0], q.shape[1]
    S = L - N_META
    nb = S // BLOCK
    G, d = B_KV_HEADS, HEAD_DIM
    slopes = alibi_slopes(B_HEADS).reshape(B_KV_HEADS, B_GROUP)

    def band(t):
        tp = jnp.pad(t[:, N_META:], ((0, 0), (BLOCK, BLOCK), (0, 0), (0, 0)))
        tp = tp.reshape(Bn, nb + 2, BLOCK, G, d)
        win = jnp.concatenate([tp[:, j:j + nb] for j in range(3)], axis=2)
        meta = jnp.broadcast_to(t[:, None, :N_META], (Bn, nb, N_META, G, d))
        return jnp.concatenate([meta, win], axis=2)

    blk = jnp.arange(nb)[:, None]
    win_idx = (blk - 1) * BLOCK + jnp.arange(3 * BLOCK)[None, :]
    meta_pos = jnp.broadcast_to(jnp.arange(N_META)[None, :], (nb, N_META))
    kpos = jnp.concatenate([meta_pos, N_META + win_idx], axis=1)
    kvalid = jnp.concatenate([jnp.ones((nb, N_META), bool),
                              (win_idx >= 0) & (win_idx < S)], axis=1)
    kreal = jnp.concatenate([jnp.zeros((nb, N_META), bool),
                             jnp.ones((nb, 3 * BLOCK), bool)], axis=1)
    qpos = N_META + jnp.arange(S).reshape(nb, BLOCK)
    q_real = q[:, N_META:].reshape(Bn, nb, BLOCK, G, B_GROUP, d)
    y_real = window_attn_blocks(q_real, band(k), band(v), qpos, kpos, kvalid, kreal, slopes, sink)
    y_real = y_real.reshape(Bn, S, B_WIDTH)

    tk = N_META + BLOCK
    kpos_m = jnp.arange(tk)[None, :]
    y_meta = window_attn_blocks(q[:, None, :N_META], k[:, None, :tk], v[:, None, :tk],
                                jnp.arange(N_META)[None, :], kpos_m,
                                jnp.ones((1, tk), bool), kpos_m >= N_META, slopes, sink)
    y_meta = y_meta.reshape(Bn, N_META, B_WIDTH)
    return jnp.concatenate([y_meta, y_real], axis=1)


def mixer_block(h, w_in, lq1, lk1, lq2, lk2, subln_g, sink, w_ba, w_bb, w_o, lambda_init):
    Bn, L = h.shape[0], h.shape[1]
    proj = h @ w_in
    o = 0
    qa = proj[..., o:o + QA_W]; o += QA_W
    ka = proj[..., o:o + KA_W]; o += KA_W
    va = proj[..., o:o + VA_W]; o += VA_W
    qb = proj[..., o:o + QB_W]; o += QB_W
    kb = proj[..., o:o + KB_W]; o += KB_W
    vb = proj[..., o:o + VB_W]; o += VB_W
    ga = proj[..., o:o + GATE_W]; o += GATE_W
    gb = proj[..., o:o + GATE_W]
    y_a = differential_attention(qa.reshape(Bn, L, A_HEADS, 2, HEAD_DIM),
                                 ka.reshape(Bn, L, A_HEADS, 2, HEAD_DIM),
                                 va.reshape(Bn, L, A_HEADS, A_V_DIM),
                                 lq1, lk1, lq2, lk2, subln_g, lambda_init)
    y_b = windowed_gqa(qb.reshape(Bn, L, B_KV_HEADS, B_GROUP, HEAD_DIM),
                       kb.reshape(Bn, L, B_KV_HEADS, HEAD_DIM),
                       vb.reshape(Bn, L, B_KV_HEADS, HEAD_DIM),
                       sink.reshape(B_KV_HEADS, B_GROUP))
    merged = jax.nn.sigmoid(ga) * (y_a @ w_ba) + jax.nn.sigmoid(gb) * (y_b @ w_bb)
    return merged @ w_o


def swiglu(h, w_gate, w_up, w_down):
    return (jax.nn.silu(h @ w_gate) * (h @ w_up)) @ w_down


def setup_inputs(seed: int = 0) -> dict:
    key = jax.random.key(seed)
    ks = jax.random.split(key, 20)

    def w(k, shape, fan_in):
        return jax.random.normal(k, shape, jnp.float32) * (fan_in ** -0.5)

    def gain(k, shape):
        return 1.0 + 0.02 * jax.random.normal(k, shape, jnp.float32)

    return {
        "x": jax.random.normal(ks[0], (BATCH, SEQ, D_MODEL), jnp.float32),
        "meta_tokens": jax.random.normal(ks[1], (N_META, D_MODEL), jnp.float32),
        "norm_mix": gain(ks[2], (DEPTH, D_MODEL)),
        "w_in": w(ks[3], (DEPTH, D_MODEL, PROJ_W), D_MODEL),
        "lambda_q1": 0.1 * jax.random.normal(ks[4], (DEPTH, HEAD_DIM), jnp.float32),
        "lambda_k1": 0.1 * jax.random.normal(ks[5], (DEPTH, HEAD_DIM), jnp.float32),
        "lambda_q2": 0.1 * jax.random.normal(ks[6], (DEPTH, HEAD_DIM), jnp.float32),
        "lambda_k2": 0.1 * jax.random.normal(ks[7], (DEPTH, HEAD_DIM), jnp.float32),
        "subln_gain": gain(ks[8], (DEPTH, A_V_DIM)),
        "sink_logits": 0.5 * jax.random.normal(ks[9], (DEPTH, B_HEADS), jnp.float32),
        "w_branch_a": w(ks[10], (DEPTH, A_WIDTH, D_MODEL), A_WIDTH),
        "w_branch_b": w(ks[11], (DEPTH, B_WIDTH, D_MODEL), B_WIDTH),
        "w_out": w(ks[12], (DEPTH, D_MODEL, D_MODEL), D_MODEL),
        "norm_ffn": gain(ks[13], (DEPTH, D_MODEL)),
        "w_ff_gate": w(ks[14], (DEPTH, D_MODEL, D_FF), D_MODEL),
        "w_ff_up": w(ks[15], (DEPTH, D_MODEL, D_FF), D_MODEL),
        "w_ff_down": w(ks[16], (DEPTH, D_FF, D_MODEL), D_FF),
        "norm_final": gain(ks[17], (D_MODEL,)),
    }


def reference(x, meta_tokens, norm_mix, w_in, lambda_q1, lambda_k1, lambda_q2, lambda_k2,
              subln_gain, sink_logits, w_branch_a, w_branch_b, w_out, norm_ffn,
              w_ff_gate, w_ff_up, w_ff_down, norm_final):
    Bn = x.shape[0]
    meta = jnp.broadcast_to(meta_tokens.astype(x.dtype)[None], (Bn, N_META, x.shape[-1]))
    h = jnp.concatenate([meta, x], axis=1)
    for layer in range(DEPTH):
        h = h + mixer_block(rms_norm(h, norm_mix[layer]), w_in[layer],
                            lambda_q1[layer], lambda_k1[layer], lambda_q2[layer], lambda_k2[layer],
                            subln_gain[layer], sink_logits[layer],
                            w_branch_a[layer], w_branch_b[layer], w_out[layer],
                            lambda_init_for(layer))
        h = h + swiglu(rms_norm(h, norm_ffn[layer]), w_ff_gate[layer], w_ff_up[layer], w_ff_down[layer])
    return rms_norm(h, norm_final)[:, N_META:]
```

```python
import os
import numpy as np
import ml_dtypes
import concourse.bass as bass
import concourse.mybir as mybir
from concourse.bass_utils import run_bass_kernel_spmd

F32 = mybir.dt.float32
BF16 = mybir.dt.bfloat16
AF = mybir.ActivationFunctionType
ALU = mybir.AluOpType

SEM_CAP = 16000
D = 1024
SEQ = 4096
L = 4112
NMETA = 16
DFF = 2816
PROJ_W = 4352
EPS = 1e-6
LAMBDA_INIT = 0.8 - 0.6 * 1.0
SLOPE_A = [2.0 ** (-2.0 * (i + 1)) for i in range(4)]
SLOPE_B = [2.0 ** (-1.0 * (i + 1)) for i in range(8)]
NEG = -30000.0


class Ins:
    __slots__ = ("eng", "fn", "deps", "signal", "count", "is_dma", "dkey", "didx", "seq")

    def __init__(self, eng, fn, is_dma=False, dkey=None):
        self.eng = eng
        self.fn = fn
        self.deps = []
        self.signal = False
        self.count = 0
        self.is_dma = is_dma
        self.dkey = dkey
        self.didx = 0


class Prog:
    ENGS = ("pe", "act", "dve", "pool", "sp")

    def __init__(self, nc):
        self.nc = nc
        self.lists = {e: [] for e in self.ENGS}
        self.bw = {}
        self.br = {}
        self.dma_counts = {}
        self.pending = {e: [] for e in self.ENGS}
        self.dmas_since = []

    def _add(self, ins, reads, writes):
        deps = []
        if self.pending[ins.eng]:
            deps.extend(self.pending[ins.eng])
            self.pending[ins.eng] = []
        for k in reads:
            deps.extend(self.bw.get(k, ()))
        for k in writes:
            deps.extend(self.bw.get(k, ()))
            deps.extend(self.br.get(k, ()))
        best = {}
        for d in deps:
            if d is ins:
                continue
            if d.is_dma:
                k = ("d", d.dkey)
                if k not in best or best[k].didx < d.didx:
                    best[k] = d
            else:
                if d.eng == "pe" and ins.eng == "pe" and not ins.is_dma:
                    continue
                k = ("e", d.eng)
                if k not in best or best[k].seq < d.seq:
                    best[k] = d
        for d in best.values():
            ins.deps.append(d)
            d.signal = True
        ins.seq = len(self.lists[ins.eng])
        for k in reads:
            self.br.setdefault(k, []).append(ins)
        for k in writes:
            self.bw[k] = [ins]
            self.br[k] = []
        self.lists[ins.eng].append(ins)
        return ins

    def op(self, eng, fn, reads=(), writes=()):
        return self._add(Ins(eng, fn), reads, writes)

    def dma(self, eng, fn, reads=(), writes=(), key=None):
        ins = Ins(eng, fn, is_dma=True, dkey=key)
        n = self.dma_counts.get(key, 0) + 1
        self.dma_counts[key] = n
        ins.didx = n
        ins.signal = True
        self.dmas_since.append(ins)
        return self._add(ins, reads, writes)

    def barrier(self):
        lasts = []
        for e in self.ENGS:
            for i in reversed(self.lists[e]):
                if not i.is_dma:
                    lasts.append(i)
                    break
        lasts += self.dmas_since
        self.dmas_since = []
        for d in lasts:
            d.signal = True
        for e in self.ENGS:
            self.pending[e] = self.pending[e] + list(lasts)

    def emit(self, final_waits=()):
        nc = self.nc
        nsig = {}
        for e in self.ENGS:
            c = 0
            for ins in self.lists[e]:
                if ins.is_dma:
                    continue
                if ins.signal:
                    c += 1
                    ins.count = c
            nsig[e] = c
        esems = {}
        for e in self.ENGS:
            nep = (nsig[e] + SEM_CAP - 1) // SEM_CAP
            esems[e] = [nc.alloc_semaphore(f"s_{e}_{i}") for i in range(max(nep, 1))]
        dsems = {k: nc.alloc_semaphore(f"d_{i}") for i, k in enumerate(self.dma_counts)}
        lists = self.lists
        dma_counts = self.dma_counts

        def run(e, eng):
            waited = {}
            for ins in lists[e]:
                need = {}
                for d in ins.deps:
                    if d.is_dma:
                        sem = ("d", d.dkey)
                        val = (0, 16 * d.didx)
                    else:
                        sem = ("e", d.eng)
                        n = d.count
                        val = ((n - 1) // SEM_CAP, (n - 1) % SEM_CAP + 1)
                    if need.get(sem, (-1, -1)) < val:
                        need[sem] = val
                for sem, val in need.items():
                    if waited.get(sem, (-1, -1)) >= val:
                        continue
                    waited[sem] = val
                    if sem[0] == "d":
                        eng.wait_ge(dsems[sem[1]], val[1])
                    else:
                        eng.wait_ge(esems[sem[1]][val[0]], val[1])
                r = ins.fn(eng)
                if ins.is_dma:
                    r.then_inc(dsems[ins.dkey], 16)
                elif ins.signal:
                    n = ins.count
                    r.then_inc(esems[e][(n - 1) // SEM_CAP], 1)
            if e == "sp":
                for k in final_waits:
                    eng.wait_ge(dsems[k], 16 * dma_counts[k])

        with nc.Block() as block:
            @block.tensor
            def _(eng):
                run("pe", eng)

            @block.scalar
            def _(eng):
                run("act", eng)

            @block.vector
            def _(eng):
                run("dve", eng)

            @block.gpsimd
            def _(eng):
                run("pool", eng)

            @block.sync
            def _(eng):
                run("sp", eng)


class Arena:
    def __init__(self, nc, base, limit):
        self.nc = nc
        self.off = base
        self.limit = limit
        self.n = 0

    def alloc(self, name, shape, dt):
        nbytes = int(np.prod(shape[1:])) * (2 if dt == BF16 else 4)
        o = (self.off + 63) // 64 * 64
        assert o + nbytes <= self.limit, (name, o, nbytes, self.limit)
        self.off = o + nbytes
        self.n += 1
        return self.nc.alloc_sbuf_tensor_at(f"{name}_{self.n}", list(shape), dt, offset=o)

    def mark(self):
        return self.off

    def release(self, m):
        self.off = m


def make_consts():
    kl = np.arange(128, dtype=np.float64)[:, None]
    cst = np.zeros((128, 288), np.float64)
    for h in range(4):
        s = SLOPE_A[h]
        for d in range(32):
            cst[:, h * 32 + d] = s * (kl[:, 0] - 128.0 * d)
            cst[:, 128 + h * 32 + d] = -s * (128.0 * d + kl[:, 0] + 1.0)
        for qb in range(4):
            cst[:, 256 + h * 4 + qb] = np.exp(-s * (128.0 * qb + kl[:, 0]))
            cst[:, 272 + h * 4 + qb] = np.exp(-s * (511.0 - 128.0 * qb - kl[:, 0]))
    wt = np.zeros((128, 4, 896), np.float64)
    c = np.arange(896, dtype=np.float64)[None, :]
    for h in range(4):
        wt[:, h, :] = -SLOPE_A[h] * np.abs(c - 384.0 - kl)
    bw = np.zeros((128, 2, 3, 4, 128), np.float64)
    ql = np.arange(128, dtype=np.float64)[None, :]
    for g in range(2):
        for r in range(4):
            s = SLOPE_B[g * 4 + r]
            d0 = 128.0 + ql - kl
            bw[:, g, 0, r, :] = np.where(d0 <= 128.0, -s * d0, NEG)
            bw[:, g, 1, r, :] = -s * np.abs(ql - kl)
            d2 = 128.0 + kl - ql
            bw[:, g, 2, r, :] = np.where(d2 <= 128.0, -s * d2, NEG)
    return (cst.astype(np.float32), wt.reshape(128, 3584).astype(np.float32),
            bw.reshape(128, 3072).astype(np.float32), np.eye(128).astype(ml_dtypes.bfloat16))


def build_nc(stage=99, dbg=False):
    nc = bass.Bass("TRN2", target_bir_lowering=False)

    def din(name, shape, dt=F32):
        return nc.dram_tensor(name, list(shape), dt, kind="ExternalInput").ap()

    x = din("x", [SEQ, D])
    meta = din("meta_tokens", [NMETA, D])
    w_in = din("w_in", [D, PROJ_W])
    w_ba = din("w_branch_a", [512, D])
    w_bb = din("w_branch_b", [512, D])
    w_out = din("w_out", [D, D])
    w_g = din("w_ff_gate", [D, DFF])
    w_u = din("w_ff_up", [D, DFF])
    w_d = din("w_ff_down", [DFF, D])
    n_mix = din("norm_mix", [D])
    n_ffn = din("norm_ffn", [D])
    n_fin = din("norm_final", [D])
    lam_in = [din(n, [64]) for n in ("lambda_q1", "lambda_k1", "lambda_q2", "lambda_k2")]
    subln_d = din("subln_gain", [128])
    sink_d = din("sink_logits", [8])
    cst_d = din("cst", [128, 288])
    wt_d = din("wtab", [128, 3584])
    bwt_d = din("bwtab", [128, 3072])
    ident_d = din("ident", [128, 128], BF16)
    out = nc.dram_tensor("out", [SEQ, D], F32, kind="ExternalOutput").ap()
    scr_gu = nc.dram_tensor("scr_gu", [11, 128, 8, 2, 256], BF16, kind="Internal").ap()
    scr_d = nc.dram_tensor("scr_d", [11, 128, 2, 1024], BF16, kind="Internal").ap()
    dbg_outs = {}

    P = Prog(nc)
    A = Arena(nc, 16512, 229344)

    PSF = nc.alloc_psum_tensor("psf", [128, 8 * 512], F32)
    PS3 = PSF.ap().rearrange("p (b n) -> p b n", n=512)

    def bank(i):
        return PS3[:, i, :]

    def bankb(i):
        return PSF.ap()[:, i * 512:(i + 1) * 512].bitcast(BF16)

    def PK(i):
        return ("ps", i)

    Zhn = A.alloc("Zhn", [128, 8, L], BF16)
    ident = A.alloc("ident", [128, 128], BF16)
    cst = A.alloc("cst", [128, 288], F32)
    gmix = A.alloc("gmix", [128, 8], F32)
    gffn = A.alloc("gffn", [128, 8], F32)
    gfin = A.alloc("gfin", [128, D], F32)
    subln = A.alloc("subln", [128, 128], F32)
    esink = A.alloc("esink", [128, 8], F32)
    lamt = A.alloc("lamt", [128, 4, 64], F32)
    lamj = A.alloc("lamj", [128, 64], F32)
    lams = A.alloc("lams", [128, 8], F32)
    mhalf = A.alloc("mhalf", [128, 1], F32)
    stt = A.alloc("stt", [128, 8, 4], F32)
    junk = A.alloc("junk", [128, D], BF16)
    zy_off = (A.mark() + 63) // 64 * 64
    Zy = A.alloc("Zy", [128, 8, SEQ], BF16)
    phase_base = A.mark()

    sp_dma = lambda fn, **kw: P.dma("sp", fn, **kw)

    P.dma("sp", lambda e: e.dma_start(out=ident[:], in_=ident_d), writes=["ident"], key="c_ident")
    P.dma("sp", lambda e: e.dma_start(out=cst[:], in_=cst_d), writes=["cst"], key="c_cst")
    P.dma("sp", lambda e: e.dma_start(out=gmix[:], in_=n_mix.rearrange("(c p) -> p c", p=128),
                                      allow_slow_non_contiguous=True), writes=["gmix"], key="c_gmix")
    P.dma("sp", lambda e: e.dma_start(out=gffn[:], in_=n_ffn.rearrange("(c p) -> p c", p=128),
                                      allow_slow_non_contiguous=True), writes=["gffn"], key="c_gffn")
    P.dma("sp", lambda e: e.dma_start(out=gfin[:], in_=n_fin.partition_broadcast(128)), writes=["gfin"], key="c_gfin")
    P.dma("sp", lambda e: e.dma_start(out=subln[:], in_=subln_d.partition_broadcast(128)), writes=["subln"], key="c_subln")
    P.dma("sp", lambda e: e.dma_start(out=esink[:], in_=sink_d.partition_broadcast(128)), writes=["esink"], key="c_sink")
    for i in range(4):
        P.dma("sp", lambda e, i=i: e.dma_start(out=lamt[:, i, :], in_=lam_in[i].partition_broadcast(128)),
              writes=[("lamt", i)], key=("c_lam", i))
    P.op("pool", lambda e: e.memset(mhalf[:], -0.5), writes=["mhalf"])
    P.op("dve", lambda e: e.tensor_scalar(out=subln[:], in0=subln[:], scalar1=1.0 - LAMBDA_INIT, scalar2=None, op0=ALU.mult),
         writes=["subln"])
    P.op("act", lambda e: e.activation(out=esink[:], in_=esink[:], func=AF.Exp), writes=["esink"])
    for i in range(2):
        P.op("dve", lambda e, i=i: e.tensor_tensor(out=lamj[:], in0=lamt[:, 2 * i, :], in1=lamt[:, 2 * i + 1, :], op=ALU.mult),
             reads=[("lamt", 2 * i), ("lamt", 2 * i + 1)], writes=["lamj"])
        P.op("dve", lambda e, i=i: e.tensor_reduce(out=lams[:, i:i + 1], in_=lamj[:], axis=mybir.AxisListType.X, op=ALU.add),
             reads=["lamj"], writes=[("lams", i)])
    P.op("act", lambda e: e.activation(out=lams[:, 2:4], in_=lams[:, 0:2], func=AF.Exp),
         reads=[("lams", 0), ("lams", 1)], writes=[("lams", 2)])
    P.op("dve", lambda e: e.tensor_tensor(out=lams[:, 4:5], in0=lams[:, 3:4], in1=lams[:, 2:3], op=ALU.subtract),
         reads=[("lams", 2)], writes=[("lams", 4)])
    P.op("dve", lambda e: e.tensor_scalar(out=lams[:, 4:5], in0=lams[:, 4:5], scalar1=-LAMBDA_INIT, scalar2=None, op0=ALU.add),
         writes=[("lams", 4)])
    neglam = lams[:, 4:5]

    scr_jobs = []
    scr_keys = []
    for fg in range(11):
        for t, wsrc in enumerate((w_g, w_u)):
            k = ("scr", "gu", fg, t)
            scr_keys.append(k)
            scr_jobs.append((k, lambda e, fg=fg, t=t, wsrc=wsrc: e.dma_start(
                out=scr_gu[fg, :, :, t, :],
                in_=wsrc[:, fg * 256:(fg + 1) * 256].rearrange("(c p) n -> p c n", p=128))))
    for fd in range(11):
        k = ("scr", "d", fd)
        scr_keys.append(k)
        scr_jobs.append((k, lambda e, fd=fd: e.dma_start(
            out=scr_d[fd], in_=w_d[fd * 256:(fd + 1) * 256, :].rearrange("(ff p) n -> p ff n", p=128))))

    def issue_scratch(n):
        for _ in range(n):
            if scr_jobs and stage >= 5:
                k, fn = scr_jobs.pop(0)
                P.dma("pool", fn, writes=[k], key="scr")

    def rms_rstd(src_ap, rows, slot, n, src_keys, on_dve=False):
        if on_dve is not False:
            sq = on_dve
            P.op("dve", lambda e: e.tensor_tensor(out=sq[:rows, 0:n], in0=src_ap, in1=src_ap, op=ALU.mult),
                 reads=src_keys, writes=["sqj"])
            P.op("dve", lambda e: e.tensor_reduce(out=stt[:rows, slot, 0:1], in_=sq[:rows, 0:n], axis=mybir.AxisListType.X, op=ALU.add),
                 reads=["sqj"], writes=[("stt", slot, 0)])
        else:
            P.op("act", lambda e: e.activation(out=junk[:rows, 0:n], in_=src_ap, func=AF.Square,
                                               accum_out=stt[:rows, slot, 0:1]),
                 reads=src_keys, writes=["junk", ("stt", slot, 0)])
        P.op("dve", lambda e: e.tensor_scalar(out=stt[:rows, slot, 1:2], in0=stt[:rows, slot, 0:1], scalar1=1.0 / n,
                                              scalar2=EPS, op0=ALU.mult, op1=ALU.add),
             reads=[("stt", slot, 0)], writes=[("stt", slot, 1)])
        P.op("pool", lambda e: e.tensor_tensor(out=stt[:rows, slot, 2:3], in0=stt[:rows, slot, 1:2], in1=mhalf[:rows, :],
                                               op=ALU.pow),
             reads=[("stt", slot, 1), "mhalf"], writes=[("stt", slot, 2)])
        return stt[:rows, slot, 2:3], ("stt", slot, 2)

    def blk_rows_col(kb):
        return (16, 0) if kb == 0 else (128, 16 + 128 * (kb - 1))

    m1 = A.mark()
    NXT, NXS, NPB = 4, 3, 4
    xt = [A.alloc("xt", [128, D], F32) for _ in range(NXT)]
    xs = [A.alloc("xs", [128, D], BF16) for _ in range(NXS)]
    p1 = {}

    def p1_a(kb):
        rows, col0 = blk_rows_col(kb)
        sl = kb % NXT
        src = meta if kb == 0 else x[(kb - 1) * 128:kb * 128, :]
        P.dma("sp", lambda e, sl=sl, rows=rows, src=src: e.dma_start(out=xt[sl][:rows, :], in_=src),
              writes=[("xt", sl)], key=("xt", sl))
        p1[kb] = rms_rstd(xt[sl][:rows, :], rows, sl, D, [("xt", sl)])

    def p1_b(kb):
        rows, col0 = blk_rows_col(kb)
        sl = kb % NXT
        ssl = kb % NXS
        rstd, rk = p1[kb]
        P.op("dve", lambda e, sl=sl, ssl=ssl, rows=rows, rstd=rstd: e.tensor_scalar(
            out=xs[ssl][:rows, :], in0=xt[sl][:rows, :], scalar1=rstd, scalar2=None, op0=ALU.mult),
            reads=[("xt", sl), rk], writes=[("xs", ssl)])
        pb = 4 + (kb % NPB)
        pv = bankb(pb).rearrange("p (c t) -> p c t", t=128)
        for c in range(8):
            P.op("pe", lambda e, c=c, ssl=ssl, rows=rows, pv=pv: e.transpose(
                out=pv[:, c, 0:rows], in_=xs[ssl][:rows, c * 128:(c + 1) * 128], identity=ident[:rows, :rows]),
                reads=[("xs", ssl), "ident"], writes=[PK(pb)])

    def p1_c(kb):
        rows, col0 = blk_rows_col(kb)
        pb = 4 + (kb % NPB)
        pv = bankb(pb).rearrange("p (c t) -> p c t", t=128)
        P.op("dve", lambda e, rows=rows, col0=col0, pv=pv: e.tensor_tensor(
            out=Zhn[:, :, col0:col0 + rows], in0=pv[:, :, 0:rows],
            in1=gmix[:, :].unsqueeze(2).broadcast_to([128, 8, rows]), op=ALU.mult),
            reads=["gmix"], writes=[PK(pb), ("Zhn", kb)])

    for i in range(33 + 2):
        if i < 33:
            p1_a(i)
        if 0 <= i - 1 < 33:
            p1_b(i - 1)
        if 0 <= i - 2 < 33:
            p1_c(i - 2)
    A.release(m1)
    if stage <= 1:
        return finish(nc, P, A, out, dbg, {"Zhn": (Zhn, [128, 8 * L], BF16)})

    P.barrier()

    m2 = A.mark()
    KT = A.alloc("KT", [128, L], BF16)
    Vaug = A.alloc("Vaug", [128, 33, 129], BF16)
    Wh = [A.alloc("Wh", [128, 8, 384], BF16) for _ in range(2)]
    QT = [A.alloc("QT", [128, 512], BF16) for _ in range(2)]
    Pt = [A.alloc("Pt", [128, 2, 512], BF16) for _ in range(3)]
    tmpf = [A.alloc("tmpf", [128, 2, 512], F32) for _ in range(2)]
    wtab = A.alloc("wtab", [128, 896], F32)
    acc = [A.alloc("acc", [128, 8, 129], F32) for _ in range(2)]
    yv4 = A.alloc("yv4", [128, 4, 128], F32)
    t4 = A.alloc("t4", [128, 4, 128], F32)
    ybf4 = A.alloc("ybf4", [128, 4, 128], BF16)
    rr8 = A.alloc("rr8", [128, 4, 2], F32)
    rn4 = A.alloc("rn4", [128, 4], F32)
    ss4 = A.alloc("ss4", [128, 3, 4], F32)
    mh4 = A.alloc("mh4", [128, 4], F32)
    P.op("pool", lambda e: e.memset(Vaug[:, :, 128:129], 1.0), writes=["Vones"])
    P.op("pool", lambda e: e.memset(mh4[:], -0.5), writes=["mh4"])
    all_zhn = [("Zhn", kb) for kb in range(33)]

    def load_wh(h):
        hs = h % 2
        for j, base in enumerate((0, 512, 1024)):
            P.dma("pool", lambda e, hs=hs, j=j, base=base, h=h: e.dma_start(
                out=Wh[hs][:, :, j * 128:(j + 1) * 128],
                in_=w_in[:, base + h * 128: base + (h + 1) * 128].rearrange("(c p) n -> p c n", p=128)),
                writes=[("Wh", hs, j)], key=("Wh", hs, j))

    load_wh(0)
    load_wh(1)
    pending_fin = []
    fill_q = []

    def flush_fin():
        while fill_q:
            fill_q.pop(0)()
        while pending_fin:
            T, hh = pending_fin.pop(0)
            for qb in range(4):
                P.op("pe", lambda e, qb=qb: e.transpose(out=bankb(7)[:, qb * 128:(qb + 1) * 128], in_=ybf4[:, qb, :], identity=ident[:, :]),
                     reads=["ybf4", "ident"], writes=[PK(7)])
            P.op("dve", lambda e, T=T, hh=hh: e.tensor_copy(out=Zy[:, hh, 512 * T:512 * T + 512], in_=bankb(7)[:, 0:512]),
                 writes=[PK(7)] + [("Zy", hh, 4 * T + qb) for qb in range(4)])

    fin_cnt = [0]
    pt_cnt = [0]
    st_cnt = [0]
    tf_cnt = [0]
    for h in range(4):
        hs = h % 2
        P.dma("sp", lambda e, h=h: e.dma_start(out=wtab[:, :], in_=wt_d[:, h * 896:(h + 1) * 896]), writes=["wtab"], key="c_wtab")
        for t in range(9):
            c0 = t * 512
            n = min(512, L - c0)
            pb = t % 4
            for c in range(8):
                P.op("pe", lambda e, c=c, pb=pb, n=n, c0=c0, hs=hs: e.matmul(
                    bank(pb)[:, 0:n], lhsT=Wh[hs][:, c, 128:256], rhs=Zhn[:, c, c0:c0 + n], start=(c == 0), stop=(c == 7)),
                    reads=[("Wh", hs, 1)] + all_zhn, writes=[PK(pb)])
            eng = "act" if t % 2 == 0 else "dve"
            if eng == "act":
                P.op("act", lambda e, pb=pb, n=n, c0=c0: e.activation(out=KT[:, c0:c0 + n], in_=bank(pb)[:, 0:n], func=AF.Copy),
                     writes=[PK(pb), ("KT", t)])
            else:
                P.op("dve", lambda e, pb=pb, n=n, c0=c0: e.tensor_copy(out=KT[:, c0:c0 + n], in_=bank(pb)[:, 0:n]),
                     writes=[PK(pb), ("KT", t)])
        flush_fin()
        for c in range(8):
            P.op("pe", lambda e, c=c, hs=hs: e.matmul(bank(4)[:16, 0:128], lhsT=Zhn[:, c, 0:16], rhs=Wh[hs][:, c, 256:384],
                                                      start=(c == 0), stop=(c == 7)),
                 reads=[("Wh", hs, 2)] + all_zhn, writes=[PK(4)])
        P.op("dve", lambda e: e.tensor_copy(out=Vaug[:16, 0, 0:128], in_=bank(4)[:16, 0:128]), writes=[PK(4), ("V", 0)])
        for q4 in range(8):
            pb = q4 % 4
            for i in range(4):
                kb = 1 + q4 * 4 + i
                col0 = 16 + 128 * (kb - 1)
                for c in range(8):
                    P.op("pe", lambda e, c=c, pb=pb, i=i, col0=col0, hs=hs: e.matmul(
                        bank(pb)[:, i * 128:(i + 1) * 128], lhsT=Zhn[:, c, col0:col0 + 128], rhs=Wh[hs][:, c, 256:384],
                        start=(c == 0), stop=(c == 7)),
                        reads=[("Wh", hs, 2)] + all_zhn, writes=[PK(pb)])
            kb0 = 1 + q4 * 4
            eng = "act" if q4 % 2 == 0 else "dve"
            if eng == "act":
                P.op("act", lambda e, pb=pb, kb0=kb0: e.activation(
                    out=Vaug[:, kb0:kb0 + 4, 0:128], in_=bank(pb).rearrange("p (i n) -> p i n", n=128), func=AF.Copy),
                    writes=[PK(pb), ("V", 1 + q4)])
            else:
                P.op("dve", lambda e, pb=pb, kb0=kb0: e.tensor_copy(
                    out=Vaug[:, kb0:kb0 + 4, 0:128], in_=bank(pb).rearrange("p (i n) -> p i n", n=128)),
                    writes=[PK(pb), ("V", 1 + q4)])
        all_kt = [("KT", t) for t in range(9)]
        all_v = [("V", i) for i in range(9)] + ["Vones"]
        def emit_qproj(T, immediate=False):
            qs = T % 2
            qcol0 = 16 + 512 * T
            jobs = []
            for c in range(8):
                jobs.append(lambda c=c, hs=hs, qcol0=qcol0: P.op("pe", lambda e: e.matmul(
                    bank(7)[:, :], lhsT=Wh[hs][:, c, 0:128], rhs=Zhn[:, c, qcol0:qcol0 + 512], start=(c == 0), stop=(c == 7)),
                    reads=[("Wh", hs, 0)] + all_zhn, writes=[PK(7)]))
            jobs.append(lambda qs=qs: P.op("dve", lambda e: e.tensor_copy(out=QT[qs][:, :], in_=bank(7)[:, :]),
                                           writes=[PK(7), ("QT", qs)]))
            if immediate:
                for j in jobs:
                    j()
            else:
                fill_q.extend(jobs)

        def drain_fill(n=None):
            k = 0
            while fill_q and (n is None or k < n):
                fill_q.pop(0)()
                k += 1

        THR = 60.0
        slope = SLOPE_A[h]
        items = []
        for T in range(8):
            below = [j for j in range(0, 4 * T) if slope * (128.0 * (4 * T - j - 1) + 1.0) < THR]
            above = [j for j in range(4 * T + 4, 32) if slope * (128.0 * (j - 4 * T - 4) + 1.0) < THR]
            groups = [("below", below), ("diag", ["meta", 4 * T, 4 * T + 1, 4 * T + 2, 4 * T + 3]), ("above", above)]
            groups = [g for g in groups if g[1]]
            for gi, (gname, blocks) in enumerate(groups):
                for bi, blk in enumerate(blocks):
                    items.append(dict(T=T, gname=gname, blk=blk, bi=bi, nb=len(blocks), first_group=(gi == 0),
                                      last_group=(gi == len(groups) - 1), first_in_tile=(gi == 0 and bi == 0)))

        def emit_qk_exp(it):
            T, gname, blk = it["T"], it["gname"], it["blk"]
            qs = T % 2
            buf = st_cnt[0] % 2
            st_cnt[0] += 1
            pbi = pt_cnt[0] % 3
            pt_cnt[0] += 1
            it["pbi"] = pbi
            if blk == "meta":
                rows, kcol0, kb = 16, 0, 0
            else:
                rows, kcol0, kb = 128, 16 + 128 * blk, blk + 1
            it["rows"], it["kb"] = rows, kb
            for s in range(2):
                P.op("pe", lambda e, s=s, buf=buf, rows=rows, kcol0=kcol0, qs=qs: e.matmul(
                    PS3[:rows, buf * 2 + s, :], lhsT=KT[s * 64:(s + 1) * 64, kcol0:kcol0 + rows],
                    rhs=QT[qs][s * 64:(s + 1) * 64, :], start=True, stop=True),
                    reads=all_kt + [("QT", qs)], writes=[PK(buf * 2 + s)])
            stin = PS3[:rows, buf * 2:buf * 2 + 2, :]
            if gname == "diag" and blk != "meta":
                jl = blk - 4 * T
                off = 384 - 128 * jl
                tb = tf_cnt[0] % 2
                tf_cnt[0] += 1
                for s in range(2):
                    P.op("dve", lambda e, s=s, buf=buf, tb=tb, off=off, h=h: e.scalar_tensor_tensor(
                        out=tmpf[tb][:, s, :], in0=PS3[:, buf * 2 + s, :], scalar=0.125,
                        in1=wtab[:, off:off + 512], op0=ALU.mult, op1=ALU.add),
                        reads=["wtab"], writes=[PK(buf * 2 + s), ("tmpf", tb, s)])
                P.op("act", lambda e, tb=tb, pbi=pbi: e.activation(out=Pt[pbi][:, :, :], in_=tmpf[tb][:, :, :], func=AF.Exp),
                     reads=[("tmpf", tb, 0), ("tmpf", tb, 1)], writes=[("Pt", pbi)])
            elif blk == "meta":
                P.op("act", lambda e, pbi=pbi, stin=stin, rows=rows: e.activation(
                    out=Pt[pbi][:rows, :, :], in_=stin, func=AF.Exp, scale=0.125),
                    writes=[PK(buf * 2), PK(buf * 2 + 1), ("Pt", pbi)])
            else:
                if gname == "below":
                    col = h * 32 + (4 * T - blk)
                else:
                    col = 128 + h * 32 + (blk - 4 * T - 4)
                P.op("act", lambda e, pbi=pbi, stin=stin, col=col: e.activation(
                    out=Pt[pbi][:, :, :], in_=stin, func=AF.Exp, scale=0.125, bias=cst[:, col:col + 1]),
                    reads=["cst"], writes=[PK(buf * 2), PK(buf * 2 + 1), ("Pt", pbi)])

        def oreg(s, qb):
            ri = qb * 2 + s
            return 4 + ri // 3, (ri % 3) * 129, ri

        def emit_pv_post(it):
            T, gname, bi, nb = it["T"], it["gname"], it["bi"], it["nb"]
            pbi, rows, kb = it["pbi"], it["rows"], it["kb"]
            ab = T % 2
            for qb in range(4):
                for s in range(2):
                    ob, oo, ri = oreg(s, qb)
                    P.op("pe", lambda e, s=s, qb=qb, ob=ob, oo=oo, ri=ri, pbi=pbi, rows=rows, kb=kb, bi=bi, nb=nb: e.matmul(
                        PS3[:, ob, oo:oo + 129], lhsT=Pt[pbi][:rows, s, qb * 128:(qb + 1) * 128],
                        rhs=Vaug[:rows, kb, 0:129], start=(bi == 0 and ri % 3 == 0), stop=(bi == nb - 1),
                        skip_group_check=True),
                        reads=all_v + [("Pt", pbi)], writes=[PK(ob)])
            if bi != nb - 1:
                return
            for qb in range(4):
                ob0, oo0, ri0 = oreg(0, qb)
                ob1, oo1, ri1 = oreg(1, qb)
                if ob0 == ob1:
                    pieces = [(ob0, oo0, ri0, 258, [("acc", ab, 0, qb), ("acc", ab, 1, qb)])]
                else:
                    pieces = [(ob0, oo0, ri0, 129, [("acc", ab, 0, qb)]), (ob1, oo1, ri1, 129, [("acc", ab, 1, qb)])]
                if gname == "below":
                    cap = cst[:, 256 + h * 4 + qb: 256 + h * 4 + qb + 1]
                elif gname == "above":
                    cap = cst[:, 272 + h * 4 + qb: 272 + h * 4 + qb + 1]
                else:
                    cap = None
                for (ob, oo, ri, w, akeys) in pieces:
                    src = PS3[:, ob, oo:oo + w]
                    dst = acc[ab][:].rearrange("p r c -> p (r c)")[:, ri * 129: ri * 129 + w]
                    if it["first_group"] and h < 2:
                        if cap is None:
                            P.op("act", lambda e, src=src, dst=dst: e.activation(out=dst, in_=src, func=AF.Copy),
                                 writes=[PK(ob)] + akeys)
                        else:
                            P.op("act", lambda e, src=src, dst=dst, cap=cap: e.activation(out=dst, in_=src, func=AF.Copy, scale=cap),
                                 reads=["cst"], writes=[PK(ob)] + akeys)
                    elif it["first_group"]:
                        if cap is None:
                            P.op("dve", lambda e, src=src, dst=dst: e.tensor_copy(out=dst, in_=src),
                                 writes=[PK(ob)] + akeys)
                        else:
                            P.op("dve", lambda e, src=src, dst=dst, cap=cap: e.tensor_scalar(
                                out=dst, in0=src, scalar1=cap, scalar2=None, op0=ALU.mult),
                                reads=["cst"], writes=[PK(ob)] + akeys)
                    else:
                        if cap is None:
                            P.op("dve", lambda e, src=src, dst=dst: e.tensor_tensor(out=dst, in0=src, in1=dst, op=ALU.add),
                                 writes=[PK(ob)] + akeys)
                        else:
                            P.op("dve", lambda e, src=src, dst=dst, cap=cap: e.scalar_tensor_tensor(
                                out=dst, in0=src, scalar=cap, in1=dst, op0=ALU.mult, op1=ALU.add),
                                reads=["cst"], writes=[PK(ob)] + akeys)
            if not it["last_group"]:
                return
            accv = acc[ab][:].rearrange("p (q s) c -> p q s c", s=2)
            akeys = [("acc", ab, s_, qb_) for s_ in range(2) for qb_ in range(4)]
            P.op("dve", lambda e, accv=accv: e.reciprocal(out=rr8[:, :, :], in_=accv[:, :, :, 128]),
                 reads=akeys, writes=["rr8"])
            P.op("pool", lambda e: e.tensor_scalar(out=rn4[:, :], in0=rr8[:, :, 1], scalar1=neglam, scalar2=None, op0=ALU.mult),
                 reads=["rr8", ("lams", 4)], writes=["rn4"])
            P.op("pool", lambda e, accv=accv: e.tensor_tensor(
                out=yv4[:, :, :], in0=accv[:, :, 0, 0:128], in1=rr8[:, :, 0].unsqueeze(2).broadcast_to([128, 4, 128]), op=ALU.mult),
                reads=akeys + ["rr8"], writes=["yv4"])
            P.op("pool", lambda e, accv=accv: e.tensor_tensor(
                out=t4[:, :, :], in0=accv[:, :, 1, 0:128], in1=rn4[:, :].unsqueeze(2).broadcast_to([128, 4, 128]), op=ALU.mult),
                reads=akeys + ["rn4"], writes=["t4"])
            P.op("pool", lambda e: e.tensor_tensor(out=yv4[:, :, :], in0=yv4[:, :, :], in1=t4[:, :, :], op=ALU.add),
                 reads=["t4"], writes=["yv4"])
            P.op("pool", lambda e: e.tensor_tensor(out=t4[:, :, :], in0=yv4[:, :, :], in1=yv4[:, :, :], op=ALU.mult),
                 reads=["yv4"], writes=["t4"])
            P.op("dve", lambda e: e.tensor_reduce(out=ss4[:, 0, :], in_=t4[:, :, :], axis=mybir.AxisListType.X, op=ALU.add),
                 reads=["t4"], writes=[("ss4", 0)])
            P.op("pool", lambda e: e.tensor_scalar(out=ss4[:, 1, :], in0=ss4[:, 0, :], scalar1=1.0 / 128, scalar2=EPS,
                                                   op0=ALU.mult, op1=ALU.add),
                 reads=[("ss4", 0)], writes=[("ss4", 1)])
            P.op("pool", lambda e: e.tensor_tensor(out=ss4[:, 2, :], in0=ss4[:, 1, :], in1=mh4[:, :], op=ALU.pow),
                 reads=[("ss4", 1), "mh4"], writes=[("ss4", 2)])
            P.op("pool", lambda e: e.tensor_tensor(
                out=yv4[:, :, :], in0=yv4[:, :, :], in1=ss4[:, 2, :].unsqueeze(2).broadcast_to([128, 4, 128]), op=ALU.mult),
                reads=[("ss4", 2)], writes=["yv4"])
            P.op("pool", lambda e: e.tensor_tensor(
                out=ybf4[:, :, :], in0=yv4[:, :, :], in1=subln[:, :].unsqueeze(1).broadcast_to([128, 4, 128]), op=ALU.mult),
                reads=["yv4", "subln"], writes=["ybf4"])
            pending_fin.append((T, h))
            issue_scratch(2)

        emit_qproj(0, immediate=True)
        prev = None
        since_fin = 0
        for it in items:
            if it["first_in_tile"]:
                drain_fill()
            emit_qk_exp(it)
            if it["first_in_tile"] and it["T"] + 1 < 8:
                emit_qproj(it["T"] + 1)
            drain_fill(2)
            if prev is not None:
                emit_pv_post(prev)
            prev = it
            if pending_fin:
                since_fin += 1
                if since_fin >= 14 or (it["last_group"] and it["bi"] >= it["nb"] - 2):
                    flush_fin()
                    since_fin = 0
        emit_pv_post(prev)
        drain_fill()
        if h + 2 < 4:
            load_wh(h + 2)
    flush_fin()
    issue_scratch(100)
    A.release(m2)
    if stage <= 2:
        return finish(nc, P, A, out, dbg, {"Zy": (Zy, [128, 8 * SEQ], BF16)})
    P.barrier()

    m3 = A.mark()
    KTb = A.alloc("KTb", [128, L], BF16)
    Vb = A.alloc("Vb", [128, 33, 2, 65], BF16)
    Wqb = A.alloc("Wqb", [128, 8, 512], BF16)
    Wkvb = A.alloc("Wkvb", [128, 8, 256], BF16)
    QTb = [A.alloc("QTb", [128, 4, 512], BF16) for _ in range(2)]
    bwt = A.alloc("bwt", [128, 2, 3, 512], F32)
    tmpb = [A.alloc("tmpb", [128, 512], F32) for _ in range(3)]
    Ptb = [A.alloc("Ptb", [128, 512], BF16) for _ in range(5)]
    d8 = [A.alloc("d8", [128, 8], F32) for _ in range(2)]
    ybb = [A.alloc("ybb", [128, 512], BF16) for _ in range(2)]
    osb = [A.alloc("osb", [128, 2, 260], F32) for _ in range(2)]
    mone4 = A.alloc("mone4", [128, 4], F32)
    P.dma("sp", lambda e: e.dma_start(out=bwt[:].rearrange("p g t n -> p (g t n)"), in_=bwt_d), writes=["bwt"], key="c_bwt")
    for g in range(2):
        for r in range(4):
            P.dma("pool", lambda e, g=g, r=r: e.dma_start(
                out=Wqb[:, :, r * 128 + g * 64:r * 128 + (g + 1) * 64],
                in_=w_in[:, 1536 + (g * 4 + r) * 64:1536 + (g * 4 + r + 1) * 64].rearrange("(c p) d -> p c d", p=128)),
                writes=[("Wqb", g, r)], key=("Wqb", g, r))
    P.dma("pool", lambda e: e.dma_start(out=Wkvb[:], in_=w_in[:, 2048:2304].rearrange("(c p) n -> p c n", p=128)),
          writes=["Wkvb"], key="Wkvb")
    P.op("pool", lambda e: e.memset(Vb[:, :, :, 64:65], 1.0), writes=["Vbones"])
    P.op("pool", lambda e: e.memset(mone4[:], -1.0), writes=["mone4"])
    for t in range(9):
        c0 = t * 512
        n = min(512, L - c0)
        pb = t % 4
        for c in range(8):
            P.op("pe", lambda e, c=c, pb=pb, n=n, c0=c0: e.matmul(
                bank(pb)[:, 0:n], lhsT=Wkvb[:, c, 0:128], rhs=Zhn[:, c, c0:c0 + n], start=(c == 0), stop=(c == 7)),
                reads=["Wkvb"] + all_zhn, writes=[PK(pb)])
        P.op("dve", lambda e, pb=pb, n=n, c0=c0: e.tensor_copy(out=KTb[:, c0:c0 + n], in_=bank(pb)[:, 0:n]),
             writes=[PK(pb), ("KTb", t)])
    for c in range(8):
        P.op("pe", lambda e, c=c: e.matmul(bank(4)[:16, 0:128], lhsT=Zhn[:, c, 0:16], rhs=Wkvb[:, c, 128:256],
                                           start=(c == 0), stop=(c == 7)),
             reads=["Wkvb"] + all_zhn, writes=[PK(4)])
    P.op("dve", lambda e: e.tensor_copy(out=Vb[:16, 0, :, 0:64], in_=bank(4)[:16, 0:128].rearrange("p (g d) -> p g d", d=64)),
         writes=[PK(4), ("Vb", 0)])
    for q4 in range(8):
        pb = q4 % 4
        for i in range(4):
            kb = 1 + q4 * 4 + i
            col0 = 16 + 128 * (kb - 1)
            for c in range(8):
                P.op("pe", lambda e, c=c, pb=pb, i=i, col0=col0: e.matmul(
                    bank(pb)[:, i * 128:(i + 1) * 128], lhsT=Zhn[:, c, col0:col0 + 128], rhs=Wkvb[:, c, 128:256],
                    start=(c == 0), stop=(c == 7)),
                    reads=["Wkvb"] + all_zhn, writes=[PK(pb)])
        kb0 = 1 + q4 * 4
        for g in range(2):
            P.op("dve", lambda e, pb=pb, kb0=kb0, g=g: e.tensor_copy(
                out=Vb[:, kb0:kb0 + 4, g, 0:64],
                in_=bank(pb).rearrange("p (i g d) -> p i g d", g=2, d=64)[:, :, g, :]),
                writes=[PK(pb), ("Vb", 1 + q4, g)])
    all_ktb = [("KTb", t) for t in range(9)]
    all_vb = [("Vb", 0), "Vbones"] + [("Vb", 1 + q4, g) for q4 in range(8) for g in range(2)]
    stb_cnt = [0]
    ptb_cnt = [0]
    tb_cnt = [0]
    def emit_qbproj(t8, rs=(0, 1, 2, 3)):
        qs = t8 % 2
        qc0 = 16 + 512 * t8
        for r in rs:
            for c in range(8):
                P.op("pe", lambda e, c=c, r=r, qc0=qc0: e.matmul(
                    bank(6)[:, :], lhsT=Wqb[:, c, r * 128:(r + 1) * 128], rhs=Zhn[:, c, qc0:qc0 + 512],
                    start=(c == 0), stop=(c == 7)),
                    reads=[("Wqb", g_, r_) for g_ in range(2) for r_ in range(4)] + all_zhn, writes=[PK(6)])
            if r % 2 == 0:
                P.op("act", lambda e, r=r, qs=qs: e.activation(out=QTb[qs][:, r, :], in_=bank(6)[:, :], func=AF.Copy),
                     writes=[PK(6), ("QTb", qs, r)])
            else:
                P.op("dve", lambda e, r=r, qs=qs: e.tensor_copy(out=QTb[qs][:, r, :], in_=bank(6)[:, :]),
                     writes=[PK(6), ("QTb", qs, r)])

    bitems = []
    for i in range(32):
        types = [("meta", None)]
        if i - 1 >= 0:
            types.append((0, i - 1))
        types.append((1, i))
        if i + 1 <= 31:
            types.append((2, i + 1))
        for ti, (ty, j) in enumerate(types):
            for g in range(2):
                bitems.append(dict(i=i, ty=ty, j=j, ti=ti, nt=len(types), g=g))

    def emit_b_qk(it):
        i, ty, j, g = it["i"], it["ty"], it["j"], it["g"]
        t8, qi = i // 4, i % 4
        qs = t8 % 2
        qoff = qi * 128
        qkeys = [("QTb", qs, r) for r in range(4)]
        sb = stb_cnt[0] % 4
        stb_cnt[0] += 1
        pbi = ptb_cnt[0] % 5
        ptb_cnt[0] += 1
        if ty == "meta":
            rows, kcol0, kb = 16, 0, 0
        else:
            rows, kcol0, kb = 128, 16 + 128 * j, j + 1
        it["pbi"], it["rows"], it["kb"], it["sb"] = pbi, rows, kb, sb
        P.op("pe", lambda e, g=g, sb=sb, rows=rows, kcol0=kcol0, qs=qs, qoff=qoff: e.matmul(
            PS3[:rows, sb, :].rearrange("p (r q) -> p r q", q=128),
            lhsT=KTb[g * 64:(g + 1) * 64, kcol0:kcol0 + rows],
            rhs=QTb[qs][g * 64:(g + 1) * 64, :, qoff:qoff + 128], start=True, stop=True),
            reads=all_ktb + qkeys, writes=[PK(sb)])

    def emit_b_exp(it):
        ty, g, sb, pbi = it["ty"], it["g"], it["sb"], it["pbi"]
        if ty == "meta":
            P.op("act", lambda e, pbi=pbi, sb=sb: e.activation(
                out=Ptb[pbi][:16, :], in_=PS3[:16, sb, :], func=AF.Exp, scale=0.125),
                writes=[PK(sb), ("Ptb", pbi)])
        else:
            tb = tb_cnt[0] % 3
            tb_cnt[0] += 1
            P.op("dve", lambda e, g=g, sb=sb, tb=tb, ty=ty: e.scalar_tensor_tensor(
                out=tmpb[tb][:, :], in0=PS3[:, sb, :], scalar=0.125, in1=bwt[:, g, ty, :],
                op0=ALU.mult, op1=ALU.add),
                reads=["bwt"], writes=[PK(sb), ("tmpb", tb)])
            P.op("act", lambda e, tb=tb, pbi=pbi: e.activation(out=Ptb[pbi][:, :], in_=tmpb[tb][:, :], func=AF.Exp),
                 reads=[("tmpb", tb)], writes=[("Ptb", pbi)])

    def emit_b_pv_post(it):
        i, ti, nt, g = it["i"], it["ti"], it["nt"], it["g"]
        pbi, rows, kb = it["pbi"], it["rows"], it["kb"]
        fsl = i % 2
        for r in range(4):
            P.op("pe", lambda e, g=g, r=r, pbi=pbi, rows=rows, kb=kb, ti=ti, nt=nt: e.matmul(
                PS3[:, 4 + g, r * 65:(r + 1) * 65], lhsT=Ptb[pbi][:rows, r * 128:(r + 1) * 128],
                rhs=Vb[:rows, kb, g, 0:65], start=(ti == 0 and r == 0), stop=(ti == nt - 1),
                skip_group_check=True),
                reads=all_vb + [("Ptb", pbi)], writes=[PK(4 + g)])
        if ti != nt - 1:
            return
        P.op("act", lambda e, g=g, fsl=fsl: e.activation(out=osb[fsl][:, g, :], in_=PS3[:, 4 + g, 0:260], func=AF.Copy),
             writes=[PK(4 + g), ("osb", fsl, g)])
        ov = osb[fsl][:, g, :].rearrange("p (r e) -> p r e", e=65)
        P.op("pool", lambda e, g=g, ov=ov, fsl=fsl: e.tensor_tensor(
            out=d8[fsl][:, g * 4:(g + 1) * 4], in0=ov[:, :, 64], in1=esink[:, g * 4:(g + 1) * 4], op=ALU.add),
            reads=["esink", ("osb", fsl, g)], writes=[("d8", fsl, g)])
        P.op("pool", lambda e, g=g, fsl=fsl: e.tensor_tensor(
            out=d8[fsl][:, g * 4:(g + 1) * 4], in0=d8[fsl][:, g * 4:(g + 1) * 4], in1=mone4[:, :], op=ALU.pow),
            reads=["mone4"], writes=[("d8", fsl, g)])
        P.op("pool", lambda e, g=g, ov=ov, fsl=fsl: e.tensor_tensor(
            out=ybb[fsl][:, g * 256:(g + 1) * 256].rearrange("p (r d) -> p r d", d=64), in0=ov[:, :, 0:64],
            in1=d8[fsl][:, g * 4:(g + 1) * 4].unsqueeze(2).broadcast_to([128, 4, 64]), op=ALU.mult),
            reads=[("d8", fsl, g), ("osb", fsl, g)], writes=[("ybb", fsl, g)])
        if g == 1:
            pending_b.append(i)

    pending_b = []

    def flush_b():
        tv = bankb(7).rearrange("p (c t) -> p c t", t=128)
        while pending_b:
            i = pending_b.pop(0)
            fsl = i % 2
            for cc in range(4):
                P.op("pe", lambda e, cc=cc, fsl=fsl, tv=tv: e.transpose(
                    out=tv[:, fsl * 4 + cc, :], in_=ybb[fsl][:, cc * 128:(cc + 1) * 128], identity=ident[:, :]),
                    reads=[("ybb", fsl, 0), ("ybb", fsl, 1), "ident"], writes=[PK(7)])
            P.op("dve", lambda e, fsl=fsl, tv=tv, i=i: e.tensor_copy(
                out=Zy[:, 4:8, i * 128:(i + 1) * 128], in_=tv[:, fsl * 4:fsl * 4 + 4, :]),
                writes=[PK(7), ("Zy", 4, i)])

    emit_qbproj(0)
    since_b = 0
    npairs = len(bitems) // 2
    for k in range(npairs):
        a, b = bitems[2 * k], bitems[2 * k + 1]
        emit_b_qk(a)
        emit_b_qk(b)
        emit_b_exp(a)
        emit_b_exp(b)
        if a["ti"] == 1 and a["i"] // 4 + 1 < 8:
            emit_qbproj(a["i"] // 4 + 1, rs=(a["i"] % 4,))
        if k >= 1:
            emit_b_pv_post(bitems[2 * k - 2])
            emit_b_pv_post(bitems[2 * k - 1])
        if pending_b:
            since_b += 1
            if since_b >= 3:
                flush_b()
                since_b = 0
    emit_b_pv_post(bitems[-2])
    emit_b_pv_post(bitems[-1])
    flush_b()
    A.release(m3)
    if stage <= 3:
        return finish(nc, P, A, out, dbg, {"Zy": (Zy, [128, 8 * SEQ], BF16)})
    P.barrier()

    m4 = A.mark()
    Wga = A.alloc("Wga", [128, 8, D], BF16)
    Wgb = A.alloc("Wgb", [128, 8, D], BF16)
    Wba = A.alloc("Wba", [128, 4, D], BF16)
    Wbb = A.alloc("Wbb", [128, 4, D], BF16)
    mT = A.alloc("mT", [128, 8, 512], BF16)
    sg = [A.alloc("sg", [128, 2, 512], F32) for _ in range(2)]
    P.dma("pool", lambda e: e.dma_start(out=Wga[:], in_=w_in[:, 2304:3328].rearrange("(c p) n -> p c n", p=128)), writes=["Wga"], key="Wga")
    P.dma("pool", lambda e: e.dma_start(out=Wgb[:], in_=w_in[:, 3328:4352].rearrange("(c p) n -> p c n", p=128)), writes=["Wgb"], key="Wgb")
    P.dma("pool", lambda e: e.dma_start(out=Wba[:], in_=w_ba.rearrange("(c p) n -> p c n", p=128)), writes=["Wba"], key="Wba")
    P.dma("pool", lambda e: e.dma_start(out=Wbb[:], in_=w_bb.rearrange("(c p) n -> p c n", p=128)), writes=["Wbb"], key="Wbb")
    for t8 in range(8):
        zc0 = 16 + 512 * t8
        yc0 = 512 * t8
        zkeys = [("Zhn", 1 + 4 * t8 + i) for i in range(4)]
        ykeys = [("Zy", hh, 4 * t8 + i) for hh in range(5) for i in range(4)]
        for f in range(8):
            b0 = (f % 2) * 4
            sl = f % 2
            for (w, bnk, nchunk, src, coff, wk, rk) in ((Wga, b0, 8, Zhn, zc0, "Wga", zkeys), (Wba, b0 + 1, 4, Zy, yc0, "Wba", ykeys),
                                                         (Wgb, b0 + 2, 8, Zhn, zc0, "Wgb", zkeys), (Wbb, b0 + 3, 4, Zy, yc0, "Wbb", ykeys)):
                for c in range(nchunk):
                    cs = c if w is not Wbb else 4 + c
                    P.op("pe", lambda e, w=w, bnk=bnk, c=c, cs=cs, nchunk=nchunk, src=src, coff=coff, f=f: e.matmul(
                        bank(bnk)[:, :], lhsT=w[:, c, f * 128:(f + 1) * 128], rhs=src[:, cs, coff:coff + 512],
                        start=(c == 0), stop=(c == nchunk - 1)),
                        reads=[wk] + rk, writes=[PK(bnk)])
            P.op("act", lambda e, b0=b0, sl=sl: e.activation(out=sg[sl][:, 0, :], in_=bank(b0)[:, :], func=AF.Sigmoid),
                 writes=[PK(b0), ("sg", sl, 0)])
            P.op("act", lambda e, b0=b0, sl=sl: e.activation(out=sg[sl][:, 1, :], in_=bank(b0 + 2)[:, :], func=AF.Sigmoid),
                 writes=[PK(b0 + 2), ("sg", sl, 1)])
            P.op("dve", lambda e, b0=b0, sl=sl: e.tensor_tensor(out=sg[sl][:, 0, :], in0=bank(b0 + 1)[:, :], in1=sg[sl][:, 0, :], op=ALU.mult),
                 writes=[PK(b0 + 1), ("sg", sl, 0)])
            P.op("dve", lambda e, b0=b0, sl=sl: e.tensor_tensor(out=sg[sl][:, 1, :], in0=bank(b0 + 3)[:, :], in1=sg[sl][:, 1, :], op=ALU.mult),
                 writes=[PK(b0 + 3), ("sg", sl, 1)])
            P.op("pool", lambda e, sl=sl, f=f: e.tensor_tensor(out=mT[:, f, :], in0=sg[sl][:, 0, :], in1=sg[sl][:, 1, :], op=ALU.add),
                 reads=[("sg", sl, 0), ("sg", sl, 1)], writes=[("mT", f)])
        P.op("pool", lambda e, zc0=zc0: e.tensor_copy(out=Zhn[:, :, zc0:zc0 + 512], in_=mT[:, :, :]),
             reads=[("mT", f) for f in range(8)], writes=zkeys)
    A.release(m4)
    if stage <= 4:
        return finish(nc, P, A, out, dbg, {"Zhn": (Zhn, [128, 8 * L], BF16)})
    P.barrier()

    A.release(zy_off)
    Wo = A.alloc("Wo", [128, 8, D], BF16)
    h1b = [A.alloc("h1", [128, 4, D], F32) for _ in range(2)]
    xin = [A.alloc("xin", [128, D], F32) for _ in range(2)]
    hsb = [A.alloc("hsb", [128, D], BF16) for _ in range(2)]
    hn1T = A.alloc("hn1T", [128, 8, 512], BF16)
    hidT = A.alloc("hidT", [128, 22, 512], BF16)
    outt = [A.alloc("outt", [128, D], F32) for _ in range(2)]
    sgf = [A.alloc("sgf", [128, 512], F32) for _ in range(2)]
    NGU, NWD = 3, 2
    Wgu = [A.alloc("Wgu", [128, 8, 2, 256], BF16) for _ in range(NGU)]
    Wd = [A.alloc("Wd", [128, 2, 1024], BF16) for _ in range(NWD)]
    P.dma("pool", lambda e: e.dma_start(out=Wo[:], in_=w_out.rearrange("(c p) n -> p c n", p=128)), writes=["Wo"], key="Wo")
    gu_cnt = [0]
    wd_cnt = [0]
    xi_cnt = [0]
    ot_cnt = [0]

    def load_gu(fg):
        sl = gu_cnt[0] % NGU
        gu_cnt[0] += 1
        P.dma("sp", lambda e, sl=sl, fg=fg: e.dma_start(out=Wgu[sl][:], in_=scr_gu[fg]), reads=scr_keys,
              writes=[("Wgu", sl)], key=("Wgu", sl))
        return sl

    def load_wd(fd):
        sl = wd_cnt[0] % NWD
        wd_cnt[0] += 1
        P.dma("sp", lambda e, sl=sl, fd=fd: e.dma_start(out=Wd[sl][:], in_=scr_d[fd]), reads=scr_keys,
              writes=[("Wd", sl)], key=("Wd", sl))
        return sl

    def stage_w(t8, blk):
        hb = t8 % 2
        h1 = h1b[hb]
        tok0 = 512 * t8 + 128 * blk
        zcol = 16 + tok0
        xsl = xi_cnt[0] % 2
        xi_cnt[0] += 1
        P.dma("sp", lambda e, xsl=xsl, tok0=tok0: e.dma_start(out=xin[xsl][:], in_=x[tok0:tok0 + 128, :]),
              writes=[("xin", xsl)], key=("xin", xsl))
        zk = [("Zhn", 1 + tok0 // 128)]
        b0 = (blk % 2) * 2
        for half in range(2):
            for f in range(8):
                P.op("pe", lambda e, half=half, f=f, zcol=zcol, b0=b0: e.matmul(
                    bank(b0 + half)[:, :], lhsT=Zhn[:, f, zcol:zcol + 128], rhs=Wo[:, f, half * 512:(half + 1) * 512],
                    start=(f == 0), stop=(f == 7)),
                    reads=["Wo"] + zk, writes=[PK(b0 + half)])
            P.op("dve", lambda e, half=half, blk=blk, xsl=xsl, b0=b0, h1=h1: e.tensor_tensor(
                out=h1[:, blk, half * 512:(half + 1) * 512], in0=bank(b0 + half)[:, :], in1=xin[xsl][:, half * 512:(half + 1) * 512],
                op=ALU.add),
                reads=[("xin", xsl)], writes=[PK(b0 + half), ("h1", hb, blk, half)])
        rstd, rk = rms_rstd(h1[:, blk, :], 128, blk, D, [("h1", hb, blk, 0), ("h1", hb, blk, 1)])
        hsl = blk % 2
        P.op("dve", lambda e, hsl=hsl, blk=blk, rstd=rstd, h1=h1: e.tensor_scalar(
            out=hsb[hsl][:, :], in0=h1[:, blk, :], scalar1=rstd, scalar2=None, op0=ALU.mult),
            reads=[("h1", hb, blk, 0), ("h1", hb, blk, 1), rk], writes=[("hsb", hsl)])

    def stage_t(t8, blk):
        hsl = blk % 2
        pb = 4 + (blk % 2)
        pv = bankb(pb).rearrange("p (c t) -> p c t", t=128)
        for c in range(8):
            P.op("pe", lambda e, c=c, hsl=hsl, pv=pv: e.transpose(
                out=pv[:, c, :], in_=hsb[hsl][:, c * 128:(c + 1) * 128], identity=ident[:, :]),
                reads=[("hsb", hsl), "ident"], writes=[PK(pb)])
        P.op("dve", lambda e, blk=blk, pv=pv: e.tensor_tensor(
            out=hn1T[:, :, blk * 128:(blk + 1) * 128], in0=pv[:, :, :],
            in1=gffn[:, :].unsqueeze(2).broadcast_to([128, 8, 128]), op=ALU.mult),
            reads=["gffn"], writes=[PK(pb), ("hn1T", blk)])

    def stage_a(t8):
        stage_w(t8, 0)
        stage_w(t8, 1)
        stage_t(t8, 0)
        stage_w(t8, 2)
        stage_t(t8, 1)
        stage_w(t8, 3)
        stage_t(t8, 2)
        stage_t(t8, 3)

    def final_blk(t8, blk):
        hb = t8 % 2
        h1 = h1b[hb]
        tok0 = 512 * t8 + 128 * blk
        rstd, rk = rms_rstd(h1[:, blk, :], 128, 4 + blk, D, [("h1", hb, blk, 0), ("h1", hb, blk, 1)])
        osl = ot_cnt[0] % 2
        ot_cnt[0] += 1
        P.op("dve", lambda e, osl=osl, blk=blk, rstd=rstd, h1=h1: e.scalar_tensor_tensor(
            out=outt[osl][:, :], in0=h1[:, blk, :], scalar=rstd, in1=gfin[:, :], op0=ALU.mult, op1=ALU.mult),
            reads=[("h1", hb, blk, 0), ("h1", hb, blk, 1), rk, "gfin"], writes=[("outt", osl)])
        P.dma("sp", lambda e, osl=osl, tok0=tok0: e.dma_start(out=out[tok0:tok0 + 128, :], in_=outt[osl][:, :]),
              reads=[("outt", osl)], writes=[("outd", tok0)], key=("outt", osl))

    hkeys = [("hn1T", b) for b in range(4)]
    deferred = []
    stage_a(0)
    for t8 in range(8):
        hb = t8 % 2
        h1 = h1b[hb]
        pending = []
        for fg in range(min(NGU - 1, 11)):
            pending.append(load_gu(fg))
        nxt = len(pending)
        for fg in range(11):
            if nxt < 11:
                pending.append(load_gu(nxt))
                nxt += 1
            sl = pending.pop(0)
            for fi in range(2):
                f = fg * 2 + fi
                bg = 4 + (f % 2) * 2
                for tsel in range(2):
                    for c in range(8):
                        P.op("pe", lambda e, c=c, sl=sl, tsel=tsel, fi=fi, bg=bg: e.matmul(
                            bank(bg + tsel)[:, :], lhsT=Wgu[sl][:, c, tsel, fi * 128:(fi + 1) * 128], rhs=hn1T[:, c, :],
                            start=(c == 0), stop=(c == 7)),
                            reads=[("Wgu", sl)] + hkeys, writes=[PK(bg + tsel)])
                ssl = f % 2
                P.op("act", lambda e, bg=bg, ssl=ssl: e.activation(out=sgf[ssl][:, :], in_=bank(bg)[:, :], func=AF.Silu),
                     writes=[PK(bg), ("sgf", ssl)])
                P.op("dve", lambda e, bg=bg, ssl=ssl, f=f: e.tensor_tensor(
                    out=hidT[:, f, :], in0=bank(bg + 1)[:, :], in1=sgf[ssl][:, :], op=ALU.mult),
                    reads=[("sgf", ssl)], writes=[PK(bg + 1), ("hidT", f)])
                if deferred and f in (1, 5, 9, 13):
                    deferred.pop(0)()
        while deferred:
            deferred.pop(0)()
        pending = []
        for fd in range(min(NWD - 1, 11)):
            pending.append(load_wd(fd))
        nxt = len(pending)
        for fd in range(11):
            if nxt < 11:
                pending.append(load_wd(nxt))
                nxt += 1
            sl = pending.pop(0)
            for ff in range(2):
                f = fd * 2 + ff
                for blk in range(4):
                    for half in range(2):
                        P.op("pe", lambda e, f=f, ff=ff, sl=sl, blk=blk, half=half: e.matmul(
                            bank(blk * 2 + half)[:, :], lhsT=hidT[:, f, blk * 128:(blk + 1) * 128],
                            rhs=Wd[sl][:, ff, half * 512:(half + 1) * 512], start=(f == 0), stop=(f == 21)),
                            reads=[("Wd", sl), ("hidT", f)], writes=[PK(blk * 2 + half)])
        for blk in range(4):
            for half in range(2):
                P.op("dve", lambda e, blk=blk, half=half, h1=h1: e.tensor_tensor(
                    out=h1[:, blk, half * 512:(half + 1) * 512], in0=bank(blk * 2 + half)[:, :],
                    in1=h1[:, blk, half * 512:(half + 1) * 512], op=ALU.add),
                    writes=[PK(blk * 2 + half), ("h1", hb, blk, half)])
        if t8 + 1 < 8:
            stage_a(t8 + 1)
            for blk in range(4):
                deferred.append(lambda t8=t8, blk=blk: final_blk(t8, blk))
        else:
            for blk in range(4):
                final_blk(t8, blk)
    return finish(nc, P, A, out, dbg, {})


def finish(nc, P, A, out, dbg, dumps):
    fw = [k for k in P.dma_counts if isinstance(k, tuple) and k[0] == "outt"]
    if dbg:
        for name, (t, shape, dt) in dumps.items():
            d = nc.dram_tensor("dbg_" + name, list(shape), dt, kind="ExternalOutput").ap()
            P.barrier()
            flat = t[:].rearrange("p a b -> p (a b)") if len(t.shape) == 3 else t[:]
            P.dma("sp", lambda e, d=d, flat=flat: e.dma_start(out=d, in_=flat), key=("dbg", name))
            fw.append(("dbg", name))
    P.emit(final_waits=fw)
    return nc


_CACHE = {}


def kernel(x, meta_tokens, norm_mix, w_in, lambda_q1, lambda_k1, lambda_q2, lambda_k2, subln_gain, sink_logits,
           w_branch_a, w_branch_b, w_out, norm_ffn, w_ff_gate, w_ff_up, w_ff_down, norm_final, _stage=99, _dbg=False):
    f32 = lambda a: np.ascontiguousarray(np.asarray(a, dtype=np.float32))
    cst, wt, bw, ident = make_consts()
    nc = build_nc(_stage, _dbg)
    shared = {
        "meta_tokens": f32(meta_tokens), "w_in": f32(w_in)[0], "w_branch_a": f32(w_branch_a)[0],
        "w_branch_b": f32(w_branch_b)[0], "w_out": f32(w_out)[0], "w_ff_gate": f32(w_ff_gate)[0],
        "w_ff_up": f32(w_ff_up)[0], "w_ff_down": f32(w_ff_down)[0], "norm_mix": f32(norm_mix)[0],
        "norm_ffn": f32(norm_ffn)[0], "norm_final": f32(norm_final), "lambda_q1": f32(lambda_q1)[0],
        "lambda_k1": f32(lambda_k1)[0], "lambda_q2": f32(lambda_q2)[0], "lambda_k2": f32(lambda_k2)[0],
        "subln_gain": f32(subln_gain)[0], "sink_logits": f32(sink_logits)[0],
        "cst": cst, "wtab": wt, "bwtab": bw, "ident": ident,
    }
    xx = f32(x)
    ncores = xx.shape[0] if not _dbg else 1
    in_maps = [dict(shared, x=xx[b]) for b in range(ncores)]
    res = run_bass_kernel_spmd(nc, in_maps, core_ids=list(range(ncores)))
    if _dbg:
        return res.results
    return np.stack([np.asarray(r["out"], dtype=np.float32) for r in res.results], axis=0)
```

```python
import os
import numpy as np
import ml_dtypes
import concourse.bass as bass
import concourse.mybir as mybir
from concourse.bass_utils import run_bass_kernel_spmd

F32 = mybir.dt.float32
BF16 = mybir.dt.bfloat16
AF = mybir.ActivationFunctionType
ALU = mybir.AluOpType

SEM_CAP = 16000
D = 1024
SEQ = 4096
L = 4112
NMETA = 16
DFF = 2816
PROJ_W = 4352
EPS = 1e-6
LAMBDA_INIT = 0.8 - 0.6 * 1.0
SLOPE_A = [2.0 ** (-2.0 * (i + 1)) for i in range(4)]
SLOPE_B = [2.0 ** (-1.0 * (i + 1)) for i in range(8)]
NEG = -30000.0


class Ins:
    __slots__ = ("eng", "fn", "deps", "signal", "count", "is_dma", "dkey", "didx", "seq")

    def __init__(self, eng, fn, is_dma=False, dkey=None):
        self.eng = eng
        self.fn = fn
        self.deps = []
        self.signal = False
        self.count = 0
        self.is_dma = is_dma
        self.dkey = dkey
        self.didx = 0


class Prog:
    ENGS = ("pe", "act", "dve", "pool", "sp")

    def __init__(self, nc):
        self.nc = nc
        self.lists = {e: [] for e in self.ENGS}
        self.bw = {}
        self.br = {}
        self.dma_counts = {}
        self.pending = {e: [] for e in self.ENGS}
        self.dmas_since = []

    def _add(self, ins, reads, writes):
        deps = []
        if self.pending[ins.eng]:
            deps.extend(self.pending[ins.eng])
            self.pending[ins.eng] = []
        for k in reads:
            deps.extend(self.bw.get(k, ()))
        for k in writes:
            deps.extend(self.bw.get(k, ()))
            deps.extend(self.br.get(k, ()))
        best = {}
        for d in deps:
            if d is ins:
                continue
            if d.is_dma:
                k = ("d", d.dkey)
                if k not in best or best[k].didx < d.didx:
                    best[k] = d
            else:
                if d.eng == "pe" and ins.eng == "pe" and not ins.is_dma:
                    continue
                k = ("e", d.eng)
                if k not in best or best[k].seq < d.seq:
                    best[k] = d
        for d in best.values():
            ins.deps.append(d)
            d.signal = True
        ins.seq = len(self.lists[ins.eng])
        for k in reads:
            self.br.setdefault(k, []).append(ins)
        for k in writes:
            self.bw[k] = [ins]
            self.br[k] = []
        self.lists[ins.eng].append(ins)
        return ins

    def op(self, eng, fn, reads=(), writes=()):
        return self._add(Ins(eng, fn), reads, writes)

    def dma(self, eng, fn, reads=(), writes=(), key=None):
        ins = Ins(eng, fn, is_dma=True, dkey=key)
        n = self.dma_counts.get(key, 0) + 1
        self.dma_counts[key] = n
        ins.didx = n
        ins.signal = True
        self.dmas_since.append(ins)
        return self._add(ins, reads, writes)

    def barrier(self):
        lasts = []
        for e in self.ENGS:
            for i in reversed(self.lists[e]):
                if not i.is_dma:
                    lasts.append(i)
                    break
        lasts += self.dmas_since
        self.dmas_since = []
        for d in lasts:
            d.signal = True
        for e in self.ENGS:
            self.pending[e] = self.pending[e] + list(lasts)

    def emit(self, final_waits=()):
        nc = self.nc
        nsig = {}
        for e in self.ENGS:
            c = 0
            for ins in self.lists[e]:
                if ins.is_dma:
                    continue
                if ins.signal:
                    c += 1
                    ins.count = c
            nsig[e] = c
        esems = {}
        for e in self.ENGS:
            nep = (nsig[e] + SEM_CAP - 1) // SEM_CAP
            esems[e] = [nc.alloc_semaphore(f"s_{e}_{i}") for i in range(max(nep, 1))]
        dsems = {k: nc.alloc_semaphore(f"d_{i}") for i, k in enumerate(self.dma_counts)}
        lists = self.lists
        dma_counts = self.dma_counts

        def run(e, eng):
            waited = {}
            for ins in lists[e]:
                need = {}
                for d in ins.deps:
                    if d.is_dma:
                        sem = ("d", d.dkey)
                        val = (0, 16 * d.didx)
                    else:
                        sem = ("e", d.eng)
                        n = d.count
                        val = ((n - 1) // SEM_CAP, (n - 1) % SEM_CAP + 1)
                    if need.get(sem, (-1, -1)) < val:
                        need[sem] = val
                for sem, val in need.items():
                    if waited.get(sem, (-1, -1)) >= val:
                        continue
                    waited[sem] = val
                    if sem[0] == "d":
                        eng.wait_ge(dsems[sem[1]], val[1])
                    else:
                        eng.wait_ge(esems[sem[1]][val[0]], val[1])
                r = ins.fn(eng)
                if ins.is_dma:
                    r.then_inc(dsems[ins.dkey], 16)
                elif ins.signal:
                    n = ins.count
                    r.then_inc(esems[e][(n - 1) // SEM_CAP], 1)
            if e == "sp":
                for k in final_waits:
                    eng.wait_ge(dsems[k], 16 * dma_counts[k])

        with nc.Block() as block:
            @block.tensor
            def _(eng):
                run("pe", eng)

            @block.scalar
            def _(eng):
                run("act", eng)

            @block.vector
            def _(eng):
                run("dve", eng)

            @block.gpsimd
            def _(eng):
                run("pool", eng)

            @block.sync
            def _(eng):
                run("sp", eng)


class Arena:
    def __init__(self, nc, base, limit):
        self.nc = nc
        self.off = base
        self.limit = limit
        self.n = 0

    def alloc(self, name, shape, dt):
        nbytes = int(np.prod(shape[1:])) * (2 if dt == BF16 else 4)
        o = (self.off + 63) // 64 * 64
        assert o + nbytes <= self.limit, (name, o, nbytes, self.limit)
        self.off = o + nbytes
        self.n += 1
        return self.nc.alloc_sbuf_tensor_at(f"{name}_{self.n}", list(shape), dt, offset=o)

    def mark(self):
        return self.off

    def release(self, m):
        self.off = m


def make_consts():
    kl = np.arange(128, dtype=np.float64)[:, None]
    cst = np.zeros((128, 288), np.float64)
    for h in range(4):
        s = SLOPE_A[h]
        for d in range(32):
            cst[:, h * 32 + d] = s * (kl[:, 0] - 128.0 * d)
            cst[:, 128 + h * 32 + d] = -s * (128.0 * d + kl[:, 0] + 1.0)
        for qb in range(4):
            cst[:, 256 + h * 4 + qb] = np.exp(-s * (128.0 * qb + kl[:, 0]))
            cst[:, 272 + h * 4 + qb] = np.exp(-s * (511.0 - 128.0 * qb - kl[:, 0]))
    wt = np.zeros((128, 4, 896), np.float64)
    c = np.arange(896, dtype=np.float64)[None, :]
    for h in range(4):
        wt[:, h, :] = -SLOPE_A[h] * np.abs(c - 384.0 - kl)
    bw = np.zeros((128, 2, 3, 4, 128), np.float64)
    ql = np.arange(128, dtype=np.float64)[None, :]
    for g in range(2):
        for r in range(4):
            s = SLOPE_B[g * 4 + r]
            d0 = 128.0 + ql - kl
            bw[:, g, 0, r, :] = np.where(d0 <= 128.0, -s * d0, NEG)
            bw[:, g, 1, r, :] = -s * np.abs(ql - kl)
            d2 = 128.0 + kl - ql
            bw[:, g, 2, r, :] = np.where(d2 <= 128.0, -s * d2, NEG)
    return (cst.astype(np.float32), wt.reshape(128, 3584).astype(np.float32),
            bw.reshape(128, 3072).astype(np.float32), np.eye(128).astype(ml_dtypes.bfloat16))


def build_nc(stage=99, dbg=False):
    nc = bass.Bass("TRN2", target_bir_lowering=False)

    def din(name, shape, dt=F32):
        return nc.dram_tensor(name, list(shape), dt, kind="ExternalInput").ap()

    x = din("x", [SEQ, D])
    meta = din("meta_tokens", [NMETA, D])
    w_in = din("w_in", [D, PROJ_W])
    w_ba = din("w_branch_a", [512, D])
    w_bb = din("w_branch_b", [512, D])
    w_out = din("w_out", [D, D])
    w_g = din("w_ff_gate", [D, DFF])
    w_u = din("w_ff_up", [D, DFF])
    w_d = din("w_ff_down", [DFF, D])
    n_mix = din("norm_mix", [D])
    n_ffn = din("norm_ffn", [D])
    n_fin = din("norm_final", [D])
    lam_in = [din(n, [64]) for n in ("lambda_q1", "lambda_k1", "lambda_q2", "lambda_k2")]
    subln_d = din("subln_gain", [128])
    sink_d = din("sink_logits", [8])
    cst_d = din("cst", [128, 288])
    wt_d = din("wtab", [128, 3584])
    bwt_d = din("bwtab", [128, 3072])
    ident_d = din("ident", [128, 128], BF16)
    out = nc.dram_tensor("out", [SEQ, D], F32, kind="ExternalOutput").ap()
    scr_gu = nc.dram_tensor("scr_gu", [11, 128, 8, 2, 256], BF16, kind="Internal").ap()
    scr_d = nc.dram_tensor("scr_d", [11, 128, 2, 1024], BF16, kind="Internal").ap()
    dbg_outs = {}

    P = Prog(nc)
    A = Arena(nc, 16512, 229344)

    PSF = nc.alloc_psum_tensor("psf", [128, 8 * 512], F32)
    PS3 = PSF.ap().rearrange("p (b n) -> p b n", n=512)

    def bank(i):
        return PS3[:, i, :]

    def bankb(i):
        return PSF.ap()[:, i * 512:(i + 1) * 512].bitcast(BF16)

    def PK(i):
        return ("ps", i)

    Zhn = A.alloc("Zhn", [128, 8, L], BF16)
    ident = A.alloc("ident", [128, 128], BF16)
    cst = A.alloc("cst", [128, 288], F32)
    gmix = A.alloc("gmix", [128, 8], F32)
    gffn = A.alloc("gffn", [128, 8], F32)
    gfin = A.alloc("gfin", [128, D], F32)
    subln = A.alloc("subln", [128, 128], F32)
    esink = A.alloc("esink", [128, 8], F32)
    lamt = A.alloc("lamt", [128, 4, 64], F32)
    lamj = A.alloc("lamj", [128, 64], F32)
    lams = A.alloc("lams", [128, 8], F32)
    mhalf = A.alloc("mhalf", [128, 1], F32)
    stt = A.alloc("stt", [128, 8, 4], F32)
    junk = A.alloc("junk", [128, D], BF16)
    zy_off = (A.mark() + 63) // 64 * 64
    Zy = A.alloc("Zy", [128, 8, SEQ], BF16)
    phase_base = A.mark()

    sp_dma = lambda fn, **kw: P.dma("sp", fn, **kw)

    P.dma("sp", lambda e: e.dma_start(out=ident[:], in_=ident_d), writes=["ident"], key="c_ident")
    P.dma("sp", lambda e: e.dma_start(out=cst[:], in_=cst_d), writes=["cst"], key="c_cst")
    P.dma("sp", lambda e: e.dma_start(out=gmix[:], in_=n_mix.rearrange("(c p) -> p c", p=128),
                                      allow_slow_non_contiguous=True), writes=["gmix"], key="c_gmix")
    P.dma("sp", lambda e: e.dma_start(out=gffn[:], in_=n_ffn.rearrange("(c p) -> p c", p=128),
                                      allow_slow_non_contiguous=True), writes=["gffn"], key="c_gffn")
    P.dma("sp", lambda e: e.dma_start(out=gfin[:], in_=n_fin.partition_broadcast(128)), writes=["gfin"], key="c_gfin")
    P.dma("sp", lambda e: e.dma_start(out=subln[:], in_=subln_d.partition_broadcast(128)), writes=["subln"], key="c_subln")
    P.dma("sp", lambda e: e.dma_start(out=esink[:], in_=sink_d.partition_broadcast(128)), writes=["esink"], key="c_sink")
    for i in range(4):
        P.dma("sp", lambda e, i=i: e.dma_start(out=lamt[:, i, :], in_=lam_in[i].partition_broadcast(128)),
              writes=[("lamt", i)], key=("c_lam", i))
    P.op("pool", lambda e: e.memset(mhalf[:], -0.5), writes=["mhalf"])
    P.op("dve", lambda e: e.tensor_scalar(out=subln[:], in0=subln[:], scalar1=1.0 - LAMBDA_INIT, scalar2=None, op0=ALU.mult),
         writes=["subln"])
    P.op("act", lambda e: e.activation(out=esink[:], in_=esink[:], func=AF.Exp), writes=["esink"])
    for i in range(2):
        P.op("dve", lambda e, i=i: e.tensor_tensor(out=lamj[:], in0=lamt[:, 2 * i, :], in1=lamt[:, 2 * i + 1, :], op=ALU.mult),
             reads=[("lamt", 2 * i), ("lamt", 2 * i + 1)], writes=["lamj"])
        P.op("dve", lambda e, i=i: e.tensor_reduce(out=lams[:, i:i + 1], in_=lamj[:], axis=mybir.AxisListType.X, op=ALU.add),
             reads=["lamj"], writes=[("lams", i)])
    P.op("act", lambda e: e.activation(out=lams[:, 2:4], in_=lams[:, 0:2], func=AF.Exp),
         reads=[("lams", 0), ("lams", 1)], writes=[("lams", 2)])
    P.op("dve", lambda e: e.tensor_tensor(out=lams[:, 4:5], in0=lams[:, 3:4], in1=lams[:, 2:3], op=ALU.subtract),
         reads=[("lams", 2)], writes=[("lams", 4)])
    P.op("dve", lambda e: e.tensor_scalar(out=lams[:, 4:5], in0=lams[:, 4:5], scalar1=-LAMBDA_INIT, scalar2=None, op0=ALU.add),
         writes=[("lams", 4)])
    neglam = lams[:, 4:5]

    scr_jobs = []
    scr_keys = []
    for fg in range(11):
        for t, wsrc in enumerate((w_g, w_u)):
            k = ("scr", "gu", fg, t)
            scr_keys.append(k)
            scr_jobs.append((k, lambda e, fg=fg, t=t, wsrc=wsrc: e.dma_start(
                out=scr_gu[fg, :, :, t, :],
                in_=wsrc[:, fg * 256:(fg + 1) * 256].rearrange("(c p) n -> p c n", p=128))))
    for fd in range(11):
        k = ("scr", "d", fd)
        scr_keys.append(k)
        scr_jobs.append((k, lambda e, fd=fd: e.dma_start(
            out=scr_d[fd], in_=w_d[fd * 256:(fd + 1) * 256, :].rearrange("(ff p) n -> p ff n", p=128))))

    def issue_scratch(n):
        for _ in range(n):
            if scr_jobs and stage >= 5:
                k, fn = scr_jobs.pop(0)
                P.dma("pool", fn, writes=[k], key="scr")

    def rms_rstd(src_ap, rows, slot, n, src_keys, on_dve=False):
        if on_dve is not False:
            sq = on_dve
            P.op("dve", lambda e: e.tensor_tensor(out=sq[:rows, 0:n], in0=src_ap, in1=src_ap, op=ALU.mult),
                 reads=src_keys, writes=["sqj"])
            P.op("dve", lambda e: e.tensor_reduce(out=stt[:rows, slot, 0:1], in_=sq[:rows, 0:n], axis=mybir.AxisListType.X, op=ALU.add),
                 reads=["sqj"], writes=[("stt", slot, 0)])
        else:
            P.op("act", lambda e: e.activation(out=junk[:rows, 0:n], in_=src_ap, func=AF.Square,
                                               accum_out=stt[:rows, slot, 0:1]),
                 reads=src_keys, writes=["junk", ("stt", slot, 0)])
        P.op("dve", lambda e: e.tensor_scalar(out=stt[:rows, slot, 1:2], in0=stt[:rows, slot, 0:1], scalar1=1.0 / n,
                                              scalar2=EPS, op0=ALU.mult, op1=ALU.add),
             reads=[("stt", slot, 0)], writes=[("stt", slot, 1)])
        P.op("pool", lambda e: e.tensor_tensor(out=stt[:rows, slot, 2:3], in0=stt[:rows, slot, 1:2], in1=mhalf[:rows, :],
                                               op=ALU.pow),
             reads=[("stt", slot, 1), "mhalf"], writes=[("stt", slot, 2)])
        return stt[:rows, slot, 2:3], ("stt", slot, 2)

    def blk_rows_col(kb):
        return (16, 0) if kb == 0 else (128, 16 + 128 * (kb - 1))

    m1 = A.mark()
    NXT, NXS, NPB = 4, 3, 4
    xt = [A.alloc("xt", [128, D], F32) for _ in range(NXT)]
    xs = [A.alloc("xs", [128, D], BF16) for _ in range(NXS)]
    p1 = {}

    def p1_a(kb):
        rows, col0 = blk_rows_col(kb)
        sl = kb % NXT
        src = meta if kb == 0 else x[(kb - 1) * 128:kb * 128, :]
        P.dma("sp", lambda e, sl=sl, rows=rows, src=src: e.dma_start(out=xt[sl][:rows, :], in_=src),
              writes=[("xt", sl)], key=("xt", sl))
        p1[kb] = rms_rstd(xt[sl][:rows, :], rows, sl, D, [("xt", sl)])

    def p1_b(kb):
        rows, col0 = blk_rows_col(kb)
        sl = kb % NXT
        ssl = kb % NXS
        rstd, rk = p1[kb]
        P.op("dve", lambda e, sl=sl, ssl=ssl, rows=rows, rstd=rstd: e.tensor_scalar(
            out=xs[ssl][:rows, :], in0=xt[sl][:rows, :], scalar1=rstd, scalar2=None, op0=ALU.mult),
            reads=[("xt", sl), rk], writes=[("xs", ssl)])
        pb = 4 + (kb % NPB)
        pv = bankb(pb).rearrange("p (c t) -> p c t", t=128)
        for c in range(8):
            P.op("pe", lambda e, c=c, ssl=ssl, rows=rows, pv=pv: e.transpose(
                out=pv[:, c, 0:rows], in_=xs[ssl][:rows, c * 128:(c + 1) * 128], identity=ident[:rows, :rows]),
                reads=[("xs", ssl), "ident"], writes=[PK(pb)])

    def p1_c(kb):
        rows, col0 = blk_rows_col(kb)
        pb = 4 + (kb % NPB)
        pv = bankb(pb).rearrange("p (c t) -> p c t", t=128)
        P.op("dve", lambda e, rows=rows, col0=col0, pv=pv: e.tensor_tensor(
            out=Zhn[:, :, col0:col0 + rows], in0=pv[:, :, 0:rows],
            in1=gmix[:, :].unsqueeze(2).broadcast_to([128, 8, rows]), op=ALU.mult),
            reads=["gmix"], writes=[PK(pb), ("Zhn", kb)])

    for i in range(33 + 2):
        if i < 33:
            p1_a(i)
        if 0 <= i - 1 < 33:
            p1_b(i - 1)
        if 0 <= i - 2 < 33:
            p1_c(i - 2)
    A.release(m1)
    if stage <= 1:
        return finish(nc, P, A, out, dbg, {"Zhn": (Zhn, [128, 8 * L], BF16)})

    P.barrier()

    m2 = A.mark()
    KT = A.alloc("KT", [128, L], BF16)
    Vaug = A.alloc("Vaug", [128, 33, 129], BF16)
    Wh = [A.alloc("Wh", [128, 8, 384], BF16) for _ in range(2)]
    QT = [A.alloc("QT", [128, 512], BF16) for _ in range(2)]
    Pt = [A.alloc("Pt", [128, 2, 512], BF16) for _ in range(3)]
    tmpf = [A.alloc("tmpf", [128, 2, 512], F32) for _ in range(2)]
    wtab = A.alloc("wtab", [128, 896], F32)
    acc = [A.alloc("acc", [128, 8, 129], F32) for _ in range(2)]
    yv4 = A.alloc("yv4", [128, 4, 128], F32)
    t4 = A.alloc("t4", [128, 4, 128], F32)
    ybf4 = A.alloc("ybf4", [128, 4, 128], BF16)
    rr8 = A.alloc("rr8", [128, 4, 2], F32)
    rn4 = A.alloc("rn4", [128, 4], F32)
    ss4 = A.alloc("ss4", [128, 3, 4], F32)
    mh4 = A.alloc("mh4", [128, 4], F32)
    P.op("pool", lambda e: e.memset(Vaug[:, :, 128:129], 1.0), writes=["Vones"])
    Wvall = A.alloc("Wvall", [128, 8, 512], BF16)
    Vall3 = nc.alloc_sbuf_tensor_at("Vall3", [128, 33, 3, 129], BF16, offset=zy_off + 4 * SEQ * 2)
    P.dma("pool", lambda e: e.dma_start(out=Wvall[:], in_=w_in[:, 1024:1536].rearrange("(c p) n -> p c n", p=128)),
          writes=["Wvall"], key="Wvall")
    P.op("pool", lambda e: e.memset(Vall3[:, :, :, 128:129], 1.0), writes=["Vones3"])
    for kb in range(33):
        rows, col0 = blk_rows_col(kb)
        pb = kb % 4
        for c in range(8):
            P.op("pe", lambda e, c=c, pb=pb, rows=rows, col0=col0: e.matmul(
                bank(pb)[:rows, :], lhsT=Zhn[:, c, col0:col0 + rows], rhs=Wvall[:, c, :], start=(c == 0), stop=(c == 7)),
                reads=["Wvall", ("Zhn", kb)], writes=[PK(pb)])
        if kb % 2 == 0:
            P.op("act", lambda e, pb=pb, rows=rows, kb=kb: e.activation(out=Vaug[:rows, kb, 0:128], in_=bank(pb)[:rows, 0:128], func=AF.Copy),
                 writes=[PK(pb), ("V0", kb)])
            P.op("dve", lambda e, pb=pb, rows=rows, kb=kb: e.tensor_copy(
                out=Vall3[:rows, kb, :, 0:128], in_=bank(pb)[:rows, 128:512].rearrange("p (h n) -> p h n", n=128)),
                writes=[PK(pb), ("V3", kb)])
        else:
            P.op("dve", lambda e, pb=pb, rows=rows, kb=kb: e.tensor_copy(out=Vaug[:rows, kb, 0:128], in_=bank(pb)[:rows, 0:128]),
                 writes=[PK(pb), ("V0", kb)])
            P.op("act", lambda e, pb=pb, rows=rows, kb=kb: e.activation(
                out=Vall3[:rows, kb, :, 0:128], in_=bank(pb)[:rows, 128:512].rearrange("p (h n) -> p h n", n=128), func=AF.Copy),
                writes=[PK(pb), ("V3", kb)])
    P.op("pool", lambda e: e.memset(mh4[:], -0.5), writes=["mh4"])
    all_zhn = [("Zhn", kb) for kb in range(33)]

    def load_wh(h):
        hs = h % 2
        for j, base in enumerate((0, 512)):
            P.dma("pool", lambda e, hs=hs, j=j, base=base, h=h: e.dma_start(
                out=Wh[hs][:, :, j * 128:(j + 1) * 128],
                in_=w_in[:, base + h * 128: base + (h + 1) * 128].rearrange("(c p) n -> p c n", p=128)),
                writes=[("Wh", hs, j)], key=("Wh", hs, j))

    load_wh(0)
    load_wh(1)
    pending_fin = []
    fill_q = []

    def flush_fin():
        while fill_q:
            fill_q.pop(0)()
        while pending_fin:
            T, hh = pending_fin.pop(0)
            for qb in range(4):
                P.op("pe", lambda e, qb=qb: e.transpose(out=bankb(7)[:, qb * 128:(qb + 1) * 128], in_=ybf4[:, qb, :], identity=ident[:, :]),
                     reads=["ybf4", "ident"], writes=[PK(7)])
            P.op("dve", lambda e, T=T, hh=hh: e.tensor_copy(out=Zy[:, hh, 512 * T:512 * T + 512], in_=bankb(7)[:, 0:512]),
                 writes=[PK(7)] + [("Zy", hh, 4 * T + qb) for qb in range(4)])

    fin_cnt = [0]
    pt_cnt = [0]
    st_cnt = [0]
    tf_cnt = [0]
    for h in range(4):
        hs = h % 2
        P.dma("sp", lambda e, h=h: e.dma_start(out=wtab[:, :], in_=wt_d[:, h * 896:(h + 1) * 896]), writes=["wtab"], key="c_wtab")
        for t in range(9):
            c0 = t * 512
            n = min(512, L - c0)
            pb = t % 4
            for c in range(8):
                P.op("pe", lambda e, c=c, pb=pb, n=n, c0=c0, hs=hs: e.matmul(
                    bank(pb)[:, 0:n], lhsT=Wh[hs][:, c, 128:256], rhs=Zhn[:, c, c0:c0 + n], start=(c == 0), stop=(c == 7)),
                    reads=[("Wh", hs, 1)] + all_zhn, writes=[PK(pb)])
            eng = "act" if t % 2 == 0 else "dve"
            if eng == "act":
                P.op("act", lambda e, pb=pb, n=n, c0=c0: e.activation(out=KT[:, c0:c0 + n], in_=bank(pb)[:, 0:n], func=AF.Copy),
                     writes=[PK(pb), ("KT", t)])
            else:
                P.op("dve", lambda e, pb=pb, n=n, c0=c0: e.tensor_copy(out=KT[:, c0:c0 + n], in_=bank(pb)[:, 0:n]),
                     writes=[PK(pb), ("KT", t)])
        flush_fin()
        all_kt = [("KT", t) for t in range(9)]
        all_v = ([("V0", kb_) for kb_ in range(33)] + ["Vones"]) if h == 0 else ([("V3", kb_) for kb_ in range(33)] + ["Vones3"])
        def emit_qproj(T, immediate=False):
            qs = T % 2
            qcol0 = 16 + 512 * T
            jobs = []
            for c in range(8):
                jobs.append(lambda c=c, hs=hs, qcol0=qcol0: P.op("pe", lambda e: e.matmul(
                    bank(7)[:, :], lhsT=Wh[hs][:, c, 0:128], rhs=Zhn[:, c, qcol0:qcol0 + 512], start=(c == 0), stop=(c == 7)),
                    reads=[("Wh", hs, 0)] + all_zhn, writes=[PK(7)]))
            jobs.append(lambda qs=qs: P.op("dve", lambda e: e.tensor_copy(out=QT[qs][:, :], in_=bank(7)[:, :]),
                                           writes=[PK(7), ("QT", qs)]))
            if immediate:
                for j in jobs:
                    j()
            else:
                fill_q.extend(jobs)

        def drain_fill(n=None):
            k = 0
            while fill_q and (n is None or k < n):
                fill_q.pop(0)()
                k += 1

        THR = 60.0
        slope = SLOPE_A[h]
        items = []
        for T in range(8):
            below = [j for j in range(0, 4 * T) if slope * (128.0 * (4 * T - j - 1) + 1.0) < THR]
            above = [j for j in range(4 * T + 4, 32) if slope * (128.0 * (j - 4 * T - 4) + 1.0) < THR]
            groups = [("below", below), ("diag", ["meta", 4 * T, 4 * T + 1, 4 * T + 2, 4 * T + 3]), ("above", above)]
            groups = [g for g in groups if g[1]]
            for gi, (gname, blocks) in enumerate(groups):
                for bi, blk in enumerate(blocks):
                    items.append(dict(T=T, gname=gname, blk=blk, bi=bi, nb=len(blocks), first_group=(gi == 0),
                                      last_group=(gi == len(groups) - 1), first_in_tile=(gi == 0 and bi == 0)))

        def emit_qk_exp(it):
            T, gname, blk = it["T"], it["gname"], it["blk"]
            qs = T % 2
            buf = st_cnt[0] % 2
            st_cnt[0] += 1
            pbi = pt_cnt[0] % 3
            pt_cnt[0] += 1
            it["pbi"] = pbi
            if blk == "meta":
                rows, kcol0, kb = 16, 0, 0
            else:
                rows, kcol0, kb = 128, 16 + 128 * blk, blk + 1
            it["rows"], it["kb"] = rows, kb
            for s in range(2):
                P.op("pe", lambda e, s=s, buf=buf, rows=rows, kcol0=kcol0, qs=qs: e.matmul(
                    PS3[:rows, buf * 2 + s, :], lhsT=KT[s * 64:(s + 1) * 64, kcol0:kcol0 + rows],
                    rhs=QT[qs][s * 64:(s + 1) * 64, :], start=True, stop=True),
                    reads=all_kt + [("QT", qs)], writes=[PK(buf * 2 + s)])
            stin = PS3[:rows, buf * 2:buf * 2 + 2, :]
            if gname == "diag" and blk != "meta":
                jl = blk - 4 * T
                off = 384 - 128 * jl
                tb = tf_cnt[0] % 2
                tf_cnt[0] += 1
                for s in range(2):
                    P.op("dve", lambda e, s=s, buf=buf, tb=tb, off=off, h=h: e.scalar_tensor_tensor(
                        out=tmpf[tb][:, s, :], in0=PS3[:, buf * 2 + s, :], scalar=0.125,
                        in1=wtab[:, off:off + 512], op0=ALU.mult, op1=ALU.add),
                        reads=["wtab"], writes=[PK(buf * 2 + s), ("tmpf", tb, s)])
                P.op("act", lambda e, tb=tb, pbi=pbi: e.activation(out=Pt[pbi][:, :, :], in_=tmpf[tb][:, :, :], func=AF.Exp),
                     reads=[("tmpf", tb, 0), ("tmpf", tb, 1)], writes=[("Pt", pbi)])
            elif blk == "meta":
                P.op("act", lambda e, pbi=pbi, stin=stin, rows=rows: e.activation(
                    out=Pt[pbi][:rows, :, :], in_=stin, func=AF.Exp, scale=0.125),
                    writes=[PK(buf * 2), PK(buf * 2 + 1), ("Pt", pbi)])
            else:
                if gname == "below":
                    col = h * 32 + (4 * T - blk)
                else:
                    col = 128 + h * 32 + (blk - 4 * T - 4)
                P.op("act", lambda e, pbi=pbi, stin=stin, col=col: e.activation(
                    out=Pt[pbi][:, :, :], in_=stin, func=AF.Exp, scale=0.125, bias=cst[:, col:col + 1]),
                    reads=["cst"], writes=[PK(buf * 2), PK(buf * 2 + 1), ("Pt", pbi)])

        def oreg(s, qb):
            ri = qb * 2 + s
            return 4 + ri // 3, (ri % 3) * 129, ri

        def emit_pv_post(it):
            T, gname, bi, nb = it["T"], it["gname"], it["bi"], it["nb"]
            pbi, rows, kb = it["pbi"], it["rows"], it["kb"]
            ab = T % 2
            for qb in range(4):
                for s in range(2):
                    ob, oo, ri = oreg(s, qb)
                    vrhs = Vaug[:rows, kb, 0:129] if h == 0 else Vall3[:rows, kb, h - 1, 0:129]
                    P.op("pe", lambda e, s=s, qb=qb, ob=ob, oo=oo, ri=ri, pbi=pbi, rows=rows, kb=kb, bi=bi, nb=nb, vrhs=vrhs: e.matmul(
                        PS3[:, ob, oo:oo + 129], lhsT=Pt[pbi][:rows, s, qb * 128:(qb + 1) * 128],
                        rhs=vrhs,
                        start=(bi == 0 and ri % 3 == 0), stop=(bi == nb - 1),
                        skip_group_check=True),
                        reads=all_v + [("Pt", pbi)], writes=[PK(ob)])
            if bi != nb - 1:
                return
            for qb in range(4):
                ob0, oo0, ri0 = oreg(0, qb)
                ob1, oo1, ri1 = oreg(1, qb)
                if ob0 == ob1:
                    pieces = [(ob0, oo0, ri0, 258, [("acc", ab, 0, qb), ("acc", ab, 1, qb)])]
                else:
                    pieces = [(ob0, oo0, ri0, 129, [("acc", ab, 0, qb)]), (ob1, oo1, ri1, 129, [("acc", ab, 1, qb)])]
                if gname == "below":
                    cap = cst[:, 256 + h * 4 + qb: 256 + h * 4 + qb + 1]
                elif gname == "above":
                    cap = cst[:, 272 + h * 4 + qb: 272 + h * 4 + qb + 1]
                else:
                    cap = None
                for (ob, oo, ri, w, akeys) in pieces:
                    src = PS3[:, ob, oo:oo + w]
                    dst = acc[ab][:].rearrange("p r c -> p (r c)")[:, ri * 129: ri * 129 + w]
                    if it["first_group"] and h < 2:
                        if cap is None:
                            P.op("act", lambda e, src=src, dst=dst: e.activation(out=dst, in_=src, func=AF.Copy),
                                 writes=[PK(ob)] + akeys)
                        else:
                            P.op("act", lambda e, src=src, dst=dst, cap=cap: e.activation(out=dst, in_=src, func=AF.Copy, scale=cap),
                                 reads=["cst"], writes=[PK(ob)] + akeys)
                    elif it["first_group"]:
                        if cap is None:
                            P.op("dve", lambda e, src=src, dst=dst: e.tensor_copy(out=dst, in_=src),
                                 writes=[PK(ob)] + akeys)
                        else:
                            P.op("dve", lambda e, src=src, dst=dst, cap=cap: e.tensor_scalar(
                                out=dst, in0=src, scalar1=cap, scalar2=None, op0=ALU.mult),
                                reads=["cst"], writes=[PK(ob)] + akeys)
                    else:
                        if cap is None:
                            P.op("dve", lambda e, src=src, dst=dst: e.tensor_tensor(out=dst, in0=src, in1=dst, op=ALU.add),
                                 writes=[PK(ob)] + akeys)
                        else:
                            P.op("dve", lambda e, src=src, dst=dst, cap=cap: e.scalar_tensor_tensor(
                                out=dst, in0=src, scalar=cap, in1=dst, op0=ALU.mult, op1=ALU.add),
                                reads=["cst"], writes=[PK(ob)] + akeys)
            if not it["last_group"]:
                return
            accv = acc[ab][:].rearrange("p (q s) c -> p q s c", s=2)
            akeys = [("acc", ab, s_, qb_) for s_ in range(2) for qb_ in range(4)]
            P.op("dve", lambda e, accv=accv: e.reciprocal(out=rr8[:, :, :], in_=accv[:, :, :, 128]),
                 reads=akeys, writes=["rr8"])
            P.op("pool", lambda e: e.tensor_scalar(out=rn4[:, :], in0=rr8[:, :, 1], scalar1=neglam, scalar2=None, op0=ALU.mult),
                 reads=["rr8", ("lams", 4)], writes=["rn4"])
            P.op("pool", lambda e, accv=accv: e.tensor_tensor(
                out=yv4[:, :, :], in0=accv[:, :, 0, 0:128], in1=rr8[:, :, 0].unsqueeze(2).broadcast_to([128, 4, 128]), op=ALU.mult),
                reads=akeys + ["rr8"], writes=["yv4"])
            P.op("pool", lambda e, accv=accv: e.tensor_tensor(
                out=t4[:, :, :], in0=accv[:, :, 1, 0:128], in1=rn4[:, :].unsqueeze(2).broadcast_to([128, 4, 128]), op=ALU.mult),
                reads=akeys + ["rn4"], writes=["t4"])
            P.op("pool", lambda e: e.tensor_tensor(out=yv4[:, :, :], in0=yv4[:, :, :], in1=t4[:, :, :], op=ALU.add),
                 reads=["t4"], writes=["yv4"])
            P.op("pool", lambda e: e.tensor_tensor(out=t4[:, :, :], in0=yv4[:, :, :], in1=yv4[:, :, :], op=ALU.mult),
                 reads=["yv4"], writes=["t4"])
            P.op("dve", lambda e: e.tensor_reduce(out=ss4[:, 0, :], in_=t4[:, :, :], axis=mybir.AxisListType.X, op=ALU.add),
                 reads=["t4"], writes=[("ss4", 0)])
            P.op("pool", lambda e: e.tensor_scalar(out=ss4[:, 1, :], in0=ss4[:, 0, :], scalar1=1.0 / 128, scalar2=EPS,
                                                   op0=ALU.mult, op1=ALU.add),
                 reads=[("ss4", 0)], writes=[("ss4", 1)])
            P.op("pool", lambda e: e.tensor_tensor(out=ss4[:, 2, :], in0=ss4[:, 1, :], in1=mh4[:, :], op=ALU.pow),
                 reads=[("ss4", 1), "mh4"], writes=[("ss4", 2)])
            P.op("pool", lambda e: e.tensor_tensor(
                out=yv4[:, :, :], in0=yv4[:, :, :], in1=ss4[:, 2, :].unsqueeze(2).broadcast_to([128, 4, 128]), op=ALU.mult),
                reads=[("ss4", 2)], writes=["yv4"])
            P.op("pool", lambda e: e.tensor_tensor(
                out=ybf4[:, :, :], in0=yv4[:, :, :], in1=subln[:, :].unsqueeze(1).broadcast_to([128, 4, 128]), op=ALU.mult),
                reads=["yv4", "subln"], writes=["ybf4"])
            pending_fin.append((T, h))
            issue_scratch(2)

        emit_qproj(0, immediate=True)
        prev = None
        since_fin = 0
        for it in items:
            if it["first_in_tile"]:
                drain_fill()
            emit_qk_exp(it)
            if it["first_in_tile"] and it["T"] + 1 < 8:
                emit_qproj(it["T"] + 1)
            drain_fill(2)
            if prev is not None:
                emit_pv_post(prev)
            prev = it
            if pending_fin:
                since_fin += 1
                if since_fin >= 14 or (it["last_group"] and it["bi"] >= it["nb"] - 2):
                    flush_fin()
                    since_fin = 0
        emit_pv_post(prev)
        drain_fill()
        if h + 2 < 4:
            load_wh(h + 2)
    flush_fin()
    issue_scratch(100)
    A.release(m2)
    if stage <= 2:
        return finish(nc, P, A, out, dbg, {"Zy": (Zy, [128, 8 * SEQ], BF16)})
    P.barrier()

    m3 = A.mark()
    KTb = A.alloc("KTb", [128, L], BF16)
    Vb = A.alloc("Vb", [128, 33, 2, 65], BF16)
    Wqb = A.alloc("Wqb", [128, 8, 512], BF16)
    Wkvb = A.alloc("Wkvb", [128, 8, 256], BF16)
    QTb = [A.alloc("QTb", [128, 4, 512], BF16) for _ in range(2)]
    bwt = A.alloc("bwt", [128, 2, 3, 512], F32)
    tmpb = [A.alloc("tmpb", [128, 512], F32) for _ in range(3)]
    Ptb = [A.alloc("Ptb", [128, 512], BF16) for _ in range(5)]
    d8 = [A.alloc("d8", [128, 8], F32) for _ in range(2)]
    ybb = [A.alloc("ybb", [128, 512], BF16) for _ in range(2)]
    osb = [A.alloc("osb", [128, 2, 260], F32) for _ in range(2)]
    P.dma("sp", lambda e: e.dma_start(out=bwt[:].rearrange("p g t n -> p (g t n)"), in_=bwt_d), writes=["bwt"], key="c_bwt")
    for g in range(2):
        for r in range(4):
            P.dma("pool", lambda e, g=g, r=r: e.dma_start(
                out=Wqb[:, :, r * 128 + g * 64:r * 128 + (g + 1) * 64],
                in_=w_in[:, 1536 + (g * 4 + r) * 64:1536 + (g * 4 + r + 1) * 64].rearrange("(c p) d -> p c d", p=128)),
                writes=[("Wqb", g, r)], key=("Wqb", g, r))
    P.dma("pool", lambda e: e.dma_start(out=Wkvb[:], in_=w_in[:, 2048:2304].rearrange("(c p) n -> p c n", p=128)),
          writes=["Wkvb"], key="Wkvb")
    P.op("pool", lambda e: e.memset(Vb[:, :, :, 64:65], 1.0), writes=["Vbones"])
    for t in range(9):
        c0 = t * 512
        n = min(512, L - c0)
        pb = t % 4
        for c in range(8):
            P.op("pe", lambda e, c=c, pb=pb, n=n, c0=c0: e.matmul(
                bank(pb)[:, 0:n], lhsT=Wkvb[:, c, 0:128], rhs=Zhn[:, c, c0:c0 + n], start=(c == 0), stop=(c == 7)),
                reads=["Wkvb"] + all_zhn, writes=[PK(pb)])
        P.op("dve", lambda e, pb=pb, n=n, c0=c0: e.tensor_copy(out=KTb[:, c0:c0 + n], in_=bank(pb)[:, 0:n]),
             writes=[PK(pb), ("KTb", t)])
    for c in range(8):
        P.op("pe", lambda e, c=c: e.matmul(bank(4)[:16, 0:128], lhsT=Zhn[:, c, 0:16], rhs=Wkvb[:, c, 128:256],
                                           start=(c == 0), stop=(c == 7)),
             reads=["Wkvb"] + all_zhn, writes=[PK(4)])
    P.op("dve", lambda e: e.tensor_copy(out=Vb[:16, 0, :, 0:64], in_=bank(4)[:16, 0:128].rearrange("p (g d) -> p g d", d=64)),
         writes=[PK(4), ("Vb", 0)])
    for q4 in range(8):
        pb = q4 % 4
        for i in range(4):
            kb = 1 + q4 * 4 + i
            col0 = 16 + 128 * (kb - 1)
            for c in range(8):
                P.op("pe", lambda e, c=c, pb=pb, i=i, col0=col0: e.matmul(
                    bank(pb)[:, i * 128:(i + 1) * 128], lhsT=Zhn[:, c, col0:col0 + 128], rhs=Wkvb[:, c, 128:256],
                    start=(c == 0), stop=(c == 7)),
                    reads=["Wkvb"] + all_zhn, writes=[PK(pb)])
        kb0 = 1 + q4 * 4
        for g in range(2):
            P.op("dve", lambda e, pb=pb, kb0=kb0, g=g: e.tensor_copy(
                out=Vb[:, kb0:kb0 + 4, g, 0:64],
                in_=bank(pb).rearrange("p (i g d) -> p i g d", g=2, d=64)[:, :, g, :]),
                writes=[PK(pb), ("Vb", 1 + q4, g)])
    all_ktb = [("KTb", t) for t in range(9)]
    all_vb = [("Vb", 0), "Vbones"] + [("Vb", 1 + q4, g) for q4 in range(8) for g in range(2)]
    stb_cnt = [0]
    ptb_cnt = [0]
    tb_cnt = [0]
    def emit_qbproj(t8, rs=(0, 1, 2, 3)):
        qs = t8 % 2
        qc0 = 16 + 512 * t8
        for r in rs:
            for c in range(8):
                P.op("pe", lambda e, c=c, r=r, qc0=qc0: e.matmul(
                    bank(6)[:, :], lhsT=Wqb[:, c, r * 128:(r + 1) * 128], rhs=Zhn[:, c, qc0:qc0 + 512],
                    start=(c == 0), stop=(c == 7)),
                    reads=[("Wqb", g_, r_) for g_ in range(2) for r_ in range(4)] + all_zhn, writes=[PK(6)])
            if r % 2 == 0:
                P.op("act", lambda e, r=r, qs=qs: e.activation(out=QTb[qs][:, r, :], in_=bank(6)[:, :], func=AF.Copy),
                     writes=[PK(6), ("QTb", qs, r)])
            else:
                P.op("dve", lambda e, r=r, qs=qs: e.tensor_copy(out=QTb[qs][:, r, :], in_=bank(6)[:, :]),
                     writes=[PK(6), ("QTb", qs, r)])

    bitems = []
    for i in range(32):
        types = [("meta", None)]
        if i - 1 >= 0:
            types.append((0, i - 1))
        types.append((1, i))
        if i + 1 <= 31:
            types.append((2, i + 1))
        for ti, (ty, j) in enumerate(types):
            for g in range(2):
                bitems.append(dict(i=i, ty=ty, j=j, ti=ti, nt=len(types), g=g))

    def emit_b_qk(it):
        i, ty, j, g = it["i"], it["ty"], it["j"], it["g"]
        t8, qi = i // 4, i % 4
        qs = t8 % 2
        qoff = qi * 128
        qkeys = [("QTb", qs, r) for r in range(4)]
        sb = stb_cnt[0] % 4
        stb_cnt[0] += 1
        pbi = ptb_cnt[0] % 5
        ptb_cnt[0] += 1
        if ty == "meta":
            rows, kcol0, kb = 16, 0, 0
        else:
            rows, kcol0, kb = 128, 16 + 128 * j, j + 1
        it["pbi"], it["rows"], it["kb"], it["sb"] = pbi, rows, kb, sb
        P.op("pe", lambda e, g=g, sb=sb, rows=rows, kcol0=kcol0, qs=qs, qoff=qoff: e.matmul(
            PS3[:rows, sb, :].rearrange("p (r q) -> p r q", q=128),
            lhsT=KTb[g * 64:(g + 1) * 64, kcol0:kcol0 + rows],
            rhs=QTb[qs][g * 64:(g + 1) * 64, :, qoff:qoff + 128], start=True, stop=True),
            reads=all_ktb + qkeys, writes=[PK(sb)])

    def emit_b_exp(it):
        ty, g, sb, pbi = it["ty"], it["g"], it["sb"], it["pbi"]
        if ty == "meta":
            P.op("act", lambda e, pbi=pbi, sb=sb: e.activation(
                out=Ptb[pbi][:16, :], in_=PS3[:16, sb, :], func=AF.Exp, scale=0.125),
                writes=[PK(sb), ("Ptb", pbi)])
        else:
            tb = tb_cnt[0] % 3
            tb_cnt[0] += 1
            P.op("dve", lambda e, g=g, sb=sb, tb=tb, ty=ty: e.scalar_tensor_tensor(
                out=tmpb[tb][:, :], in0=PS3[:, sb, :], scalar=0.125, in1=bwt[:, g, ty, :],
                op0=ALU.mult, op1=ALU.add),
                reads=["bwt"], writes=[PK(sb), ("tmpb", tb)])
            P.op("act", lambda e, tb=tb, pbi=pbi: e.activation(out=Ptb[pbi][:, :], in_=tmpb[tb][:, :], func=AF.Exp),
                 reads=[("tmpb", tb)], writes=[("Ptb", pbi)])

    def emit_b_pv_post(it):
        i, ti, nt, g = it["i"], it["ti"], it["nt"], it["g"]
        pbi, rows, kb = it["pbi"], it["rows"], it["kb"]
        fsl = i % 2
        for r in range(4):
            P.op("pe", lambda e, g=g, r=r, pbi=pbi, rows=rows, kb=kb, ti=ti, nt=nt: e.matmul(
                PS3[:, 4 + g, r * 65:(r + 1) * 65], lhsT=Ptb[pbi][:rows, r * 128:(r + 1) * 128],
                rhs=Vb[:rows, kb, g, 0:65], start=(ti == 0 and r == 0), stop=(ti == nt - 1),
                skip_group_check=True),
                reads=all_vb + [("Ptb", pbi)], writes=[PK(4 + g)])
        if ti != nt - 1:
            return
        P.op("dve", lambda e, g=g, fsl=fsl: e.tensor_copy(out=osb[fsl][:, g, :], in_=PS3[:, 4 + g, 0:260]),
             writes=[PK(4 + g), ("osb", fsl, g)])
        ov = osb[fsl][:, g, :].rearrange("p (r e) -> p r e", e=65)
        P.op("dve", lambda e, g=g, ov=ov, fsl=fsl: e.tensor_tensor(
            out=d8[fsl][:, g * 4:(g + 1) * 4], in0=ov[:, :, 64], in1=esink[:, g * 4:(g + 1) * 4], op=ALU.add),
            reads=["esink", ("osb", fsl, g)], writes=[("d8", fsl, g)])
        P.op("dve", lambda e, g=g, fsl=fsl: e.reciprocal(out=d8[fsl][:, g * 4:(g + 1) * 4], in_=d8[fsl][:, g * 4:(g + 1) * 4]),
             writes=[("d8", fsl, g)])
        P.op("dve", lambda e, g=g, ov=ov, fsl=fsl: e.tensor_tensor(
            out=ybb[fsl][:, g * 256:(g + 1) * 256].rearrange("p (r d) -> p r d", d=64), in0=ov[:, :, 0:64],
            in1=d8[fsl][:, g * 4:(g + 1) * 4].unsqueeze(2).broadcast_to([128, 4, 64]), op=ALU.mult),
            reads=[("d8", fsl, g), ("osb", fsl, g)], writes=[("ybb", fsl, g)])
        if g == 1:
            pending_b.append(i)

    pending_b = []

    def flush_b():
        tv = bankb(7).rearrange("p (c t) -> p c t", t=128)
        while pending_b:
            i = pending_b.pop(0)
            fsl = i % 2
            for cc in range(4):
                P.op("pe", lambda e, cc=cc, fsl=fsl, tv=tv: e.transpose(
                    out=tv[:, fsl * 4 + cc, :], in_=ybb[fsl][:, cc * 128:(cc + 1) * 128], identity=ident[:, :]),
                    reads=[("ybb", fsl, 0), ("ybb", fsl, 1), "ident"], writes=[PK(7)])
            P.op("dve", lambda e, fsl=fsl, tv=tv, i=i: e.tensor_copy(
                out=Zy[:, 4:8, i * 128:(i + 1) * 128], in_=tv[:, fsl * 4:fsl * 4 + 4, :]),
                writes=[PK(7), ("Zy", 4, i)])

    emit_qbproj(0)
    since_b = 0
    npairs = len(bitems) // 2
    for k in range(npairs):
        a, b = bitems[2 * k], bitems[2 * k + 1]
        emit_b_qk(a)
        emit_b_qk(b)
        emit_b_exp(a)
        emit_b_exp(b)
        if a["ti"] == 1 and a["i"] // 4 + 1 < 8:
            emit_qbproj(a["i"] // 4 + 1, rs=(a["i"] % 4,))
        if k >= 1:
            emit_b_pv_post(bitems[2 * k - 2])
            emit_b_pv_post(bitems[2 * k - 1])
        if pending_b:
            since_b += 1
            if since_b >= 3:
                flush_b()
                since_b = 0
    emit_b_pv_post(bitems[-2])
    emit_b_pv_post(bitems[-1])
    flush_b()
    A.release(m3)
    if stage <= 3:
        return finish(nc, P, A, out, dbg, {"Zy": (Zy, [128, 8 * SEQ], BF16)})
    P.barrier()

    m4 = A.mark()
    Wga = A.alloc("Wga", [128, 8, D], BF16)
    Wgb = A.alloc("Wgb", [128, 8, D], BF16)
    Wba = A.alloc("Wba", [128, 4, D], BF16)
    Wbb = A.alloc("Wbb", [128, 4, D], BF16)
    mT = A.alloc("mT", [128, 8, 512], BF16)
    sg = [A.alloc("sg", [128, 2, 512], F32) for _ in range(2)]
    P.dma("pool", lambda e: e.dma_start(out=Wga[:], in_=w_in[:, 2304:3328].rearrange("(c p) n -> p c n", p=128)), writes=["Wga"], key="Wga")
    P.dma("pool", lambda e: e.dma_start(out=Wgb[:], in_=w_in[:, 3328:4352].rearrange("(c p) n -> p c n", p=128)), writes=["Wgb"], key="Wgb")
    P.dma("pool", lambda e: e.dma_start(out=Wba[:], in_=w_ba.rearrange("(c p) n -> p c n", p=128)), writes=["Wba"], key="Wba")
    P.dma("pool", lambda e: e.dma_start(out=Wbb[:], in_=w_bb.rearrange("(c p) n -> p c n", p=128)), writes=["Wbb"], key="Wbb")
    for t8 in range(8):
        zc0 = 16 + 512 * t8
        yc0 = 512 * t8
        zkeys = [("Zhn", 1 + 4 * t8 + i) for i in range(4)]
        ykeys = [("Zy", hh, 4 * t8 + i) for hh in range(5) for i in range(4)]
        for f in range(8):
            b0 = (f % 2) * 4
            sl = f % 2
            for (w, bnk, nchunk, src, coff, wk, rk) in ((Wga, b0, 8, Zhn, zc0, "Wga", zkeys), (Wba, b0 + 1, 4, Zy, yc0, "Wba", ykeys),
                                                         (Wgb, b0 + 2, 8, Zhn, zc0, "Wgb", zkeys), (Wbb, b0 + 3, 4, Zy, yc0, "Wbb", ykeys)):
                for c in range(nchunk):
                    cs = c if w is not Wbb else 4 + c
                    P.op("pe", lambda e, w=w, bnk=bnk, c=c, cs=cs, nchunk=nchunk, src=src, coff=coff, f=f: e.matmul(
                        bank(bnk)[:, :], lhsT=w[:, c, f * 128:(f + 1) * 128], rhs=src[:, cs, coff:coff + 512],
                        start=(c == 0), stop=(c == nchunk - 1)),
                        reads=[wk] + rk, writes=[PK(bnk)])
            P.op("act", lambda e, b0=b0, sl=sl: e.activation(out=sg[sl][:, 0, :], in_=bank(b0)[:, :], func=AF.Sigmoid),
                 writes=[PK(b0), ("sg", sl, 0)])
            P.op("act", lambda e, b0=b0, sl=sl: e.activation(out=sg[sl][:, 1, :], in_=bank(b0 + 2)[:, :], func=AF.Sigmoid),
                 writes=[PK(b0 + 2), ("sg", sl, 1)])
            P.op("dve", lambda e, b0=b0, sl=sl: e.tensor_tensor(out=sg[sl][:, 0, :], in0=bank(b0 + 1)[:, :], in1=sg[sl][:, 0, :], op=ALU.mult),
                 writes=[PK(b0 + 1), ("sg", sl, 0)])
            P.op("dve", lambda e, b0=b0, sl=sl: e.tensor_tensor(out=sg[sl][:, 1, :], in0=bank(b0 + 3)[:, :], in1=sg[sl][:, 1, :], op=ALU.mult),
                 writes=[PK(b0 + 3), ("sg", sl, 1)])
            P.op("pool", lambda e, sl=sl, f=f: e.tensor_tensor(out=mT[:, f, :], in0=sg[sl][:, 0, :], in1=sg[sl][:, 1, :], op=ALU.add),
                 reads=[("sg", sl, 0), ("sg", sl, 1)], writes=[("mT", f)])
        P.op("pool", lambda e, zc0=zc0: e.tensor_copy(out=Zhn[:, :, zc0:zc0 + 512], in_=mT[:, :, :]),
             reads=[("mT", f) for f in range(8)], writes=zkeys)
    A.release(m4)
    if stage <= 4:
        return finish(nc, P, A, out, dbg, {"Zhn": (Zhn, [128, 8 * L], BF16)})
    P.barrier()

    A.release(zy_off)
    Wo = A.alloc("Wo", [128, 8, D], BF16)
    h1b = [A.alloc("h1", [128, 4, D], F32) for _ in range(2)]
    xin = [A.alloc("xin", [128, D], F32) for _ in range(2)]
    hsb = [A.alloc("hsb", [128, D], BF16) for _ in range(2)]
    hn1T = A.alloc("hn1T", [128, 8, 512], BF16)
    hidT = A.alloc("hidT", [128, 22, 512], BF16)
    outt = [A.alloc("outt", [128, D], F32) for _ in range(2)]
    sgf = [A.alloc("sgf", [128, 512], F32) for _ in range(2)]
    NGU, NWD = 3, 2
    Wgu = [A.alloc("Wgu", [128, 8, 2, 256], BF16) for _ in range(NGU)]
    Wd = [A.alloc("Wd", [128, 2, 1024], BF16) for _ in range(NWD)]
    P.dma("pool", lambda e: e.dma_start(out=Wo[:], in_=w_out.rearrange("(c p) n -> p c n", p=128)), writes=["Wo"], key="Wo")
    gu_cnt = [0]
    wd_cnt = [0]
    xi_cnt = [0]
    ot_cnt = [0]

    def load_gu(fg):
        sl = gu_cnt[0] % NGU
        gu_cnt[0] += 1
        P.dma("sp", lambda e, sl=sl, fg=fg: e.dma_start(out=Wgu[sl][:], in_=scr_gu[fg]), reads=scr_keys,
              writes=[("Wgu", sl)], key=("Wgu", sl))
        return sl

    def load_wd(fd):
        sl = wd_cnt[0] % NWD
        wd_cnt[0] += 1
        P.dma("sp", lambda e, sl=sl, fd=fd: e.dma_start(out=Wd[sl][:], in_=scr_d[fd]), reads=scr_keys,
              writes=[("Wd", sl)], key=("Wd", sl))
        return sl

    def stage_w(t8, blk):
        hb = t8 % 2
        h1 = h1b[hb]
        tok0 = 512 * t8 + 128 * blk
        zcol = 16 + tok0
        xsl = xi_cnt[0] % 2
        xi_cnt[0] += 1
        P.dma("sp", lambda e, xsl=xsl, tok0=tok0: e.dma_start(out=xin[xsl][:], in_=x[tok0:tok0 + 128, :]),
              writes=[("xin", xsl)], key=("xin", xsl))
        zk = [("Zhn", 1 + tok0 // 128)]
        b0 = (blk % 2) * 2
        for half in range(2):
            for f in range(8):
                P.op("pe", lambda e, half=half, f=f, zcol=zcol, b0=b0: e.matmul(
                    bank(b0 + half)[:, :], lhsT=Zhn[:, f, zcol:zcol + 128], rhs=Wo[:, f, half * 512:(half + 1) * 512],
                    start=(f == 0), stop=(f == 7)),
                    reads=["Wo"] + zk, writes=[PK(b0 + half)])
            P.op("dve", lambda e, half=half, blk=blk, xsl=xsl, b0=b0, h1=h1: e.tensor_tensor(
                out=h1[:, blk, half * 512:(half + 1) * 512], in0=bank(b0 + half)[:, :], in1=xin[xsl][:, half * 512:(half + 1) * 512],
                op=ALU.add),
                reads=[("xin", xsl)], writes=[PK(b0 + half), ("h1", hb, blk, half)])
        rstd, rk = rms_rstd(h1[:, blk, :], 128, blk, D, [("h1", hb, blk, 0), ("h1", hb, blk, 1)])
        hsl = blk % 2
        P.op("dve", lambda e, hsl=hsl, blk=blk, rstd=rstd, h1=h1: e.tensor_scalar(
            out=hsb[hsl][:, :], in0=h1[:, blk, :], scalar1=rstd, scalar2=None, op0=ALU.mult),
            reads=[("h1", hb, blk, 0), ("h1", hb, blk, 1), rk], writes=[("hsb", hsl)])

    def stage_t(t8, blk):
        hsl = blk % 2
        pb = 4 + (blk % 2)
        pv = bankb(pb).rearrange("p (c t) -> p c t", t=128)
        for c in range(8):
            P.op("pe", lambda e, c=c, hsl=hsl, pv=pv: e.transpose(
                out=pv[:, c, :], in_=hsb[hsl][:, c * 128:(c + 1) * 128], identity=ident[:, :]),
                reads=[("hsb", hsl), "ident"], writes=[PK(pb)])
        P.op("dve", lambda e, blk=blk, pv=pv: e.tensor_tensor(
            out=hn1T[:, :, blk * 128:(blk + 1) * 128], in0=pv[:, :, :],
            in1=gffn[:, :].unsqueeze(2).broadcast_to([128, 8, 128]), op=ALU.mult),
            reads=["gffn"], writes=[PK(pb), ("hn1T", blk)])

    def stage_a(t8):
        stage_w(t8, 0)
        stage_w(t8, 1)
        stage_t(t8, 0)
        stage_w(t8, 2)
        stage_t(t8, 1)
        stage_w(t8, 3)
        stage_t(t8, 2)
        stage_t(t8, 3)

    def final_blk(t8, blk):
        hb = t8 % 2
        h1 = h1b[hb]
        tok0 = 512 * t8 + 128 * blk
        rstd, rk = rms_rstd(h1[:, blk, :], 128, 4 + blk, D, [("h1", hb, blk, 0), ("h1", hb, blk, 1)])
        osl = ot_cnt[0] % 2
        ot_cnt[0] += 1
        P.op("dve", lambda e, osl=osl, blk=blk, rstd=rstd, h1=h1: e.scalar_tensor_tensor(
            out=outt[osl][:, :], in0=h1[:, blk, :], scalar=rstd, in1=gfin[:, :], op0=ALU.mult, op1=ALU.mult),
            reads=[("h1", hb, blk, 0), ("h1", hb, blk, 1), rk, "gfin"], writes=[("outt", osl)])
        P.dma("sp", lambda e, osl=osl, tok0=tok0: e.dma_start(out=out[tok0:tok0 + 128, :], in_=outt[osl][:, :]),
              reads=[("outt", osl)], writes=[("outd", tok0)], key=("outt", osl))

    hkeys = [("hn1T", b) for b in range(4)]
    deferred = []
    stage_a(0)
    for t8 in range(8):
        hb = t8 % 2
        h1 = h1b[hb]
        pending = []
        for fg in range(min(NGU - 1, 11)):
            pending.append(load_gu(fg))
        nxt = len(pending)
        for fg in range(11):
            if nxt < 11:
                pending.append(load_gu(nxt))
                nxt += 1
            sl = pending.pop(0)
            for fi in range(2):
                f = fg * 2 + fi
                bg = 4 + (f % 2) * 2
                for tsel in range(2):
                    for c in range(8):
                        P.op("pe", lambda e, c=c, sl=sl, tsel=tsel, fi=fi, bg=bg: e.matmul(
                            bank(bg + tsel)[:, :], lhsT=Wgu[sl][:, c, tsel, fi * 128:(fi + 1) * 128], rhs=hn1T[:, c, :],
                            start=(c == 0), stop=(c == 7)),
                            reads=[("Wgu", sl)] + hkeys, writes=[PK(bg + tsel)])
                ssl = f % 2
                P.op("act", lambda e, bg=bg, ssl=ssl: e.activation(out=sgf[ssl][:, :], in_=bank(bg)[:, :], func=AF.Silu),
                     writes=[PK(bg), ("sgf", ssl)])
                P.op("dve", lambda e, bg=bg, ssl=ssl, f=f: e.tensor_tensor(
                    out=hidT[:, f, :], in0=bank(bg + 1)[:, :], in1=sgf[ssl][:, :], op=ALU.mult),
                    reads=[("sgf", ssl)], writes=[PK(bg + 1), ("hidT", f)])
                if deferred and f in (1, 5, 9, 13):
                    deferred.pop(0)()
        while deferred:
            deferred.pop(0)()
        pending = []
        for fd in range(min(NWD - 1, 11)):
            pending.append(load_wd(fd))
        nxt = len(pending)
        for fd in range(11):
            if nxt < 11:
                pending.append(load_wd(nxt))
                nxt += 1
            sl = pending.pop(0)
            for ff in range(2):
                f = fd * 2 + ff
                for blk in range(4):
                    for half in range(2):
                        P.op("pe", lambda e, f=f, ff=ff, sl=sl, blk=blk, half=half: e.matmul(
                            bank(blk * 2 + half)[:, :], lhsT=hidT[:, f, blk * 128:(blk + 1) * 128],
                            rhs=Wd[sl][:, ff, half * 512:(half + 1) * 512], start=(f == 0), stop=(f == 21)),
                            reads=[("Wd", sl), ("hidT", f)], writes=[PK(blk * 2 + half)])
        for blk in range(4):
            for half in range(2):
                P.op("dve", lambda e, blk=blk, half=half, h1=h1: e.tensor_tensor(
                    out=h1[:, blk, half * 512:(half + 1) * 512], in0=bank(blk * 2 + half)[:, :],
                    in1=h1[:, blk, half * 512:(half + 1) * 512], op=ALU.add),
                    writes=[PK(blk * 2 + half), ("h1", hb, blk, half)])
        if t8 + 1 < 8:
            stage_a(t8 + 1)
            for blk in range(4):
                deferred.append(lambda t8=t8, blk=blk: final_blk(t8, blk))
        else:
            for blk in range(4):
                final_blk(t8, blk)
    return finish(nc, P, A, out, dbg, {})


def finish(nc, P, A, out, dbg, dumps):
    fw = [k for k in P.dma_counts if isinstance(k, tuple) and k[0] == "outt"]
    if dbg:
        for name, (t, shape, dt) in dumps.items():
            d = nc.dram_tensor("dbg_" + name, list(shape), dt, kind="ExternalOutput").ap()
            P.barrier()
            flat = t[:].rearrange("p a b -> p (a b)") if len(t.shape) == 3 else t[:]
            P.dma("sp", lambda e, d=d, flat=flat: e.dma_start(out=d, in_=flat), key=("dbg", name))
            fw.append(("dbg", name))
    P.emit(final_waits=fw)
    return nc


_CACHE = {}


def kernel(x, meta_tokens, norm_mix, w_in, lambda_q1, lambda_k1, lambda_q2, lambda_k2, subln_gain, sink_logits,
           w_branch_a, w_branch_b, w_out, norm_ffn, w_ff_gate, w_ff_up, w_ff_down, norm_final, _stage=99, _dbg=False):
    f32 = lambda a: np.ascontiguousarray(np.asarray(a, dtype=np.float32))
    cst, wt, bw, ident = make_consts()
    nc = build_nc(_stage, _dbg)
    shared = {
        "meta_tokens": f32(meta_tokens), "w_in": f32(w_in)[0], "w_branch_a": f32(w_branch_a)[0],
        "w_branch_b": f32(w_branch_b)[0], "w_out": f32(w_out)[0], "w_ff_gate": f32(w_ff_gate)[0],
        "w_ff_up": f32(w_ff_up)[0], "w_ff_down": f32(w_ff_down)[0], "norm_mix": f32(norm_mix)[0],
        "norm_ffn": f32(norm_ffn)[0], "norm_final": f32(norm_final), "lambda_q1": f32(lambda_q1)[0],
        "lambda_k1": f32(lambda_k1)[0], "lambda_q2": f32(lambda_q2)[0], "lambda_k2": f32(lambda_k2)[0],
        "subln_gain": f32(subln_gain)[0], "sink_logits": f32(sink_logits)[0],
        "cst": cst, "wtab": wt, "bwtab": bw, "ident": ident,
    }
    xx = f32(x)
    ncores = xx.shape[0] if not _dbg else 1
    in_maps = [dict(shared, x=xx[b]) for b in range(ncores)]
    res = run_bass_kernel_spmd(nc, in_maps, core_ids=list(range(ncores)))
    if _dbg:
        return res.results
    return np.stack([np.asarray(r["out"], dtype=np.float32) for r in res.results], axis=0)
```

```python
import os
import numpy as np
import ml_dtypes
import concourse.bass as bass
import concourse.mybir as mybir
from concourse.bass_utils import run_bass_kernel_spmd

F32 = mybir.dt.float32
BF16 = mybir.dt.bfloat16
AF = mybir.ActivationFunctionType
ALU = mybir.AluOpType

SEM_CAP = 16000
D = 1024
SEQ = 4096
L = 4112
NMETA = 16
DFF = 2816
PROJ_W = 4352
EPS = 1e-6
LAMBDA_INIT = 0.8 - 0.6 * 1.0
SLOPE_A = [2.0 ** (-2.0 * (i + 1)) for i in range(4)]
SLOPE_B = [2.0 ** (-1.0 * (i + 1)) for i in range(8)]
NEG = -30000.0


class Ins:
    __slots__ = ("eng", "fn", "deps", "signal", "count", "is_dma", "dkey", "didx", "seq")

    def __init__(self, eng, fn, is_dma=False, dkey=None):
        self.eng = eng
        self.fn = fn
        self.deps = []
        self.signal = False
        self.count = 0
        self.is_dma = is_dma
        self.dkey = dkey
        self.didx = 0


class Prog:
    ENGS = ("pe", "act", "dve", "pool", "sp")

    def __init__(self, nc):
        self.nc = nc
        self.lists = {e: [] for e in self.ENGS}
        self.bw = {}
        self.br = {}
        self.dma_counts = {}
        self.pending = {e: [] for e in self.ENGS}
        self.dmas_since = []

    def _add(self, ins, reads, writes):
        deps = []
        if self.pending[ins.eng]:
            deps.extend(self.pending[ins.eng])
            self.pending[ins.eng] = []
        for k in reads:
            deps.extend(self.bw.get(k, ()))
        for k in writes:
            deps.extend(self.bw.get(k, ()))
            deps.extend(self.br.get(k, ()))
        best = {}
        for d in deps:
            if d is ins:
                continue
            if d.is_dma:
                k = ("d", d.dkey)
                if k not in best or best[k].didx < d.didx:
                    best[k] = d
            else:
                if d.eng == "pe" and ins.eng == "pe" and not ins.is_dma:
                    continue
                k = ("e", d.eng)
                if k not in best or best[k].seq < d.seq:
                    best[k] = d
        for d in best.values():
            ins.deps.append(d)
            d.signal = True
        ins.seq = len(self.lists[ins.eng])
        for k in reads:
            self.br.setdefault(k, []).append(ins)
        for k in writes:
            self.bw[k] = [ins]
            self.br[k] = []
        self.lists[ins.eng].append(ins)
        return ins

    def op(self, eng, fn, reads=(), writes=()):
        return self._add(Ins(eng, fn), reads, writes)

    def dma(self, eng, fn, reads=(), writes=(), key=None):
        ins = Ins(eng, fn, is_dma=True, dkey=key)
        n = self.dma_counts.get(key, 0) + 1
        self.dma_counts[key] = n
        ins.didx = n
        ins.signal = True
        self.dmas_since.append(ins)
        return self._add(ins, reads, writes)

    def barrier(self):
        lasts = []
        for e in self.ENGS:
            for i in reversed(self.lists[e]):
                if not i.is_dma:
                    lasts.append(i)
                    break
        lasts += self.dmas_since
        self.dmas_since = []
        for d in lasts:
            d.signal = True
        for e in self.ENGS:
            self.pending[e] = self.pending[e] + list(lasts)

    def emit(self, final_waits=()):
        nc = self.nc
        nsig = {}
        for e in self.ENGS:
            c = 0
            for ins in self.lists[e]:
                if ins.is_dma:
                    continue
                if ins.signal:
                    c += 1
                    ins.count = c
            nsig[e] = c
        esems = {}
        for e in self.ENGS:
            nep = (nsig[e] + SEM_CAP - 1) // SEM_CAP
            esems[e] = [nc.alloc_semaphore(f"s_{e}_{i}") for i in range(max(nep, 1))]
        dsems = {k: nc.alloc_semaphore(f"d_{i}") for i, k in enumerate(self.dma_counts)}
        lists = self.lists
        dma_counts = self.dma_counts

        def run(e, eng):
            waited = {}
            for ins in lists[e]:
                need = {}
                for d in ins.deps:
                    if d.is_dma:
                        sem = ("d", d.dkey)
                        val = (0, 16 * d.didx)
                    else:
                        sem = ("e", d.eng)
                        n = d.count
                        val = ((n - 1) // SEM_CAP, (n - 1) % SEM_CAP + 1)
                    if need.get(sem, (-1, -1)) < val:
                        need[sem] = val
                for sem, val in need.items():
                    if waited.get(sem, (-1, -1)) >= val:
                        continue
                    waited[sem] = val
                    if sem[0] == "d":
                        eng.wait_ge(dsems[sem[1]], val[1])
                    else:
                        eng.wait_ge(esems[sem[1]][val[0]], val[1])
                r = ins.fn(eng)
                if ins.is_dma:
                    r.then_inc(dsems[ins.dkey], 16)
                elif ins.signal:
                    n = ins.count
                    r.then_inc(esems[e][(n - 1) // SEM_CAP], 1)
            if e == "sp":
                for k in final_waits:
                    eng.wait_ge(dsems[k], 16 * dma_counts[k])

        with nc.Block() as block:
            @block.tensor
            def _(eng):
                run("pe", eng)

            @block.scalar
            def _(eng):
                run("act", eng)

            @block.vector
            def _(eng):
                run("dve", eng)

            @block.gpsimd
            def _(eng):
                run("pool", eng)

            @block.sync
            def _(eng):
                run("sp", eng)


class Arena:
    def __init__(self, nc, base, limit):
        self.nc = nc
        self.off = base
        self.limit = limit
        self.n = 0

    def alloc(self, name, shape, dt):
        nbytes = int(np.prod(shape[1:])) * (2 if dt == BF16 else 4)
        o = (self.off + 63) // 64 * 64
        assert o + nbytes <= self.limit, (name, o, nbytes, self.limit)
        self.off = o + nbytes
        self.n += 1
        return self.nc.alloc_sbuf_tensor_at(f"{name}_{self.n}", list(shape), dt, offset=o)

    def mark(self):
        return self.off

    def release(self, m):
        self.off = m


def make_consts():
    kl = np.arange(128, dtype=np.float64)[:, None]
    cst = np.zeros((128, 288), np.float64)
    for h in range(4):
        s = SLOPE_A[h]
        for d in range(32):
            cst[:, h * 32 + d] = s * (kl[:, 0] - 128.0 * d)
            cst[:, 128 + h * 32 + d] = -s * (128.0 * d + kl[:, 0] + 1.0)
        for qb in range(4):
            cst[:, 256 + h * 4 + qb] = np.exp(-s * (128.0 * qb + kl[:, 0]))
            cst[:, 272 + h * 4 + qb] = np.exp(-s * (511.0 - 128.0 * qb - kl[:, 0]))
    wt = np.zeros((128, 4, 896), np.float64)
    c = np.arange(896, dtype=np.float64)[None, :]
    for h in range(4):
        wt[:, h, :] = -SLOPE_A[h] * np.abs(c - 384.0 - kl)
    bw = np.zeros((128, 2, 3, 4, 128), np.float64)
    ql = np.arange(128, dtype=np.float64)[None, :]
    for g in range(2):
        for r in range(4):
            s = SLOPE_B[g * 4 + r]
            d0 = 128.0 + ql - kl
            bw[:, g, 0, r, :] = np.where(d0 <= 128.0, -s * d0, NEG)
            bw[:, g, 1, r, :] = -s * np.abs(ql - kl)
            d2 = 128.0 + kl - ql
            bw[:, g, 2, r, :] = np.where(d2 <= 128.0, -s * d2, NEG)
    return (cst.astype(np.float32), wt.reshape(128, 3584).astype(np.float32),
            bw.reshape(128, 3072).astype(np.float32), np.eye(128).astype(ml_dtypes.bfloat16))


def build_nc(stage=99, dbg=False):
    nc = bass.Bass("TRN2", target_bir_lowering=False)

    def din(name, shape, dt=F32):
        return nc.dram_tensor(name, list(shape), dt, kind="ExternalInput").ap()

    x = din("x", [SEQ, D])
    meta = din("meta_tokens", [NMETA, D])
    w_in = din("w_in", [D, PROJ_W])
    w_ba = din("w_branch_a", [512, D])
    w_bb = din("w_branch_b", [512, D])
    w_out = din("w_out", [D, D])
    w_g = din("w_ff_gate", [D, DFF])
    w_u = din("w_ff_up", [D, DFF])
    w_d = din("w_ff_down", [DFF, D])
    n_mix = din("norm_mix", [D])
    n_ffn = din("norm_ffn", [D])
    n_fin = din("norm_final", [D])
    lam_in = [din(n, [64]) for n in ("lambda_q1", "lambda_k1", "lambda_q2", "lambda_k2")]
    subln_d = din("subln_gain", [128])
    sink_d = din("sink_logits", [8])
    cst_d = din("cst", [128, 288])
    wt_d = din("wtab", [128, 3584])
    bwt_d = din("bwtab", [128, 3072])
    ident_d = din("ident", [128, 128], BF16)
    out = nc.dram_tensor("out", [SEQ, D], F32, kind="ExternalOutput").ap()
    scr_gu = nc.dram_tensor("scr_gu", [11, 128, 8, 2, 256], BF16, kind="Internal").ap()
    scr_d = nc.dram_tensor("scr_d", [11, 128, 2, 1024], BF16, kind="Internal").ap()
    scr_q = nc.dram_tensor("scr_q", [128, 8, 512], BF16, kind="Internal").ap()
    scr_kv = nc.dram_tensor("scr_kv", [128, 8, 256], BF16, kind="Internal").ap()
    scr_ga = nc.dram_tensor("scr_ga", [128, 8, 1024], BF16, kind="Internal").ap()
    scr_gb = nc.dram_tensor("scr_gb", [128, 8, 1024], BF16, kind="Internal").ap()
    scr_ba = nc.dram_tensor("scr_ba", [128, 4, 1024], BF16, kind="Internal").ap()
    scr_bb = nc.dram_tensor("scr_bb", [128, 4, 1024], BF16, kind="Internal").ap()
    scr_o = nc.dram_tensor("scr_o", [128, 8, 1024], BF16, kind="Internal").ap()
    dbg_outs = {}

    P = Prog(nc)
    A = Arena(nc, 16512, 229344)

    PSF = nc.alloc_psum_tensor("psf", [128, 8 * 512], F32)
    PS3 = PSF.ap().rearrange("p (b n) -> p b n", n=512)

    def bank(i):
        return PS3[:, i, :]

    def bankb(i):
        return PSF.ap()[:, i * 512:(i + 1) * 512].bitcast(BF16)

    def PK(i):
        return ("ps", i)

    Zhn = A.alloc("Zhn", [128, 8, L], BF16)
    ident = A.alloc("ident", [128, 128], BF16)
    cst = A.alloc("cst", [128, 288], F32)
    gmix = A.alloc("gmix", [128, 8], F32)
    gffn = A.alloc("gffn", [128, 8], F32)
    gfin = A.alloc("gfin", [128, D], F32)
    subln = A.alloc("subln", [128, 128], F32)
    esink = A.alloc("esink", [128, 8], F32)
    lamt = A.alloc("lamt", [128, 4, 64], F32)
    lamj = A.alloc("lamj", [128, 64], F32)
    lams = A.alloc("lams", [128, 8], F32)
    mhalf = A.alloc("mhalf", [128, 1], F32)
    stt = A.alloc("stt", [128, 8, 4], F32)
    junk = A.alloc("junk", [128, D], BF16)
    zy_off = (A.mark() + 63) // 64 * 64
    Zy = A.alloc("Zy", [128, 8, SEQ], BF16)
    phase_base = A.mark()

    sp_dma = lambda fn, **kw: P.dma("sp", fn, **kw)

    P.dma("sp", lambda e: e.dma_start(out=ident[:], in_=ident_d), writes=["ident"], key="c_ident")
    P.dma("sp", lambda e: e.dma_start(out=cst[:], in_=cst_d), writes=["cst"], key="c_cst")
    P.dma("sp", lambda e: e.dma_start(out=gmix[:], in_=n_mix.rearrange("(c p) -> p c", p=128),
                                      allow_slow_non_contiguous=True), writes=["gmix"], key="c_gmix")
    P.dma("sp", lambda e: e.dma_start(out=gffn[:], in_=n_ffn.rearrange("(c p) -> p c", p=128),
                                      allow_slow_non_contiguous=True), writes=["gffn"], key="c_gffn")
    P.dma("sp", lambda e: e.dma_start(out=gfin[:], in_=n_fin.partition_broadcast(128)), writes=["gfin"], key="c_gfin")
    P.dma("sp", lambda e: e.dma_start(out=subln[:], in_=subln_d.partition_broadcast(128)), writes=["subln"], key="c_subln")
    P.dma("sp", lambda e: e.dma_start(out=esink[:], in_=sink_d.partition_broadcast(128)), writes=["esink"], key="c_sink")
    for i in range(4):
        P.dma("sp", lambda e, i=i: e.dma_start(out=lamt[:, i, :], in_=lam_in[i].partition_broadcast(128)),
              writes=[("lamt", i)], key=("c_lam", i))
    P.op("pool", lambda e: e.memset(mhalf[:], -0.5), writes=["mhalf"])
    P.op("dve", lambda e: e.tensor_scalar(out=subln[:], in0=subln[:], scalar1=1.0 - LAMBDA_INIT, scalar2=None, op0=ALU.mult),
         writes=["subln"])
    P.op("act", lambda e: e.activation(out=esink[:], in_=esink[:], func=AF.Exp), writes=["esink"])
    for i in range(2):
        P.op("dve", lambda e, i=i: e.tensor_tensor(out=lamj[:], in0=lamt[:, 2 * i, :], in1=lamt[:, 2 * i + 1, :], op=ALU.mult),
             reads=[("lamt", 2 * i), ("lamt", 2 * i + 1)], writes=["lamj"])
        P.op("dve", lambda e, i=i: e.tensor_reduce(out=lams[:, i:i + 1], in_=lamj[:], axis=mybir.AxisListType.X, op=ALU.add),
             reads=["lamj"], writes=[("lams", i)])
    P.op("act", lambda e: e.activation(out=lams[:, 2:4], in_=lams[:, 0:2], func=AF.Exp),
         reads=[("lams", 0), ("lams", 1)], writes=[("lams", 2)])
    P.op("dve", lambda e: e.tensor_tensor(out=lams[:, 4:5], in0=lams[:, 3:4], in1=lams[:, 2:3], op=ALU.subtract),
         reads=[("lams", 2)], writes=[("lams", 4)])
    P.op("dve", lambda e: e.tensor_scalar(out=lams[:, 4:5], in0=lams[:, 4:5], scalar1=-LAMBDA_INIT, scalar2=None, op0=ALU.add),
         writes=[("lams", 4)])
    neglam = lams[:, 4:5]

    scr_jobs = []
    scr_keys = []
    scrA_keys, scrB_keys = [], []
    k = ("scrA", "kv")
    scrA_keys.append(k)
    scr_jobs.append((k, lambda e: e.dma_start(
        out=scr_kv[:, :, :], in_=w_in[:, 2048:2304].rearrange("(c p) n -> p c n", p=128)), "scrA"))
    for g in range(2):
        for r in range(4):
            k = ("scrA", "q", g, r)
            scrA_keys.append(k)
            scr_jobs.append((k, lambda e, g=g, r=r: e.dma_start(
                out=scr_q[:, :, r * 128 + g * 64:r * 128 + (g + 1) * 64],
                in_=w_in[:, 1536 + (g * 4 + r) * 64:1536 + (g * 4 + r + 1) * 64].rearrange("(c p) d -> p c d", p=128)), "scrA"))
    for nm, dst, src in (("ga", scr_ga, w_in[:, 2304:3328]), ("ba", scr_ba, w_ba), ("gb", scr_gb, w_in[:, 3328:4352]), ("bb", scr_bb, w_bb)):
        k = ("scrB", nm)
        scrB_keys.append(k)
        scr_jobs.append((k, lambda e, dst=dst, src=src: e.dma_start(
            out=dst[:, :, :], in_=src.rearrange("(c p) n -> p c n", p=128)), "scrB"))
    scr_jobs.append((("scrC", "o"), lambda e: e.dma_start(
        out=scr_o[:, :, :], in_=w_out.rearrange("(c p) n -> p c n", p=128)), "scrC"))
    for fg in range(11):
        for t, wsrc in enumerate((w_g, w_u)):
            k = ("scr", "gu", fg, t)
            scr_keys.append(k)
            scr_jobs.append((k, lambda e, fg=fg, t=t, wsrc=wsrc: e.dma_start(
                out=scr_gu[fg, :, :, t, :],
                in_=wsrc[:, fg * 256:(fg + 1) * 256].rearrange("(c p) n -> p c n", p=128)), "scr"))
    for fd in range(11):
        k = ("scr", "d", fd)
        scr_keys.append(k)
        scr_jobs.append((k, lambda e, fd=fd: e.dma_start(
            out=scr_d[fd], in_=w_d[fd * 256:(fd + 1) * 256, :].rearrange("(ff p) n -> p ff n", p=128)), "scr"))

    def issue_scratch(n):
        for _ in range(n):
            if scr_jobs and stage >= 5:
                k, fn, dk = scr_jobs.pop(0)
                P.dma("pool", fn, writes=[k], key=dk)

    def rms_rstd(src_ap, rows, slot, n, src_keys, on_dve=False):
        if on_dve is not False:
            sq = on_dve
            P.op("dve", lambda e: e.tensor_tensor(out=sq[:rows, 0:n], in0=src_ap, in1=src_ap, op=ALU.mult),
                 reads=src_keys, writes=["sqj"])
            P.op("dve", lambda e: e.tensor_reduce(out=stt[:rows, slot, 0:1], in_=sq[:rows, 0:n], axis=mybir.AxisListType.X, op=ALU.add),
                 reads=["sqj"], writes=[("stt", slot, 0)])
        else:
            P.op("act", lambda e: e.activation(out=junk[:rows, 0:n], in_=src_ap, func=AF.Square,
                                               accum_out=stt[:rows, slot, 0:1]),
                 reads=src_keys, writes=["junk", ("stt", slot, 0)])
        P.op("dve", lambda e: e.tensor_scalar(out=stt[:rows, slot, 1:2], in0=stt[:rows, slot, 0:1], scalar1=1.0 / n,
                                              scalar2=EPS, op0=ALU.mult, op1=ALU.add),
             reads=[("stt", slot, 0)], writes=[("stt", slot, 1)])
        P.op("pool", lambda e: e.tensor_tensor(out=stt[:rows, slot, 2:3], in0=stt[:rows, slot, 1:2], in1=mhalf[:rows, :],
                                               op=ALU.pow),
             reads=[("stt", slot, 1), "mhalf"], writes=[("stt", slot, 2)])
        return stt[:rows, slot, 2:3], ("stt", slot, 2)

    def blk_rows_col(kb):
        return (16, 0) if kb == 0 else (128, 16 + 128 * (kb - 1))

    m1 = A.mark()
    NXT, NXS, NPB = 4, 3, 4
    xt = [A.alloc("xt", [128, D], F32) for _ in range(NXT)]
    xs = [A.alloc("xs", [128, D], BF16) for _ in range(NXS)]
    p1 = {}

    def p1_a(kb):
        rows, col0 = blk_rows_col(kb)
        sl = kb % NXT
        src = meta if kb == 0 else x[(kb - 1) * 128:kb * 128, :]
        P.dma("sp", lambda e, sl=sl, rows=rows, src=src: e.dma_start(out=xt[sl][:rows, :], in_=src),
              writes=[("xt", sl)], key=("xt", sl))
        p1[kb] = rms_rstd(xt[sl][:rows, :], rows, sl, D, [("xt", sl)])

    def p1_b(kb):
        rows, col0 = blk_rows_col(kb)
        sl = kb % NXT
        ssl = kb % NXS
        rstd, rk = p1[kb]
        P.op("dve", lambda e, sl=sl, ssl=ssl, rows=rows, rstd=rstd: e.tensor_scalar(
            out=xs[ssl][:rows, :], in0=xt[sl][:rows, :], scalar1=rstd, scalar2=None, op0=ALU.mult),
            reads=[("xt", sl), rk], writes=[("xs", ssl)])
        pb = 4 + (kb % NPB)
        pv = bankb(pb).rearrange("p (c t) -> p c t", t=128)
        for c in range(8):
            P.op("pe", lambda e, c=c, ssl=ssl, rows=rows, pv=pv: e.transpose(
                out=pv[:, c, 0:rows], in_=xs[ssl][:rows, c * 128:(c + 1) * 128], identity=ident[:rows, :rows]),
                reads=[("xs", ssl), "ident"], writes=[PK(pb)])

    def p1_c(kb):
        rows, col0 = blk_rows_col(kb)
        pb = 4 + (kb % NPB)
        pv = bankb(pb).rearrange("p (c t) -> p c t", t=128)
        P.op("dve", lambda e, rows=rows, col0=col0, pv=pv: e.tensor_tensor(
            out=Zhn[:, :, col0:col0 + rows], in0=pv[:, :, 0:rows],
            in1=gmix[:, :].unsqueeze(2).broadcast_to([128, 8, rows]), op=ALU.mult),
            reads=["gmix"], writes=[PK(pb), ("Zhn", kb)])

    for i in range(33 + 2):
        if i < 33:
            p1_a(i)
        if 0 <= i - 1 < 33:
            p1_b(i - 1)
        if 0 <= i - 2 < 33:
            p1_c(i - 2)
    A.release(m1)
    if stage <= 1:
        return finish(nc, P, A, out, dbg, {"Zhn": (Zhn, [128, 8 * L], BF16)})

    P.barrier()

    m2 = A.mark()
    KT = A.alloc("KT", [128, L], BF16)
    Vaug = A.alloc("Vaug", [128, 33, 129], BF16)
    Wh = [A.alloc("Wh", [128, 8, 384], BF16) for _ in range(2)]
    QT = [A.alloc("QT", [128, 512], BF16) for _ in range(2)]
    Pt = [A.alloc("Pt", [128, 2, 512], BF16) for _ in range(3)]
    tmpf = [A.alloc("tmpf", [128, 2, 512], F32) for _ in range(2)]
    wtab = A.alloc("wtab", [128, 896], F32)
    acc = [A.alloc("acc", [128, 8, 129], F32) for _ in range(2)]
    yv4 = A.alloc("yv4", [128, 4, 128], F32)
    t4 = A.alloc("t4", [128, 4, 128], F32)
    ybf4 = A.alloc("ybf4", [128, 4, 128], BF16)
    rr8 = A.alloc("rr8", [128, 4, 2], F32)
    rn4 = A.alloc("rn4", [128, 4], F32)
    ss4 = A.alloc("ss4", [128, 3, 4], F32)
    mh4 = A.alloc("mh4", [128, 4], F32)
    P.op("pool", lambda e: e.memset(Vaug[:, :, 128:129], 1.0), writes=["Vones"])
    P.op("pool", lambda e: e.memset(mh4[:], -0.5), writes=["mh4"])
    all_zhn = [("Zhn", kb) for kb in range(33)]

    def load_wh(h):
        hs = h % 2
        for j, base in enumerate((0, 512, 1024)):
            P.dma("pool", lambda e, hs=hs, j=j, base=base, h=h: e.dma_start(
                out=Wh[hs][:, :, j * 128:(j + 1) * 128],
                in_=w_in[:, base + h * 128: base + (h + 1) * 128].rearrange("(c p) n -> p c n", p=128)),
                writes=[("Wh", hs, j)], key=("Wh", hs, j))

    load_wh(0)
    load_wh(1)
    pending_fin = []
    fill_q = []

    def flush_fin():
        while fill_q:
            fill_q.pop(0)()
        while pending_fin:
            T, hh = pending_fin.pop(0)
            for qb in range(4):
                P.op("pe", lambda e, qb=qb: e.transpose(out=bankb(7)[:, qb * 128:(qb + 1) * 128], in_=ybf4[:, qb, :], identity=ident[:, :]),
                     reads=["ybf4", "ident"], writes=[PK(7)])
            P.op("dve", lambda e, T=T, hh=hh: e.tensor_copy(out=Zy[:, hh, 512 * T:512 * T + 512], in_=bankb(7)[:, 0:512]),
                 writes=[PK(7)] + [("Zy", hh, 4 * T + qb) for qb in range(4)])

    fin_cnt = [0]
    pt_cnt = [0]
    st_cnt = [0]
    tf_cnt = [0]
    for h in range(4):
        hs = h % 2
        P.dma("sp", lambda e, h=h: e.dma_start(out=wtab[:, :], in_=wt_d[:, h * 896:(h + 1) * 896]), writes=["wtab"], key="c_wtab")
        for t in range(9):
            c0 = t * 512
            n = min(512, L - c0)
            pb = t % 4
            for c in range(8):
                P.op("pe", lambda e, c=c, pb=pb, n=n, c0=c0, hs=hs: e.matmul(
                    bank(pb)[:, 0:n], lhsT=Wh[hs][:, c, 128:256], rhs=Zhn[:, c, c0:c0 + n], start=(c == 0), stop=(c == 7)),
                    reads=[("Wh", hs, 1)] + all_zhn, writes=[PK(pb)])
            eng = "act" if t % 2 == 0 else "dve"
            if eng == "act":
                P.op("act", lambda e, pb=pb, n=n, c0=c0: e.activation(out=KT[:, c0:c0 + n], in_=bank(pb)[:, 0:n], func=AF.Copy),
                     writes=[PK(pb), ("KT", t)])
            else:
                P.op("dve", lambda e, pb=pb, n=n, c0=c0: e.tensor_copy(out=KT[:, c0:c0 + n], in_=bank(pb)[:, 0:n]),
                     writes=[PK(pb), ("KT", t)])
        flush_fin()
        for c in range(8):
            P.op("pe", lambda e, c=c, hs=hs: e.matmul(bank(4)[:16, 0:128], lhsT=Zhn[:, c, 0:16], rhs=Wh[hs][:, c, 256:384],
                                                      start=(c == 0), stop=(c == 7)),
                 reads=[("Wh", hs, 2)] + all_zhn, writes=[PK(4)])
        P.op("dve", lambda e: e.tensor_copy(out=Vaug[:16, 0, 0:128], in_=bank(4)[:16, 0:128]), writes=[PK(4), ("V", 0)])
        for q4 in range(8):
            pb = q4 % 4
            for i in range(4):
                kb = 1 + q4 * 4 + i
                col0 = 16 + 128 * (kb - 1)
                for c in range(8):
                    P.op("pe", lambda e, c=c, pb=pb, i=i, col0=col0, hs=hs: e.matmul(
                        bank(pb)[:, i * 128:(i + 1) * 128], lhsT=Zhn[:, c, col0:col0 + 128], rhs=Wh[hs][:, c, 256:384],
                        start=(c == 0), stop=(c == 7)),
                        reads=[("Wh", hs, 2)] + all_zhn, writes=[PK(pb)])
            kb0 = 1 + q4 * 4
            eng = "act" if q4 % 2 == 0 else "dve"
            if eng == "act":
                P.op("act", lambda e, pb=pb, kb0=kb0: e.activation(
                    out=Vaug[:, kb0:kb0 + 4, 0:128], in_=bank(pb).rearrange("p (i n) -> p i n", n=128), func=AF.Copy),
                    writes=[PK(pb), ("V", 1 + q4)])
            else:
                P.op("dve", lambda e, pb=pb, kb0=kb0: e.tensor_copy(
                    out=Vaug[:, kb0:kb0 + 4, 0:128], in_=bank(pb).rearrange("p (i n) -> p i n", n=128)),
                    writes=[PK(pb), ("V", 1 + q4)])
        all_kt = [("KT", t) for t in range(9)]
        all_v = [("V", i) for i in range(9)] + ["Vones"]
        def emit_qproj(T, immediate=False):
            qs = T % 2
            qcol0 = 16 + 512 * T
            jobs = []
            for c in range(8):
                jobs.append(lambda c=c, hs=hs, qcol0=qcol0: P.op("pe", lambda e: e.matmul(
                    bank(7)[:, :], lhsT=Wh[hs][:, c, 0:128], rhs=Zhn[:, c, qcol0:qcol0 + 512], start=(c == 0), stop=(c == 7)),
                    reads=[("Wh", hs, 0)] + all_zhn, writes=[PK(7)]))
            jobs.append(lambda qs=qs: P.op("dve", lambda e: e.tensor_copy(out=QT[qs][:, :], in_=bank(7)[:, :]),
                                           writes=[PK(7), ("QT", qs)]))
            if immediate:
                for j in jobs:
                    j()
            else:
                fill_q.extend(jobs)

        def drain_fill(n=None):
            k = 0
            while fill_q and (n is None or k < n):
                fill_q.pop(0)()
                k += 1

        THR = 60.0
        slope = SLOPE_A[h]
        items = []
        for T in range(8):
            below = [j for j in range(0, 4 * T) if slope * (128.0 * (4 * T - j - 1) + 1.0) < THR]
            above = [j for j in range(4 * T + 4, 32) if slope * (128.0 * (j - 4 * T - 4) + 1.0) < THR]
            groups = [("below", below), ("diag", ["meta", 4 * T, 4 * T + 1, 4 * T + 2, 4 * T + 3]), ("above", above)]
            groups = [g for g in groups if g[1]]
            for gi, (gname, blocks) in enumerate(groups):
                for bi, blk in enumerate(blocks):
                    items.append(dict(T=T, gname=gname, blk=blk, bi=bi, nb=len(blocks), first_group=(gi == 0),
                                      last_group=(gi == len(groups) - 1), first_in_tile=(gi == 0 and bi == 0)))

        def emit_qk_exp(it):
            T, gname, blk = it["T"], it["gname"], it["blk"]
            qs = T % 2
            buf = st_cnt[0] % 2
            st_cnt[0] += 1
            pbi = pt_cnt[0] % 3
            pt_cnt[0] += 1
            it["pbi"] = pbi
            if blk == "meta":
                rows, kcol0, kb = 16, 0, 0
            else:
                rows, kcol0, kb = 128, 16 + 128 * blk, blk + 1
            it["rows"], it["kb"] = rows, kb
            for s in range(2):
                P.op("pe", lambda e, s=s, buf=buf, rows=rows, kcol0=kcol0, qs=qs: e.matmul(
                    PS3[:rows, buf * 2 + s, :], lhsT=KT[s * 64:(s + 1) * 64, kcol0:kcol0 + rows],
                    rhs=QT[qs][s * 64:(s + 1) * 64, :], start=True, stop=True),
                    reads=all_kt + [("QT", qs)], writes=[PK(buf * 2 + s)])
            stin = PS3[:rows, buf * 2:buf * 2 + 2, :]
            if gname == "diag" and blk != "meta":
                jl = blk - 4 * T
                off = 384 - 128 * jl
                tb = tf_cnt[0] % 2
                tf_cnt[0] += 1
                for s in range(2):
                    P.op("dve", lambda e, s=s, buf=buf, tb=tb, off=off, h=h: e.scalar_tensor_tensor(
                        out=tmpf[tb][:, s, :], in0=PS3[:, buf * 2 + s, :], scalar=0.125,
                        in1=wtab[:, off:off + 512], op0=ALU.mult, op1=ALU.add),
                        reads=["wtab"], writes=[PK(buf * 2 + s), ("tmpf", tb, s)])
                P.op("act", lambda e, tb=tb, pbi=pbi: e.activation(out=Pt[pbi][:, :, :], in_=tmpf[tb][:, :, :], func=AF.Exp),
                     reads=[("tmpf", tb, 0), ("tmpf", tb, 1)], writes=[("Pt", pbi)])
            elif blk == "meta":
                P.op("act", lambda e, pbi=pbi, stin=stin, rows=rows: e.activation(
                    out=Pt[pbi][:rows, :, :], in_=stin, func=AF.Exp, scale=0.125),
                    writes=[PK(buf * 2), PK(buf * 2 + 1), ("Pt", pbi)])
            else:
                if gname == "below":
                    col = h * 32 + (4 * T - blk)
                else:
                    col = 128 + h * 32 + (blk - 4 * T - 4)
                P.op("act", lambda e, pbi=pbi, stin=stin, col=col: e.activation(
                    out=Pt[pbi][:, :, :], in_=stin, func=AF.Exp, scale=0.125, bias=cst[:, col:col + 1]),
                    reads=["cst"], writes=[PK(buf * 2), PK(buf * 2 + 1), ("Pt", pbi)])

        def oreg(s, qb):
            ri = qb * 2 + s
            return 4 + ri // 3, (ri % 3) * 129, ri

        def emit_pv_post(it):
            T, gname, bi, nb = it["T"], it["gname"], it["bi"], it["nb"]
            pbi, rows, kb = it["pbi"], it["rows"], it["kb"]
            ab = T % 2
            for qb in range(4):
                for s in range(2):
                    ob, oo, ri = oreg(s, qb)
                    P.op("pe", lambda e, s=s, qb=qb, ob=ob, oo=oo, ri=ri, pbi=pbi, rows=rows, kb=kb, bi=bi, nb=nb: e.matmul(
                        PS3[:, ob, oo:oo + 129], lhsT=Pt[pbi][:rows, s, qb * 128:(qb + 1) * 128],
                        rhs=Vaug[:rows, kb, 0:129], start=(bi == 0 and ri % 3 == 0), stop=(bi == nb - 1),
                        skip_group_check=True),
                        reads=all_v + [("Pt", pbi)], writes=[PK(ob)])
            if bi != nb - 1:
                return
            for qb in range(4):
                ob0, oo0, ri0 = oreg(0, qb)
                ob1, oo1, ri1 = oreg(1, qb)
                if ob0 == ob1:
                    pieces = [(ob0, oo0, ri0, 258, [("acc", ab, 0, qb), ("acc", ab, 1, qb)])]
                else:
                    pieces = [(ob0, oo0, ri0, 129, [("acc", ab, 0, qb)]), (ob1, oo1, ri1, 129, [("acc", ab, 1, qb)])]
                if gname == "below":
                    cap = cst[:, 256 + h * 4 + qb: 256 + h * 4 + qb + 1]
                elif gname == "above":
                    cap = cst[:, 272 + h * 4 + qb: 272 + h * 4 + qb + 1]
                else:
                    cap = None
                for (ob, oo, ri, w, akeys) in pieces:
                    src = PS3[:, ob, oo:oo + w]
                    dst = acc[ab][:].rearrange("p r c -> p (r c)")[:, ri * 129: ri * 129 + w]
                    if it["first_group"] and h < 2:
                        if cap is None:
                            P.op("act", lambda e, src=src, dst=dst: e.activation(out=dst, in_=src, func=AF.Copy),
                                 writes=[PK(ob)] + akeys)
                        else:
                            P.op("act", lambda e, src=src, dst=dst, cap=cap: e.activation(out=dst, in_=src, func=AF.Copy, scale=cap),
                                 reads=["cst"], writes=[PK(ob)] + akeys)
                    elif it["first_group"]:
                        if cap is None:
                            P.op("dve", lambda e, src=src, dst=dst: e.tensor_copy(out=dst, in_=src),
                                 writes=[PK(ob)] + akeys)
                        else:
                            P.op("dve", lambda e, src=src, dst=dst, cap=cap: e.tensor_scalar(
                                out=dst, in0=src, scalar1=cap, scalar2=None, op0=ALU.mult),
                                reads=["cst"], writes=[PK(ob)] + akeys)
                    else:
                        if cap is None:
                            P.op("dve", lambda e, src=src, dst=dst: e.tensor_tensor(out=dst, in0=src, in1=dst, op=ALU.add),
                                 writes=[PK(ob)] + akeys)
                        else:
                            P.op("dve", lambda e, src=src, dst=dst, cap=cap: e.scalar_tensor_tensor(
                                out=dst, in0=src, scalar=cap, in1=dst, op0=ALU.mult, op1=ALU.add),
                                reads=["cst"], writes=[PK(ob)] + akeys)
            if not it["last_group"]:
                return
            accv = acc[ab][:].rearrange("p (q s) c -> p q s c", s=2)
            akeys = [("acc", ab, s_, qb_) for s_ in range(2) for qb_ in range(4)]
            P.op("dve", lambda e, accv=accv: e.reciprocal(out=rr8[:, :, :], in_=accv[:, :, :, 128]),
                 reads=akeys, writes=["rr8"])
            P.op("pool", lambda e: e.tensor_scalar(out=rn4[:, :], in0=rr8[:, :, 1], scalar1=neglam, scalar2=None, op0=ALU.mult),
                 reads=["rr8", ("lams", 4)], writes=["rn4"])
            P.op("pool", lambda e, accv=accv: e.tensor_tensor(
                out=yv4[:, :, :], in0=accv[:, :, 0, 0:128], in1=rr8[:, :, 0].unsqueeze(2).broadcast_to([128, 4, 128]), op=ALU.mult),
                reads=akeys + ["rr8"], writes=["yv4"])
            P.op("pool", lambda e, accv=accv: e.tensor_tensor(
                out=t4[:, :, :], in0=accv[:, :, 1, 0:128], in1=rn4[:, :].unsqueeze(2).broadcast_to([128, 4, 128]), op=ALU.mult),
                reads=akeys + ["rn4"], writes=["t4"])
            P.op("pool", lambda e: e.tensor_tensor(out=yv4[:, :, :], in0=yv4[:, :, :], in1=t4[:, :, :], op=ALU.add),
                 reads=["t4"], writes=["yv4"])
            P.op("pool", lambda e: e.tensor_tensor(out=t4[:, :, :], in0=yv4[:, :, :], in1=yv4[:, :, :], op=ALU.mult),
                 reads=["yv4"], writes=["t4"])
            P.op("dve", lambda e: e.tensor_reduce(out=ss4[:, 0, :], in_=t4[:, :, :], axis=mybir.AxisListType.X, op=ALU.add),
                 reads=["t4"], writes=[("ss4", 0)])
            P.op("pool", lambda e: e.tensor_scalar(out=ss4[:, 1, :], in0=ss4[:, 0, :], scalar1=1.0 / 128, scalar2=EPS,
                                                   op0=ALU.mult, op1=ALU.add),
                 reads=[("ss4", 0)], writes=[("ss4", 1)])
            P.op("pool", lambda e: e.tensor_tensor(out=ss4[:, 2, :], in0=ss4[:, 1, :], in1=mh4[:, :], op=ALU.pow),
                 reads=[("ss4", 1), "mh4"], writes=[("ss4", 2)])
            P.op("pool", lambda e: e.tensor_tensor(
                out=yv4[:, :, :], in0=yv4[:, :, :], in1=ss4[:, 2, :].unsqueeze(2).broadcast_to([128, 4, 128]), op=ALU.mult),
                reads=[("ss4", 2)], writes=["yv4"])
            P.op("pool", lambda e: e.tensor_tensor(
                out=ybf4[:, :, :], in0=yv4[:, :, :], in1=subln[:, :].unsqueeze(1).broadcast_to([128, 4, 128]), op=ALU.mult),
                reads=["yv4", "subln"], writes=["ybf4"])
            pending_fin.append((T, h))
            issue_scratch(2)

        emit_qproj(0, immediate=True)
        prev = None
        since_fin = 0
        for it in items:
            if it["first_in_tile"]:
                drain_fill()
            emit_qk_exp(it)
            if it["first_in_tile"] and it["T"] + 1 < 8:
                emit_qproj(it["T"] + 1)
            drain_fill(2)
            if prev is not None:
                emit_pv_post(prev)
            prev = it
            if pending_fin:
                since_fin += 1
                if since_fin >= 14 or (it["last_group"] and it["bi"] >= it["nb"] - 2):
                    flush_fin()
                    since_fin = 0
        emit_pv_post(prev)
        drain_fill()
        if h + 2 < 4:
            load_wh(h + 2)
    flush_fin()
    issue_scratch(100)
    A.release(m2)
    if stage <= 2:
        return finish(nc, P, A, out, dbg, {"Zy": (Zy, [128, 8 * SEQ], BF16)})
    P.barrier()

    m3 = A.mark()
    KTb = A.alloc("KTb", [128, L], BF16)
    Vb = A.alloc("Vb", [128, 33, 2, 65], BF16)
    Wqb = A.alloc("Wqb", [128, 8, 512], BF16)
    Wkvb = A.alloc("Wkvb", [128, 8, 256], BF16)
    QTb = [A.alloc("QTb", [128, 4, 512], BF16) for _ in range(2)]
    bwt = A.alloc("bwt", [128, 2, 3, 512], F32)
    tmpb = [A.alloc("tmpb", [128, 512], F32) for _ in range(3)]
    Ptb = [A.alloc("Ptb", [128, 512], BF16) for _ in range(5)]
    d8 = [A.alloc("d8", [128, 8], F32) for _ in range(2)]
    ybb = [A.alloc("ybb", [128, 512], BF16) for _ in range(2)]
    osb = [A.alloc("osb", [128, 2, 260], F32) for _ in range(2)]
    P.dma("sp", lambda e: e.dma_start(out=bwt[:].rearrange("p g t n -> p (g t n)"), in_=bwt_d), writes=["bwt"], key="c_bwt")
    P.dma("sp", lambda e: e.dma_start(out=Wkvb[:], in_=scr_kv), reads=scrA_keys, writes=["Wkvb"], key="Wkvb")
    P.dma("sp", lambda e: e.dma_start(out=Wqb[:], in_=scr_q), reads=scrA_keys,
          writes=[("Wqb", g, r) for g in range(2) for r in range(4)], key="Wqb")
    P.op("pool", lambda e: e.memset(Vb[:, :, :, 64:65], 1.0), writes=["Vbones"])
    for t in range(9):
        c0 = t * 512
        n = min(512, L - c0)
        pb = t % 4
        for c in range(8):
            P.op("pe", lambda e, c=c, pb=pb, n=n, c0=c0: e.matmul(
                bank(pb)[:, 0:n], lhsT=Wkvb[:, c, 0:128], rhs=Zhn[:, c, c0:c0 + n], start=(c == 0), stop=(c == 7)),
                reads=["Wkvb"] + all_zhn, writes=[PK(pb)])
        P.op("dve", lambda e, pb=pb, n=n, c0=c0: e.tensor_copy(out=KTb[:, c0:c0 + n], in_=bank(pb)[:, 0:n]),
             writes=[PK(pb), ("KTb", t)])
    for c in range(8):
        P.op("pe", lambda e, c=c: e.matmul(bank(4)[:16, 0:128], lhsT=Zhn[:, c, 0:16], rhs=Wkvb[:, c, 128:256],
                                           start=(c == 0), stop=(c == 7)),
             reads=["Wkvb"] + all_zhn, writes=[PK(4)])
    P.op("dve", lambda e: e.tensor_copy(out=Vb[:16, 0, :, 0:64], in_=bank(4)[:16, 0:128].rearrange("p (g d) -> p g d", d=64)),
         writes=[PK(4), ("Vb", 0)])
    for q4 in range(8):
        pb = q4 % 4
        for i in range(4):
            kb = 1 + q4 * 4 + i
            col0 = 16 + 128 * (kb - 1)
            for c in range(8):
                P.op("pe", lambda e, c=c, pb=pb, i=i, col0=col0: e.matmul(
                    bank(pb)[:, i * 128:(i + 1) * 128], lhsT=Zhn[:, c, col0:col0 + 128], rhs=Wkvb[:, c, 128:256],
                    start=(c == 0), stop=(c == 7)),
                    reads=["Wkvb"] + all_zhn, writes=[PK(pb)])
        kb0 = 1 + q4 * 4
        for g in range(2):
            P.op("dve", lambda e, pb=pb, kb0=kb0, g=g: e.tensor_copy(
                out=Vb[:, kb0:kb0 + 4, g, 0:64],
                in_=bank(pb).rearrange("p (i g d) -> p i g d", g=2, d=64)[:, :, g, :]),
                writes=[PK(pb), ("Vb", 1 + q4, g)])
    all_ktb = [("KTb", t) for t in range(9)]
    all_vb = [("Vb", 0), "Vbones"] + [("Vb", 1 + q4, g) for q4 in range(8) for g in range(2)]
    stb_cnt = [0]
    ptb_cnt = [0]
    tb_cnt = [0]
    def emit_qbproj(t8, rs=(0, 1, 2, 3)):
        qs = t8 % 2
        qc0 = 16 + 512 * t8
        for r in rs:
            for c in range(8):
                P.op("pe", lambda e, c=c, r=r, qc0=qc0: e.matmul(
                    bank(6)[:, :], lhsT=Wqb[:, c, r * 128:(r + 1) * 128], rhs=Zhn[:, c, qc0:qc0 + 512],
                    start=(c == 0), stop=(c == 7)),
                    reads=[("Wqb", g_, r_) for g_ in range(2) for r_ in range(4)] + all_zhn, writes=[PK(6)])
            if r % 2 == 0:
                P.op("act", lambda e, r=r, qs=qs: e.activation(out=QTb[qs][:, r, :], in_=bank(6)[:, :], func=AF.Copy),
                     writes=[PK(6), ("QTb", qs, r)])
            else:
                P.op("dve", lambda e, r=r, qs=qs: e.tensor_copy(out=QTb[qs][:, r, :], in_=bank(6)[:, :]),
                     writes=[PK(6), ("QTb", qs, r)])

    bitems = []
    for i in range(32):
        types = [("meta", None)]
        if i - 1 >= 0:
            types.append((0, i - 1))
        types.append((1, i))
        if i + 1 <= 31:
            types.append((2, i + 1))
        for ti, (ty, j) in enumerate(types):
            for g in range(2):
                bitems.append(dict(i=i, ty=ty, j=j, ti=ti, nt=len(types), g=g))

    def emit_b_qk(it):
        i, ty, j, g = it["i"], it["ty"], it["j"], it["g"]
        t8, qi = i // 4, i % 4
        qs = t8 % 2
        qoff = qi * 128
        qkeys = [("QTb", qs, r) for r in range(4)]
        sb = stb_cnt[0] % 4
        stb_cnt[0] += 1
        pbi = ptb_cnt[0] % 5
        ptb_cnt[0] += 1
        if ty == "meta":
            rows, kcol0, kb = 16, 0, 0
        else:
            rows, kcol0, kb = 128, 16 + 128 * j, j + 1
        it["pbi"], it["rows"], it["kb"], it["sb"] = pbi, rows, kb, sb
        P.op("pe", lambda e, g=g, sb=sb, rows=rows, kcol0=kcol0, qs=qs, qoff=qoff: e.matmul(
            PS3[:rows, sb, :].rearrange("p (r q) -> p r q", q=128),
            lhsT=KTb[g * 64:(g + 1) * 64, kcol0:kcol0 + rows],
            rhs=QTb[qs][g * 64:(g + 1) * 64, :, qoff:qoff + 128], start=True, stop=True),
            reads=all_ktb + qkeys, writes=[PK(sb)])

    def emit_b_exp(it):
        ty, g, sb, pbi = it["ty"], it["g"], it["sb"], it["pbi"]
        if ty == "meta":
            P.op("act", lambda e, pbi=pbi, sb=sb: e.activation(
                out=Ptb[pbi][:16, :], in_=PS3[:16, sb, :], func=AF.Exp, scale=0.125),
                writes=[PK(sb), ("Ptb", pbi)])
        else:
            tb = tb_cnt[0] % 3
            tb_cnt[0] += 1
            P.op("dve", lambda e, g=g, sb=sb, tb=tb, ty=ty: e.scalar_tensor_tensor(
                out=tmpb[tb][:, :], in0=PS3[:, sb, :], scalar=0.125, in1=bwt[:, g, ty, :],
                op0=ALU.mult, op1=ALU.add),
                reads=["bwt"], writes=[PK(sb), ("tmpb", tb)])
            P.op("act", lambda e, tb=tb, pbi=pbi: e.activation(out=Ptb[pbi][:, :], in_=tmpb[tb][:, :], func=AF.Exp),
                 reads=[("tmpb", tb)], writes=[("Ptb", pbi)])

    def emit_b_pv_post(it):
        i, ti, nt, g = it["i"], it["ti"], it["nt"], it["g"]
        pbi, rows, kb = it["pbi"], it["rows"], it["kb"]
        fsl = i % 2
        for r in range(4):
            P.op("pe", lambda e, g=g, r=r, pbi=pbi, rows=rows, kb=kb, ti=ti, nt=nt: e.matmul(
                PS3[:, 4 + g, r * 65:(r + 1) * 65], lhsT=Ptb[pbi][:rows, r * 128:(r + 1) * 128],
                rhs=Vb[:rows, kb, g, 0:65], start=(ti == 0 and r == 0), stop=(ti == nt - 1),
                skip_group_check=True),
                reads=all_vb + [("Ptb", pbi)], writes=[PK(4 + g)])
        if ti != nt - 1:
            return
        P.op("dve", lambda e, g=g, fsl=fsl: e.tensor_copy(out=osb[fsl][:, g, :], in_=PS3[:, 4 + g, 0:260]),
             writes=[PK(4 + g), ("osb", fsl, g)])
        ov = osb[fsl][:, g, :].rearrange("p (r e) -> p r e", e=65)
        P.op("dve", lambda e, g=g, ov=ov, fsl=fsl: e.tensor_tensor(
            out=d8[fsl][:, g * 4:(g + 1) * 4], in0=ov[:, :, 64], in1=esink[:, g * 4:(g + 1) * 4], op=ALU.add),
            reads=["esink", ("osb", fsl, g)], writes=[("d8", fsl, g)])
        P.op("dve", lambda e, g=g, fsl=fsl: e.reciprocal(out=d8[fsl][:, g * 4:(g + 1) * 4], in_=d8[fsl][:, g * 4:(g + 1) * 4]),
             writes=[("d8", fsl, g)])
        P.op("dve", lambda e, g=g, ov=ov, fsl=fsl: e.tensor_tensor(
            out=ybb[fsl][:, g * 256:(g + 1) * 256].rearrange("p (r d) -> p r d", d=64), in0=ov[:, :, 0:64],
            in1=d8[fsl][:, g * 4:(g + 1) * 4].unsqueeze(2).broadcast_to([128, 4, 64]), op=ALU.mult),
            reads=[("d8", fsl, g), ("osb", fsl, g)], writes=[("ybb", fsl, g)])
        if g == 1:
            pending_b.append(i)

    pending_b = []

    def flush_b():
        tv = bankb(7).rearrange("p (c t) -> p c t", t=128)
        while pending_b:
            i = pending_b.pop(0)
            fsl = i % 2
            for cc in range(4):
                P.op("pe", lambda e, cc=cc, fsl=fsl, tv=tv: e.transpose(
                    out=tv[:, fsl * 4 + cc, :], in_=ybb[fsl][:, cc * 128:(cc + 1) * 128], identity=ident[:, :]),
                    reads=[("ybb", fsl, 0), ("ybb", fsl, 1), "ident"], writes=[PK(7)])
            P.op("dve", lambda e, fsl=fsl, tv=tv, i=i: e.tensor_copy(
                out=Zy[:, 4:8, i * 128:(i + 1) * 128], in_=tv[:, fsl * 4:fsl * 4 + 4, :]),
                writes=[PK(7), ("Zy", 4, i)])

    emit_qbproj(0)
    since_b = 0
    npairs = len(bitems) // 2
    for k in range(npairs):
        a, b = bitems[2 * k], bitems[2 * k + 1]
        emit_b_qk(a)
        emit_b_qk(b)
        emit_b_exp(a)
        emit_b_exp(b)
        if a["ti"] == 1 and a["i"] // 4 + 1 < 8:
            emit_qbproj(a["i"] // 4 + 1, rs=(a["i"] % 4,))
        if k >= 1:
            emit_b_pv_post(bitems[2 * k - 2])
            emit_b_pv_post(bitems[2 * k - 1])
        if pending_b:
            since_b += 1
            if since_b >= 3:
                flush_b()
                since_b = 0
    emit_b_pv_post(bitems[-2])
    emit_b_pv_post(bitems[-1])
    flush_b()
    A.release(m3)
    if stage <= 3:
        return finish(nc, P, A, out, dbg, {"Zy": (Zy, [128, 8 * SEQ], BF16)})
    P.barrier()

    m4 = A.mark()
    Wga = A.alloc("Wga", [128, 8, D], BF16)
    Wgb = A.alloc("Wgb", [128, 8, D], BF16)
    Wba = A.alloc("Wba", [128, 4, D], BF16)
    Wbb = A.alloc("Wbb", [128, 4, D], BF16)
    mT = A.alloc("mT", [128, 8, 512], BF16)
    sg = [A.alloc("sg", [128, 2, 512], F32) for _ in range(2)]
    P.dma("sp", lambda e: e.dma_start(out=Wga[:], in_=scr_ga), reads=scrB_keys, writes=["Wga"], key="Wga")
    P.dma("sp", lambda e: e.dma_start(out=Wba[:], in_=scr_ba), reads=scrB_keys, writes=["Wba"], key="Wba")
    P.dma("sp", lambda e: e.dma_start(out=Wgb[:], in_=scr_gb), reads=scrB_keys, writes=["Wgb"], key="Wgb")
    P.dma("sp", lambda e: e.dma_start(out=Wbb[:], in_=scr_bb), reads=scrB_keys, writes=["Wbb"], key="Wbb")
    for t8 in range(8):
        zc0 = 16 + 512 * t8
        yc0 = 512 * t8
        zkeys = [("Zhn", 1 + 4 * t8 + i) for i in range(4)]
        ykeys = [("Zy", hh, 4 * t8 + i) for hh in range(5) for i in range(4)]
        for f in range(8):
            b0 = (f % 2) * 4
            sl = f % 2
            for (w, bnk, nchunk, src, coff, wk, rk) in ((Wga, b0, 8, Zhn, zc0, "Wga", zkeys), (Wba, b0 + 1, 4, Zy, yc0, "Wba", ykeys),
                                                         (Wgb, b0 + 2, 8, Zhn, zc0, "Wgb", zkeys), (Wbb, b0 + 3, 4, Zy, yc0, "Wbb", ykeys)):
                for c in range(nchunk):
                    cs = c if w is not Wbb else 4 + c
                    P.op("pe", lambda e, w=w, bnk=bnk, c=c, cs=cs, nchunk=nchunk, src=src, coff=coff, f=f: e.matmul(
                        bank(bnk)[:, :], lhsT=w[:, c, f * 128:(f + 1) * 128], rhs=src[:, cs, coff:coff + 512],
                        start=(c == 0), stop=(c == nchunk - 1)),
                        reads=[wk] + rk, writes=[PK(bnk)])
            P.op("act", lambda e, b0=b0, sl=sl: e.activation(out=sg[sl][:, 0, :], in_=bank(b0)[:, :], func=AF.Sigmoid),
                 writes=[PK(b0), ("sg", sl, 0)])
            P.op("act", lambda e, b0=b0, sl=sl: e.activation(out=sg[sl][:, 1, :], in_=bank(b0 + 2)[:, :], func=AF.Sigmoid),
                 writes=[PK(b0 + 2), ("sg", sl, 1)])
            P.op("dve", lambda e, b0=b0, sl=sl: e.tensor_tensor(out=sg[sl][:, 0, :], in0=bank(b0 + 1)[:, :], in1=sg[sl][:, 0, :], op=ALU.mult),
                 writes=[PK(b0 + 1), ("sg", sl, 0)])
            P.op("dve", lambda e, b0=b0, sl=sl: e.tensor_tensor(out=sg[sl][:, 1, :], in0=bank(b0 + 3)[:, :], in1=sg[sl][:, 1, :], op=ALU.mult),
                 writes=[PK(b0 + 3), ("sg", sl, 1)])
            P.op("pool", lambda e, sl=sl, f=f: e.tensor_tensor(out=mT[:, f, :], in0=sg[sl][:, 0, :], in1=sg[sl][:, 1, :], op=ALU.add),
                 reads=[("sg", sl, 0), ("sg", sl, 1)], writes=[("mT", f)])
        P.op("pool", lambda e, zc0=zc0: e.tensor_copy(out=Zhn[:, :, zc0:zc0 + 512], in_=mT[:, :, :]),
             reads=[("mT", f) for f in range(8)], writes=zkeys)
    A.release(m4)
    if stage <= 4:
        return finish(nc, P, A, out, dbg, {"Zhn": (Zhn, [128, 8 * L], BF16)})
    P.barrier()

    A.release(zy_off)
    Wo = A.alloc("Wo", [128, 8, D], BF16)
    h1b = [A.alloc("h1", [128, 4, D], F32) for _ in range(2)]
    xin = [A.alloc("xin", [128, D], F32) for _ in range(2)]
    hsb = [A.alloc("hsb", [128, D], BF16) for _ in range(2)]
    hn1T = A.alloc("hn1T", [128, 8, 512], BF16)
    hidT = A.alloc("hidT", [128, 22, 512], BF16)
    outt = [A.alloc("outt", [128, D], F32) for _ in range(2)]
    sgf = [A.alloc("sgf", [128, 512], F32) for _ in range(2)]
    NGU, NWD = 3, 2
    Wgu = [A.alloc("Wgu", [128, 8, 2, 256], BF16) for _ in range(NGU)]
    Wd = [A.alloc("Wd", [128, 2, 1024], BF16) for _ in range(NWD)]
    P.dma("sp", lambda e: e.dma_start(out=Wo[:], in_=scr_o), reads=[("scrC", "o")], writes=["Wo"], key="Wo")
    gu_cnt = [0]
    wd_cnt = [0]
    xi_cnt = [0]
    ot_cnt = [0]

    def load_gu(fg):
        sl = gu_cnt[0] % NGU
        gu_cnt[0] += 1
        P.dma("sp", lambda e, sl=sl, fg=fg: e.dma_start(out=Wgu[sl][:], in_=scr_gu[fg]), reads=scr_keys,
              writes=[("Wgu", sl)], key=("Wgu", sl))
        return sl

    def load_wd(fd):
        sl = wd_cnt[0] % NWD
        wd_cnt[0] += 1
        P.dma("sp", lambda e, sl=sl, fd=fd: e.dma_start(out=Wd[sl][:], in_=scr_d[fd]), reads=scr_keys,
              writes=[("Wd", sl)], key=("Wd", sl))
        return sl

    def stage_w(t8, blk):
        hb = t8 % 2
        h1 = h1b[hb]
        tok0 = 512 * t8 + 128 * blk
        zcol = 16 + tok0
        xsl = xi_cnt[0] % 2
        xi_cnt[0] += 1
        P.dma("sp", lambda e, xsl=xsl, tok0=tok0: e.dma_start(out=xin[xsl][:], in_=x[tok0:tok0 + 128, :]),
              writes=[("xin", xsl)], key=("xin", xsl))
        zk = [("Zhn", 1 + tok0 // 128)]
        b0 = (blk % 2) * 2
        for half in range(2):
            for f in range(8):
                P.op("pe", lambda e, half=half, f=f, zcol=zcol, b0=b0: e.matmul(
                    bank(b0 + half)[:, :], lhsT=Zhn[:, f, zcol:zcol + 128], rhs=Wo[:, f, half * 512:(half + 1) * 512],
                    start=(f == 0), stop=(f == 7)),
                    reads=["Wo"] + zk, writes=[PK(b0 + half)])
            P.op("dve", lambda e, half=half, blk=blk, xsl=xsl, b0=b0, h1=h1: e.tensor_tensor(
                out=h1[:, blk, half * 512:(half + 1) * 512], in0=bank(b0 + half)[:, :], in1=xin[xsl][:, half * 512:(half + 1) * 512],
                op=ALU.add),
                reads=[("xin", xsl)], writes=[PK(b0 + half), ("h1", hb, blk, half)])
        rstd, rk = rms_rstd(h1[:, blk, :], 128, blk, D, [("h1", hb, blk, 0), ("h1", hb, blk, 1)])
        hsl = blk % 2
        P.op("dve", lambda e, hsl=hsl, blk=blk, rstd=rstd, h1=h1: e.tensor_scalar(
            out=hsb[hsl][:, :], in0=h1[:, blk, :], scalar1=rstd, scalar2=None, op0=ALU.mult),
            reads=[("h1", hb, blk, 0), ("h1", hb, blk, 1), rk], writes=[("hsb", hsl)])

    def stage_t(t8, blk):
        hsl = blk % 2
        pb = 4 + (blk % 2)
        pv = bankb(pb).rearrange("p (c t) -> p c t", t=128)
        for c in range(8):
            P.op("pe", lambda e, c=c, hsl=hsl, pv=pv: e.transpose(
                out=pv[:, c, :], in_=hsb[hsl][:, c * 128:(c + 1) * 128], identity=ident[:, :]),
                reads=[("hsb", hsl), "ident"], writes=[PK(pb)])
        P.op("dve", lambda e, blk=blk, pv=pv: e.tensor_tensor(
            out=hn1T[:, :, blk * 128:(blk + 1) * 128], in0=pv[:, :, :],
            in1=gffn[:, :].unsqueeze(2).broadcast_to([128, 8, 128]), op=ALU.mult),
            reads=["gffn"], writes=[PK(pb), ("hn1T", blk)])

    def stage_a(t8):
        stage_w(t8, 0)
        stage_w(t8, 1)
        stage_t(t8, 0)
        stage_w(t8, 2)
        stage_t(t8, 1)
        stage_w(t8, 3)
        stage_t(t8, 2)
        stage_t(t8, 3)

    def final_blk(t8, blk):
        hb = t8 % 2
        h1 = h1b[hb]
        tok0 = 512 * t8 + 128 * blk
        rstd, rk = rms_rstd(h1[:, blk, :], 128, 4 + blk, D, [("h1", hb, blk, 0), ("h1", hb, blk, 1)])
        osl = ot_cnt[0] % 2
        ot_cnt[0] += 1
        P.op("dve", lambda e, osl=osl, blk=blk, rstd=rstd, h1=h1: e.scalar_tensor_tensor(
            out=outt[osl][:, :], in0=h1[:, blk, :], scalar=rstd, in1=gfin[:, :], op0=ALU.mult, op1=ALU.mult),
            reads=[("h1", hb, blk, 0), ("h1", hb, blk, 1), rk, "gfin"], writes=[("outt", osl)])
        P.dma("sp", lambda e, osl=osl, tok0=tok0: e.dma_start(out=out[tok0:tok0 + 128, :], in_=outt[osl][:, :]),
              reads=[("outt", osl)], writes=[("outd", tok0)], key=("outt", osl))

    hkeys = [("hn1T", b) for b in range(4)]
    deferred = []
    stage_a(0)
    for t8 in range(8):
        hb = t8 % 2
        h1 = h1b[hb]
        pending = []
        for fg in range(min(NGU - 1, 11)):
            pending.append(load_gu(fg))
        nxt = len(pending)
        for fg in range(11):
            if nxt < 11:
                pending.append(load_gu(nxt))
                nxt += 1
            sl = pending.pop(0)
            for fi in range(2):
                f = fg * 2 + fi
                bg = 4 + (f % 2) * 2
                for tsel in range(2):
                    for c in range(8):
                        P.op("pe", lambda e, c=c, sl=sl, tsel=tsel, fi=fi, bg=bg: e.matmul(
                            bank(bg + tsel)[:, :], lhsT=Wgu[sl][:, c, tsel, fi * 128:(fi + 1) * 128], rhs=hn1T[:, c, :],
                            start=(c == 0), stop=(c == 7)),
                            reads=[("Wgu", sl)] + hkeys, writes=[PK(bg + tsel)])
                ssl = f % 2
                P.op("act", lambda e, bg=bg, ssl=ssl: e.activation(out=sgf[ssl][:, :], in_=bank(bg)[:, :], func=AF.Silu),
                     writes=[PK(bg), ("sgf", ssl)])
                P.op("dve", lambda e, bg=bg, ssl=ssl, f=f: e.tensor_tensor(
                    out=hidT[:, f, :], in0=bank(bg + 1)[:, :], in1=sgf[ssl][:, :], op=ALU.mult),
                    reads=[("sgf", ssl)], writes=[PK(bg + 1), ("hidT", f)])
                if deferred and f in (1, 5, 9, 13):
                    deferred.pop(0)()
        while deferred:
            deferred.pop(0)()
        pending = []
        for fd in range(min(NWD - 1, 11)):
            pending.append(load_wd(fd))
        nxt = len(pending)
        for fd in range(11):
            if nxt < 11:
                pending.append(load_wd(nxt))
                nxt += 1
            sl = pending.pop(0)
            for ff in range(2):
                f = fd * 2 + ff
                for blk in range(4):
                    for half in range(2):
                        P.op("pe", lambda e, f=f, ff=ff, sl=sl, blk=blk, half=half: e.matmul(
                            bank(blk * 2 + half)[:, :], lhsT=hidT[:, f, blk * 128:(blk + 1) * 128],
                            rhs=Wd[sl][:, ff, half * 512:(half + 1) * 512], start=(f == 0), stop=(f == 21)),
                            reads=[("Wd", sl), ("hidT", f)], writes=[PK(blk * 2 + half)])
        for blk in range(4):
            for half in range(2):
                P.op("dve", lambda e, blk=blk, half=half, h1=h1: e.tensor_tensor(
                    out=h1[:, blk, half * 512:(half + 1) * 512], in0=bank(blk * 2 + half)[:, :],
                    in1=h1[:, blk, half * 512:(half + 1) * 512], op=ALU.add),
                    writes=[PK(blk * 2 + half), ("h1", hb, blk, half)])
        if t8 + 1 < 8:
            stage_a(t8 + 1)
            for blk in range(4):
                deferred.append(lambda t8=t8, blk=blk: final_blk(t8, blk))
        else:
            for blk in range(4):
                final_blk(t8, blk)
    return finish(nc, P, A, out, dbg, {})


def finish(nc, P, A, out, dbg, dumps):
    fw = [k for k in P.dma_counts if isinstance(k, tuple) and k[0] == "outt"]
    if dbg:
        for name, (t, shape, dt) in dumps.items():
            d = nc.dram_tensor("dbg_" + name, list(shape), dt, kind="ExternalOutput").ap()
            P.barrier()
            flat = t[:].rearrange("p a b -> p (a b)") if len(t.shape) == 3 else t[:]
            P.dma("sp", lambda e, d=d, flat=flat: e.dma_start(out=d, in_=flat), key=("dbg", name))
            fw.append(("dbg", name))
    P.emit(final_waits=fw)
    return nc


_CACHE = {}


def kernel(x, meta_tokens, norm_mix, w_in, lambda_q1, lambda_k1, lambda_q2, lambda_k2, subln_gain, sink_logits,
           w_branch_a, w_branch_b, w_out, norm_ffn, w_ff_gate, w_ff_up, w_ff_down, norm_final, _stage=99, _dbg=False):
    f32 = lambda a: np.ascontiguousarray(np.asarray(a, dtype=np.float32))
    cst, wt, bw, ident = make_consts()
    nc = build_nc(_stage, _dbg)
    shared = {
        "meta_tokens": f32(meta_tokens), "w_in": f32(w_in)[0], "w_branch_a": f32(w_branch_a)[0],
        "w_branch_b": f32(w_branch_b)[0], "w_out": f32(w_out)[0], "w_ff_gate": f32(w_ff_gate)[0],
        "w_ff_up": f32(w_ff_up)[0], "w_ff_down": f32(w_ff_down)[0], "norm_mix": f32(norm_mix)[0],
        "norm_ffn": f32(norm_ffn)[0], "norm_final": f32(norm_final), "lambda_q1": f32(lambda_q1)[0],
        "lambda_k1": f32(lambda_k1)[0], "lambda_q2": f32(lambda_q2)[0], "lambda_k2": f32(lambda_k2)[0],
        "subln_gain": f32(subln_gain)[0], "sink_logits": f32(sink_logits)[0],
        "cst": cst, "wtab": wt, "bwtab": bw, "ident": ident,
    }
    xx = f32(x)
    ncores = xx.shape[0] if not _dbg else 1
    in_maps = [dict(shared, x=xx[b]) for b in range(ncores)]
    res = run_bass_kernel_spmd(nc, in_maps, core_ids=list(range(ncores)))
    if _dbg:
        return res.results
    return np.stack([np.asarray(r["out"], dtype=np.float32) for r in res.results], axis=0)
```
